# Optimizing a Trainium2 kernel written in Bass

```python
import jax, jax.numpy as jnp
from jax import lax
import numpy as np

D_MODEL = 2048
BATCH = 32
SEQ = 256
DEPTH = 2
DEC_BATCH = 4
DEC_SEQ = 2048
PAST_LEN = 256

GRID_W = 64
HEAD_DIM = 128
ATTN_WIDTH = D_MODEL // 2
ATTN_HEADS = ATTN_WIDTH // HEAD_DIM
ATTN_KV_HEADS = 2
ATTN_GROUP = ATTN_HEADS // ATTN_KV_HEADS
KV_WIDTH = ATTN_KV_HEADS * HEAD_DIM
WINDOW = 128
BLOCK = 128
GLA_WIDTH = D_MODEL - ATTN_WIDTH
GLA_HEADS = 4
GLA_DV = GLA_WIDTH // GLA_HEADS
GLA_DK = GLA_DV // 2
GLA_KEY_WIDTH = GLA_HEADS * GLA_DK
GATE_RANK = 16
GATE_NORM = 16.0
GLA_CHUNK = 64
IN_SPLIT = (ATTN_WIDTH, KV_WIDTH, KV_WIDTH, GLA_KEY_WIDTH, GLA_KEY_WIDTH, GLA_WIDTH, GLA_WIDTH, GATE_RANK, GATE_RANK)
IN_WIDTH = ATTN_WIDTH + 2 * KV_WIDTH + 2 * GLA_KEY_WIDTH + 2 * GLA_WIDTH + 2 * GATE_RANK
D_FF = 5632
CONV_WIDTH = 3
ROPE_THETA = 10000.0
LN_EPS = 1e-5
NEG_INF = -1e30
DEEPNORM_ALPHA = (2 * DEPTH) ** 0.25
DEEPNORM_BETA = (8 * DEPTH) ** -0.25

kernel_name = 'hybrid_swa_gla_deepnorm_diffusion_step'


def _split_cols(a, sizes):
    offs, s = [], 0
    for n in sizes[:-1]:
        s += n
        offs.append(s)
    return jnp.split(a, offs, axis=-1)


def _layer_norm(x, w, b):
    xf = x.astype(jnp.float32)
    mu = jnp.mean(xf, axis=-1, keepdims=True)
    var = jnp.mean(jnp.square(xf - mu), axis=-1, keepdims=True)
    y = (xf - mu) * lax.rsqrt(var + LN_EPS) * w.astype(jnp.float32) + b.astype(jnp.float32)
    return y.astype(x.dtype)


def _rope_2d(x):
    T = x.shape[1]
    rows = T // GRID_W
    row = jnp.repeat(jnp.arange(rows), GRID_W).astype(jnp.float32)
    col = jnp.tile(jnp.arange(GRID_W), rows).astype(jnp.float32)
    half = HEAD_DIM // 2
    n_freq = half // 2
    freqs = ROPE_THETA ** (-jnp.arange(n_freq, dtype=jnp.float32) / n_freq)

    def rot(xa, pos):
        ang = pos[:, None] * freqs[None, :]
        cos = jnp.cos(ang)[None, :, None, :]
        sin = jnp.sin(ang)[None, :, None, :]
        x1, x2 = xa[..., :n_freq], xa[..., n_freq:]
        return jnp.concatenate([x1 * cos - x2 * sin, x1 * sin + x2 * cos], axis=-1)

    xf = x.astype(jnp.float32)
    return jnp.concatenate([rot(xf[..., :half], row), rot(xf[..., half:], col)], axis=-1).astype(x.dtype)


def _sink_softmax(sink, s):
    sk = sink.astype(jnp.float32).reshape(1, ATTN_KV_HEADS, ATTN_GROUP, 1, 1)
    sk = jnp.broadcast_to(sk, s.shape[:-1] + (1,))
    p = jax.nn.softmax(jnp.concatenate([sk, s], axis=-1), axis=-1)
    return p[..., 1:]


def _context_attention(q, k, v, sink):
    B, S = q.shape[:2]
    nb = S // BLOCK
    scale = HEAD_DIM ** -0.5
    qb = q.reshape(B, nb, BLOCK, ATTN_KV_HEADS, ATTN_GROUP, HEAD_DIM).transpose(1, 0, 2, 3, 4, 5)

    def one(qi):
        s = jnp.einsum('bqkgd,bskd->bkgqs', qi, k).astype(jnp.float32) * scale
        p = _sink_softmax(sink, s).astype(v.dtype)
        return jnp.einsum('bkgqs,bskd->bqkgd', p, v)

    o = lax.map(one, qb)
    return o.transpose(1, 0, 2, 3, 4, 5).reshape(B, S, ATTN_WIDTH)


def _latent_attention(q, k, v, ck, cv, sink):
    B, T = q.shape[:2]
    nb = T // BLOCK
    scale = HEAD_DIM ** -0.5
    qb = q.reshape(B, nb, BLOCK, ATTN_KV_HEADS, ATTN_GROUP, HEAD_DIM).transpose(1, 0, 2, 3, 4, 5)
    pad = ((0, 0), (BLOCK, BLOCK), (0, 0), (0, 0))
    kp = jnp.pad(k, pad)
    vp = jnp.pad(v, pad)
    key_off = jnp.arange(3 * BLOCK) - BLOCK
    q_off = jnp.arange(BLOCK)

    def one(args):
        qi, i = args
        start = i * BLOCK
        kb = lax.dynamic_slice_in_dim(kp, start, 3 * BLOCK, axis=1)
        vb = lax.dynamic_slice_in_dim(vp, start, 3 * BLOCK, axis=1)
        kpos = start + key_off
        qpos = start + q_off
        valid = (jnp.abs(qpos[:, None] - kpos[None, :]) <= WINDOW) & (kpos >= 0)[None, :] & (kpos < T)[None, :]
        s_loc = jnp.einsum('bqkgd,bskd->bkgqs', qi, kb).astype(jnp.float32) * scale
        s_loc = jnp.where(valid, s_loc, NEG_INF)
        s_ctx = jnp.einsum('bqkgd,bskd->bkgqs', qi, ck).astype(jnp.float32) * scale
        p = _sink_softmax(sink, jnp.concatenate([s_loc, s_ctx], axis=-1)).astype(v.dtype)
        return (jnp.einsum('bkgqs,bskd->bqkgd', p[..., :3 * BLOCK], vb)
                + jnp.einsum('bkgqs,bskd->bqkgd', p[..., 3 * BLOCK:], cv))

    o = lax.map(one, (qb, jnp.arange(nb)))
    return o.transpose(1, 0, 2, 3, 4, 5).reshape(B, T, ATTN_WIDTH)


def _gla_direction(q, k, v, g, s0):
    B, T, H, DK = q.shape
    DV = v.shape[-1]
    N = T // GLA_CHUNK
    q = q.reshape(B, N, GLA_CHUNK, H, DK)
    k = k.reshape(B, N, GLA_CHUNK, H, DK)
    v = v.reshape(B, N, GLA_CHUNK, H, DV)
    g = g.reshape(B, N, GLA_CHUNK, H, DK)
    b = jnp.cumsum(g, axis=2)
    b_last = b[:, :, -1:]
    qe = q * jnp.exp(b)
    ke = k * jnp.exp(-b)
    kd = k * jnp.exp(b_last - b)
    causal = jnp.tril(jnp.ones((GLA_CHUNK, GLA_CHUNK), dtype=bool))
    a = jnp.einsum('bnihd,bnjhd->bnhij', qe, ke)
    a = jnp.where(causal, a, 0.0)
    o = jnp.einsum('bnhij,bnjhv->bnihv', a, v)
    u = jnp.einsum('bnjhd,bnjhv->bnhdv', kd, v)
    decay = jnp.exp(b[:, :, -1])

    def step(s, inp):
        dcy, ui = inp
        return dcy[..., None] * s + ui, s

    s_fin, s_in = lax.scan(step, s0, (decay.transpose(1, 0, 2, 3), u.transpose(1, 0, 2, 3, 4)))
    s_in = s_in.transpose(1, 0, 2, 3, 4)
    o = o + jnp.einsum('bnihd,bnhdv->bnihv', qe, s_in)
    return o.reshape(B, T, H, DV), s_fin


def _gla(gq, gk, gv, gog, lr_f, lr_b, p, s0f, s0b):
    B, T = gq.shape[:2]
    f32 = jnp.float32
    q = gq.reshape(B, T, GLA_HEADS, GLA_DK).astype(f32) * (GLA_DK ** -0.5)
    k = gk.reshape(B, T, GLA_HEADS, GLA_DK).astype(f32)
    v = gv.reshape(B, T, GLA_HEADS, GLA_DV).astype(f32)
    gf = jax.nn.log_sigmoid((lr_f @ p['w_gate_f'] + p['b_gate_f']).astype(f32)) / GATE_NORM
    gb = jax.nn.log_sigmoid((lr_b @ p['w_gate_b'] + p['b_gate_b']).astype(f32)) / GATE_NORM
    gf = gf.reshape(B, T, GLA_HEADS, GLA_DK)
    gb = gb.reshape(B, T, GLA_HEADS, GLA_DK)
    of, sf = _gla_direction(q, k, v, gf, s0f.astype(f32))
    ob, sb = _gla_direction(q[:, ::-1], k[:, ::-1], v[:, ::-1], gb[:, ::-1], s0b.astype(f32))
    o = of + ob[:, ::-1]
    o = o * lax.rsqrt(jnp.mean(jnp.square(o), axis=-1, keepdims=True) + LN_EPS) * p['gla_norm_w'].astype(f32)
    o = o.reshape(B, T, GLA_WIDTH) * jax.nn.silu(gog.astype(f32))
    return o.astype(gq.dtype), sf.astype(gq.dtype), sb.astype(gq.dtype)


def _mixer(h, p, s0f, s0b, ctx_kv):
    B, T = h.shape[:2]
    q, k, v, gq, gk, gv, gog, lr_f, lr_b = _split_cols(h @ p['w_in'], IN_SPLIT)
    q = q.reshape(B, T, ATTN_HEADS, HEAD_DIM)
    k = k.reshape(B, T, ATTN_KV_HEADS, HEAD_DIM)
    v = v.reshape(B, T, ATTN_KV_HEADS, HEAD_DIM)
    if ctx_kv is None:
        o_att = _context_attention(q, k, v, p['attn_sink'])
    else:
        q = _rope_2d(q)
        k = _rope_2d(k)
        o_att = _latent_attention(q, k, v, ctx_kv[0], ctx_kv[1], p['attn_sink'])
    o_gla, sf, sb = _gla(gq, gk, gv, gog, lr_f, lr_b, p, s0f, s0b)
    out = jnp.concatenate([o_att, o_gla], axis=-1) @ p['w_out']
    return out, k, v, sf, sb


def _conv_ffn(h, p):
    u = h @ p['w_up']
    up = jnp.pad(u, ((0, 0), (1, 1), (0, 0)))
    w = p['conv_w']
    u = up[:, :-2] * w[0] + up[:, 1:-1] * w[1] + up[:, 2:] * w[2] + p['conv_b']
    a, g = jnp.split(u, 2, axis=-1)
    return (jax.nn.silu(g) * a) @ p['w_down']


def _layer(x, mods, p, s0f, s0b, ctx_kv):
    sh1, sc1, g1, sh2, sc2, g2 = jnp.split(mods, 6, axis=-1)
    out, k, v, sf, sb = _mixer(x * (1 + sc1) + sh1, p, s0f, s0b, ctx_kv)
    x = _layer_norm(DEEPNORM_ALPHA * x + g1 * out, p['ln1_w'], p['ln1_b'])
    f = _conv_ffn(x * (1 + sc2) + sh2, p)
    x = _layer_norm(DEEPNORM_ALPHA * x + g2 * f, p['ln2_w'], p['ln2_b'])
    return x, k, v, sf, sb


def setup_inputs(seed: int = 0) -> dict:
    key = jax.random.key(seed)
    ks = jax.random.split(key, 32)
    f32 = jnp.float32

    def nrm(k, shape, scale=1.0):
        return jax.random.normal(k, shape, f32) * scale

    return {
        'x_prompt': nrm(ks[0], (BATCH, SEQ, D_MODEL)),
        'x_sample': nrm(ks[1], (DEC_BATCH, DEC_SEQ, D_MODEL)),
        'cache_k': nrm(ks[2], (DEC_BATCH, DEPTH, PAST_LEN, ATTN_KV_HEADS, HEAD_DIM)),
        'cache_v': nrm(ks[3], (DEC_BATCH, DEPTH, PAST_LEN, ATTN_KV_HEADS, HEAD_DIM)),
        'state_gla_fwd': nrm(ks[4], (DEC_BATCH, DEPTH, GLA_HEADS, GLA_DK, GLA_DV)),
        'state_gla_bwd': nrm(ks[5], (DEC_BATCH, DEPTH, GLA_HEADS, GLA_DK, GLA_DV)),
        'c': nrm(ks[6], (DEC_BATCH, D_MODEL)),
        'c_ctx': nrm(ks[7], (D_MODEL,)),
        'w_ada': nrm(ks[8], (DEPTH, D_MODEL, 6 * D_MODEL), D_MODEL ** -0.5),
        'b_ada': nrm(ks[9], (DEPTH, 6 * D_MODEL), 0.02),
        'w_in': nrm(ks[10], (DEPTH, D_MODEL, IN_WIDTH), D_MODEL ** -0.5),
        'attn_sink': nrm(ks[11], (DEPTH, ATTN_HEADS), 0.5),
        'w_gate_f': nrm(ks[12], (DEPTH, GATE_RANK, GLA_KEY_WIDTH), GATE_RANK ** -0.5),
        'b_gate_f': nrm(ks[13], (DEPTH, GLA_KEY_WIDTH), 0.1),
        'w_gate_b': nrm(ks[14], (DEPTH, GATE_RANK, GLA_KEY_WIDTH), GATE_RANK ** -0.5),
        'b_gate_b': nrm(ks[15], (DEPTH, GLA_KEY_WIDTH), 0.1),
        'gla_norm_w': 1.0 + nrm(ks[16], (DEPTH, GLA_DV), 0.02),
        'w_out': nrm(ks[17], (DEPTH, D_MODEL, D_MODEL), D_MODEL ** -0.5 * DEEPNORM_BETA),
        'ln1_w': 1.0 + nrm(ks[18], (DEPTH, D_MODEL), 0.02),
        'ln1_b': nrm(ks[19], (DEPTH, D_MODEL), 0.02),
        'w_up': nrm(ks[20], (DEPTH, D_MODEL, 2 * D_FF), D_MODEL ** -0.5),
        'conv_w': nrm(ks[21], (DEPTH, CONV_WIDTH, 2 * D_FF), CONV_WIDTH ** -0.5),
        'conv_b': nrm(ks[22], (DEPTH, 2 * D_FF), 0.02),
        'w_down': nrm(ks[23], (DEPTH, D_FF, D_MODEL), D_FF ** -0.5 * DEEPNORM_BETA),
        'ln2_w': 1.0 + nrm(ks[24], (DEPTH, D_MODEL), 0.02),
        'ln2_b': nrm(ks[25], (DEPTH, D_MODEL), 0.02),
    }


def reference(x_prompt, x_sample, cache_k, cache_v, state_gla_fwd, state_gla_bwd, c, c_ctx,
              w_ada, b_ada, w_in, attn_sink, w_gate_f, b_gate_f, w_gate_b, b_gate_b, gla_norm_w,
              w_out, ln1_w, ln1_b, w_up, conv_w, conv_b, w_down, ln2_w, ln2_b):
    xp, xs = x_prompt, x_sample
    new_k, new_v, new_sf, new_sb = [], [], [], []
    for l in range(DEPTH):
        p = {'w_in': w_in[l], 'attn_sink': attn_sink[l], 'w_gate_f': w_gate_f[l], 'b_gate_f': b_gate_f[l],
             'w_gate_b': w_gate_b[l], 'b_gate_b': b_gate_b[l], 'gla_norm_w': gla_norm_w[l], 'w_out': w_out[l],
             'ln1_w': ln1_w[l], 'ln1_b': ln1_b[l], 'w_up': w_up[l], 'conv_w': conv_w[l], 'conv_b': conv_b[l],
             'w_down': w_down[l], 'ln2_w': ln2_w[l], 'ln2_b': ln2_b[l]}
        mods_ctx = (jax.nn.silu(c_ctx) @ w_ada[l] + b_ada[l])[None, None, :]
        s_zero = jnp.zeros((xp.shape[0], GLA_HEADS, GLA_DK, GLA_DV), jnp.float32)
        xp, k_l, v_l, sf_l, sb_l = _layer(xp, mods_ctx, p, s_zero, s_zero, None)
        new_k.append(k_l)
        new_v.append(v_l)
        new_sf.append(sf_l)
        new_sb.append(sb_l)
        mods_lat = (jax.nn.silu(c) @ w_ada[l] + b_ada[l])[:, None, :]
        xs = _layer(xs, mods_lat, p, state_gla_fwd[:, l], state_gla_bwd[:, l], (cache_k[:, l], cache_v[:, l]))[0]
    return (xp, xs, jnp.stack(new_k, axis=1), jnp.stack(new_v, axis=1), jnp.stack(new_sf, axis=1), jnp.stack(new_sb, axis=1))
```

```python
import numpy as np
import concourse.bass as bass
import concourse.mybir as mybir
from concourse.bass_utils import run_bass_kernel_spmd

F32 = mybir.dt.float32
BF16 = mybir.dt.bfloat16
AF = mybir.ActivationFunctionType
ALU = mybir.AluOpType

T = 2048
D = 2048
KC = 16
L = 2
IN_W = 4640
DFF = 5632
NH = 44
Q0, K0, V0, GQ0, GK0, GV0, GO0, LR0 = 0, 1024, 1280, 1536, 2048, 2560, 3584, 4608
ALPHA = 4.0 ** 0.25
EPS1 = 1e-5 / (ALPHA * ALPHA)
SELF_SYNC = True
SCRATCH_AS_OUTPUT = True
SCOPES = False
import os as _os
DBGF = set(_os.environ.get('KDBG', '').split(','))
RING = 16


class Ev:
    __slots__ = ("sem", "val", "key")

    def __init__(self, sem, val, key):
        self.sem, self.val, self.key = sem, val, key


class Tl:
    def __init__(self, ap, name="", ww=False):
        self.ap = ap
        self.w = {}
        self.r = {}
        self.name = name
        self.ww = ww

    def __getitem__(self, idx):
        return self.ap[idx]


class Sched:
    def __init__(self, nc):
        self.nc = nc
        self.q = {e: [] for e in ("pe", "act", "dve", "pool", "sp")}
        self.seen = {e: {} for e in self.q}
        self.esem = {}
        self.epoch = 0
        self.ring = {qn: [[nc.alloc_semaphore(f"r_{qn}_{i}"), 0] for i in range(RING)] for qn in ("sp", "pool")}
        self.rr = {"sp": 0, "pool": 0}
        self.label = "init"
        self.new_epoch()

    def new_epoch(self):
        self.epoch += 1
        for e in ("pe", "act", "dve"):
            self.esem[e] = [self.nc.alloc_semaphore(f"e_{e}_{self.epoch}"), 0]

    def _wait(self, eng, ev):
        if ev.key[0] == eng and (eng == "pe" or not SELF_SYNC):
            return
        if self.seen[eng].get(ev.key, 0) >= ev.val:
            return
        self.seen[eng][ev.key] = ev.val
        sem, val = ev.sem, ev.val
        self.q[eng].append((self.label, lambda E, sem=sem, val=val: E.wait_ge(sem, val)))

    def _deps(self, eng, r, w):
        for t in r:
            for ev in t.w.values():
                self._wait(eng, ev)
        for t in w:
            if not t.ww:
                for ev in t.w.values():
                    self._wait(eng, ev)
            for ev in t.r.values():
                self._wait(eng, ev)

    def _record(self, ev, r, w):
        for t in r:
            t.r[ev.key] = ev
        for t in w:
            if t.ww:
                t.w[ev.key] = ev
            else:
                t.w = {ev.key: ev}
                t.r = {}

    def op(self, eng, fn, r=(), w=()):
        self._deps(eng, r, w)
        s = self.esem[eng]
        s[1] += 1
        sem = s[0]
        ev = Ev(sem, s[1], (eng, self.epoch))
        self.q[eng].append((self.label, lambda E, fn=fn, sem=sem: fn(E).then_inc(sem, 1)))
        self._record(ev, r, w)

    def dma(self, qn, out, in_, r=(), w=(), slow=False):
        self._deps(qn, r, w)
        i = self.rr[qn] % RING
        self.rr[qn] += 1
        slot = self.ring[qn][i]
        slot[1] += 16
        sem = slot[0]
        ev = Ev(sem, slot[1], ("dma", qn, i))
        if slow:
            self.q[qn].append((self.label, lambda E, out=out, in_=in_, sem=sem: E.dma_start(out=out, in_=in_, allow_slow_non_contiguous=True).then_inc(sem, 16)))
        else:
            self.q[qn].append((self.label, lambda E, out=out, in_=in_, sem=sem: E.dma_start(out=out, in_=in_).then_inc(sem, 16)))
        self._record(ev, r, w)

    def barrier(self, new_epoch=False):
        evs = []
        for e in ("pe", "act", "dve"):
            s = self.esem[e]
            if s[1] > 0:
                evs.append(Ev(s[0], s[1], (e, self.epoch)))
        for qn in ("sp", "pool"):
            for i, (sem, val) in enumerate(self.ring[qn]):
                if val > 0:
                    evs.append(Ev(sem, val, ("dma", qn, i)))
        for e in self.q:
            for ev in evs:
                if ev.key[0] == e and e == "pe":
                    continue
                if self.seen[e].get(ev.key, 0) >= ev.val:
                    continue
                self.seen[e][ev.key] = ev.val
                self.q[e].append((self.label, lambda E, sem=ev.sem, val=ev.val: E.wait_ge(sem, val)))
        if new_epoch:
            self.new_epoch()

    def emit(self):
        nc = self.nc
        q = self.q

        def run(E, items):
            if not SCOPES:
                for _, f in items:
                    f(E)
                return
            cur = None
            sid = None
            for lab, f in items:
                if lab != cur:
                    if cur is not None:
                        nc.leave_named_scope(cur, sid, False)
                    sid, _ = nc.enter_named_scope(lab, False)
                    cur = lab
                f(E)
            if cur is not None:
                nc.leave_named_scope(cur, sid, False)

        with nc.Block() as block:
            @block.tensor
            def _(E):
                run(E, q["pe"])

            @block.scalar
            def _(E):
                run(E, q["act"])

            @block.vector
            def _(E):
                run(E, q["dve"])

            @block.gpsimd
            def _(E):
                run(E, q["pool"])

            @block.sync
            def _(E):
                run(E, q["sp"])


class Arena:
    def __init__(self, nc, ncols):
        self.t = nc.alloc_sbuf_tensor("arena", [128, ncols], F32)
        self.n = ncols
        self.top = 0

    def _take(self, cols):
        cols = (cols + 7) // 8 * 8
        a = self.top
        assert a + cols <= self.n, f"arena overflow {a}+{cols}>{self.n}"
        self.top = a + cols
        return a, cols

    def f32(self, shape, name="", ww=False):
        n = int(np.prod(shape[1:]))
        a, c = self._take(n)
        ap = self.t[0:shape[0], a:a + n]
        if len(shape) == 3:
            ap = ap.rearrange("p (a b) -> p a b", a=shape[1])
        return Tl(ap, name, ww)

    def bf16(self, shape, name="", ww=False):
        n = int(np.prod(shape[1:]))
        a, c = self._take((n + 1) // 2)
        ap = self.t[0:shape[0], a:a + (n + 1) // 2].bitcast(BF16)[:, 0:n]
        if len(shape) == 3:
            ap = ap.rearrange("p (a b) -> p a b", a=shape[1])
        return Tl(ap, name, ww)


class _Stop(Exception):
    pass


def build_program(stop_after=None):
    nc = bass.Bass("TRN2", target_bir_lowering=False)
    try:
        _build_body(nc, stop_after)
    except _Stop:
        pass
    return nc


def _build_body(nc, stop_after):

    def din(name, shape):
        return nc.dram_tensor(name, list(shape), F32, kind="ExternalInput").ap()

    def dout(name, shape):
        return nc.dram_tensor(name, list(shape), F32, kind="ExternalOutput").ap()

    x_d = din("x", [T, D])
    cvec_d = din("cvec", [KC, 128])
    ck_d = din("ck", [L, 256, 2, 128])
    cv_d = din("cv", [L, 256, 2, 128])
    s0f_d = din("s0f", [L, 4, 128, 256])
    s0b_d = din("s0b", [L, 4, 128, 256])
    ident_d = din("ident", [128, 128])
    ropeP_d = din("ropeP", [128, 128])
    cos_d = din("cosT", [128, T])
    sin_d = din("sinT", [128, T])
    masks_d = din("masks", [128, 4, 128])
    abias_d = din("abias", [128, 8])
    flags_d = din("flags", [128, 2])
    w_ada_d = din("w_ada", [L, D, 6 * D])
    b_ada_d = din("b_ada", [L, 96, 128])
    w_in_d = din("w_in", [L, D, IN_W])
    sink_d = din("attn_sink", [L, 8])
    wgf_d = din("w_gate_f", [L, 16, 512])
    bgf_d = din("b_gate_f", [L, 4, 128])
    wgb_d = din("w_gate_b", [L, 16, 512])
    bgb_d = din("b_gate_b", [L, 4, 128])
    gnw_d = din("gla_norm_w", [L, 2, 128])
    w_out_d = din("w_out", [L, D, D])
    ln1w_d = din("ln1_w", [L, KC, 128])
    ln1b_d = din("ln1_b", [L, KC, 128])
    w_up_d = din("w_up", [L, D, 2 * DFF])
    convw_d = din("conv_w", [L, 3 * 88, 128])
    convb_d = din("conv_b", [L, 88, 128])
    w_down_d = din("w_down", [L, DFF, D])
    ln2w_d = din("ln2_w", [L, KC, 128])
    ln2b_d = din("ln2_b", [L, KC, 128])

    y_o = dout("y", [T, D])
    kc_o = dout("kc_o", [L, T, 2, 128])
    vc_o = dout("vc_o", [L, T, 2, 128])
    sf_o = dout("sf_o", [L, 8, 4, 128, 256])
    sb_o = dout("sb_o", [L, 8, 4, 128, 256])

    SCR = dict(kind="ExternalOutput") if SCRATCH_AS_OUTPUT else {}
    xT_d = [Tl(nc.dram_tensor(f"xT_s{i}", [KC, 128, T], F32, **SCR).ap(), f"xT{i}", True) for i in range(2)]
    catT_d = Tl(nc.dram_tensor("catT_s", [KC, 128, T], BF16, **SCR).ap(), "catT", True)
    h2T_d = Tl(nc.dram_tensor("h2T_s", [KC, 128, T], BF16, **SCR).ap(), "h2T", True)

    S = Sched(nc)
    A = Arena(nc, 51800)
    PS = [Tl(nc.alloc_psum_tensor(f"ps{i}", [128, 512], F32)[:, :], f"ps{i}") for i in range(8)]

    def ckpt(name, dumps=()):
        if stop_after != name:
            return
        S.barrier()
        for label, src_ap, shape, dt, deps in dumps:
            o = nc.dram_tensor("dbg_" + label, list(shape), dt, kind="ExternalOutput").ap()
            S.dma("sp", o, src_ap, r=deps)
        S.barrier()
        S.emit()
        raise _Stop

    def act(fn, r=(), w=()):
        S.op("act", fn, r, w)

    def dve(fn, r=(), w=()):
        S.op("dve", fn, r, w)

    def pe(fn, r=(), w=()):
        S.op("pe", fn, r, w)

    def A_copy(out, in_, r, w):
        act(lambda E: E.activation(out=out, in_=in_, func=AF.Copy), r, w)

    def A_ident(out, in_, scale, bias, r, w):
        act(lambda E: E.activation(out=out, in_=in_, func=AF.Identity, scale=scale, bias=bias), r, w)

    def mm_group(out_ap, pairs, r, w):
        n = len(pairs)

        def fn(E):
            ins = None
            for i, (lt, rh) in enumerate(pairs):
                ins = E.matmul(out_ap, lt, rh, start=(i == 0), stop=(i == n - 1))
            return ins
        pe(fn, r, w)

    ident_f = A.f32([128, 128], "ident_f")
    ident_b = A.bf16([128, 128], "ident_b")
    ropeP_b = A.bf16([128, 128], "ropeP_b")
    masks_b = A.bf16([128, 4, 128], "masks_b")
    ones_b = A.bf16([128, 128], "ones_b")
    ones_f = A.f32([128, 128], "ones_f")
    abias = A.f32([128, 8], "abias")
    flags = A.f32([128, 2], "flags")
    mods = [A.f32([128, 96], f"mods{l}") for l in range(L)]
    sc1p = [A.f32([128, 16], f"sc1p{l}") for l in range(L)]
    sc2p = [A.f32([128, 16], f"sc2p{l}") for l in range(L)]
    g1a = [A.f32([128, 16], f"g1a{l}") for l in range(L)]
    g2a = [A.f32([128, 16], f"g2a{l}") for l in range(L)]
    ln1w = [A.f32([128, 16], f"ln1w{l}") for l in range(L)]
    ln1b = [A.f32([128, 16], f"ln1b{l}") for l in range(L)]
    ln2w = [A.f32([128, 16], f"ln2w{l}") for l in range(L)]
    ln2b = [A.f32([128, 16], f"ln2b{l}") for l in range(L)]
    A2 = [A.f32([128, 16], f"A2{l}") for l in range(L)]
    B2 = [A.f32([128, 16], f"B2{l}") for l in range(L)]
    A1n = A.f32([128, 16], "A1n")
    B1n = A.f32([128, 16], "B1n")
    convw = [A.f32([128, 3 * 88], f"convw{l}") for l in range(L)]
    convb = [A.f32([128, 88], f"convb{l}") for l in range(L)]
    nbg = [A.f32([128, 8], f"nbg{l}") for l in range(L)]
    gnw = [A.f32([128, 2], f"gnw{l}") for l in range(L)]
    esink = [A.f32([128, 8], f"esink{l}") for l in range(L)]
    wgbd = [A.bf16([32, 1024], f"wgbd{l}") for l in range(L)]
    cT_b = A.bf16([128, 16], "cT_b")
    stage = A.f32([128, 128], "stage")
    stage2 = A.f32([128, 128], "stage2")
    base_top = A.top

    tmpc = A.f32([128, 512], "tmpc")
    S.dma("sp", ident_f[:, :], ident_d[:, :], w=[ident_f])
    dve(lambda E: E.tensor_copy(out=ident_b[:, :], in_=ident_f[:, :]), [ident_f], [ident_b])
    S.dma("sp", tmpc[:, 0:128], ropeP_d[:, :], w=[tmpc])
    dve(lambda E: E.tensor_copy(out=ropeP_b[:, :], in_=tmpc[:, 0:128]), [tmpc], [ropeP_b])
    S.dma("sp", tmpc[:, :], masks_d.rearrange("p a b -> p (a b)"), w=[tmpc])
    dve(lambda E: E.tensor_copy(out=masks_b.ap.rearrange("p a b -> p (a b)"), in_=tmpc[:, :]), [tmpc], [masks_b])
    dve(lambda E: E.memset(ones_b[:, :], 1.0), [], [ones_b])
    dve(lambda E: E.memset(ones_f[:, :], 1.0), [], [ones_f])
    S.dma("sp", abias[:, :], abias_d[:, :], w=[abias])
    S.dma("sp", flags[:, :], flags_d[:, :], w=[flags])
    cont = flags.ap[:, 0:1]

    def vec_fm(dst_tl, dst_ap, src_ap, n, st=None):
        st = st or stage
        S.dma("sp", st[0:n, :], src_ap, w=[st])
        pe(lambda E: E.transpose(PS[7][:, 0:n], st[0:n, :], ident_f[0:n, 0:n]), [st, ident_f], [PS[7]])
        A_copy(dst_ap, PS[7][:, 0:n], [PS[7]], [dst_tl])

    for l in range(L):
        vec_fm(ln1w[l], ln1w[l][:, :], ln1w_d[l], 16)
        vec_fm(ln1b[l], ln1b[l][:, :], ln1b_d[l], 16, stage2)
        vec_fm(ln2w[l], ln2w[l][:, :], ln2w_d[l], 16)
        vec_fm(ln2b[l], ln2b[l][:, :], ln2b_d[l], 16, stage2)
        for k in range(3):
            vec_fm(convw[l], convw[l][:, k * 88:(k + 1) * 88], convw_d[l, k * 88:(k + 1) * 88, :], 88, stage if k % 2 == 0 else stage2)
        vec_fm(convb[l], convb[l][:, :], convb_d[l], 88, stage2)
        vec_fm(nbg[l], nbg[l][:, 0:4], bgf_d[l], 4)
        vec_fm(nbg[l], nbg[l][:, 4:8], bgb_d[l], 4, stage2)
        dve(lambda E, l=l: E.tensor_scalar(out=nbg[l][:, :], in0=nbg[l][:, :], scalar1=-1.0, scalar2=None, op0=ALU.mult), [nbg[l]], [nbg[l]])
        vec_fm(gnw[l], gnw[l][:, :], gnw_d[l], 2)
        S.dma("sp", esink[l][:, :], sink_d[l].partition_broadcast(128), w=[esink[l]])
        act(lambda E, l=l: E.activation(out=esink[l][:, :], in_=esink[l][:, :], func=AF.Exp), [esink[l]], [esink[l]])
        dve(lambda E, l=l: E.memset(wgbd[l][:, :], 0.0), [], [wgbd[l]])
        S.dma("pool", wgbd[l][0:16, 0:512], wgf_d[l], w=[wgbd[l]])
        S.dma("pool", wgbd[l][16:32, 512:1024], wgb_d[l], w=[wgbd[l]])

    ckpt("consts", [("ln1w0", ln1w[0][:, :], [128, 16], F32, [ln1w[0]]), ("convw0", convw[0][:, :], [128, 264], F32, [convw[0]]),
                    ("nbg0", nbg[0][:, :], [128, 8], F32, [nbg[0]]), ("esink0", esink[0][:, :], [128, 8], F32, [esink[0]]),
                    ("wgbd0", wgbd[0][:, :], [32, 1024], BF16, [wgbd[0]]), ("masks", masks_b.ap, [128, 4, 128], BF16, [masks_b])])
    S.label = "M"
    S.dma("sp", stage[0:16, :], cvec_d[:, :], w=[stage])
    pe(lambda E: E.transpose(PS[7][:, 0:16], stage[0:16, :], ident_f[0:16, 0:16]), [stage, ident_f], [PS[7]])
    act(lambda E: E.activation(out=cT_b[:, :], in_=PS[7][:, 0:16], func=AF.Silu), [PS[7]], [cT_b])
    m0 = A.top
    wada = [A.bf16([128, 6 * D], f"wada{i}") for i in range(2)]
    bada = A.f32([128, 96], "bada")
    for l in range(L):
        for kc in range(KC):
            wb = wada[kc % 2]
            S.dma("pool", wb[:, :], w_ada_d[l, kc * 128:(kc + 1) * 128, :], w=[wb])

            def fn(E, wb=wb, kc=kc):
                ins = None
                for j in range(96):
                    ins = E.matmul(PS[6][:, j:j + 1], wb[:, j * 128:(j + 1) * 128], cT_b[:, kc:kc + 1],
                                   start=(kc == 0 and j == 0), stop=(kc == KC - 1), skip_group_check=True)
                return ins
            pe(fn, [wb, cT_b], [PS[6]])
        vec_fm(bada, bada[:, :], b_ada_d[l], 96)
        dve(lambda E, l=l: E.tensor_tensor(out=mods[l][:, :], in0=PS[6][:, 0:96], in1=bada[:, :], op=ALU.add), [PS[6], bada], [mods[l]])
        dve(lambda E, l=l: E.tensor_scalar(out=sc1p[l][:, :], in0=mods[l][:, 16:32], scalar1=1.0, scalar2=None, op0=ALU.add), [mods[l]], [sc1p[l]])
        dve(lambda E, l=l: E.tensor_scalar(out=sc2p[l][:, :], in0=mods[l][:, 64:80], scalar1=1.0, scalar2=None, op0=ALU.add), [mods[l]], [sc2p[l]])
        dve(lambda E, l=l: E.tensor_scalar(out=g1a[l][:, :], in0=mods[l][:, 32:48], scalar1=1.0 / ALPHA, scalar2=None, op0=ALU.mult), [mods[l]], [g1a[l]])
        dve(lambda E, l=l: E.tensor_scalar(out=g2a[l][:, :], in0=mods[l][:, 80:96], scalar1=1.0 / ALPHA, scalar2=None, op0=ALU.mult), [mods[l]], [g2a[l]])
        dve(lambda E, l=l: E.tensor_tensor(out=A2[l][:, :], in0=ln1w[l][:, :], in1=sc2p[l][:, :], op=ALU.mult), [ln1w[l], sc2p[l]], [A2[l]])
        dve(lambda E, l=l: E.tensor_tensor(out=B2[l][:, :], in0=ln1b[l][:, :], in1=sc2p[l][:, :], op=ALU.mult), [ln1b[l], sc2p[l]], [B2[l]])
        dve(lambda E, l=l: E.tensor_tensor(out=B2[l][:, :], in0=B2[l][:, :], in1=mods[l][:, 48:64], op=ALU.add), [B2[l], mods[l]], [B2[l]])
    dve(lambda E: E.tensor_tensor(out=A1n[:, :], in0=ln2w[0][:, :], in1=sc1p[1][:, :], op=ALU.mult), [ln2w[0], sc1p[1]], [A1n])
    dve(lambda E: E.tensor_tensor(out=B1n[:, :], in0=ln2b[0][:, :], in1=sc1p[1][:, :], op=ALU.mult), [ln2b[0], sc1p[1]], [B1n])
    dve(lambda E: E.tensor_tensor(out=B1n[:, :], in0=B1n[:, :], in1=mods[1][:, 0:16], op=ALU.add), [B1n, mods[1]], [B1n])
    S.barrier()
    A.top = m0
    ckpt("M", [("mods0", mods[0][:, :], [128, 96], F32, [mods[0]]), ("mods1", mods[1][:, :], [128, 96], F32, [mods[1]]),
               ("B1n", B1n[:, :], [128, 16], F32, [B1n])])

    hT_base = A.top
    hT = A.bf16([128, KC, T], "hT", ww=True)
    work0 = A.top

    S.label = "p0"
    xb = [A.f32([128, D], f"xb{i}") for i in range(2)]
    xTs = [A.f32([128, KC, 128], f"xTs{i}", ww=True) for i in range(2)]
    for b in range(16):
        xbt = xb[b % 2]
        xs = xTs[b % 2]
        S.dma("sp", xbt[:, :], x_d[b * 128:(b + 1) * 128, :], w=[xbt])
        for c4 in range(4):
            pb = PS[c4 % 4]

            def fn(E, pb=pb, xbt=xbt, c4=c4):
                ins = None
                for j in range(4):
                    c = c4 * 4 + j
                    ins = E.transpose(pb[:, j * 128:(j + 1) * 128], xbt[:, c * 128:(c + 1) * 128], ident_f[:, :])
                return ins
            if 'nope' in DBGF:
                continue
            pe(fn, [xbt, ident_f], [pb])
            if 'noact' not in DBGF:
                A_copy(xs[:, c4 * 4:(c4 + 1) * 4, :], pb[:, :].rearrange("p (a b) -> p a b", a=4), [pb], [xs])
            for j in range(4):
                c = c4 * 4 + j
                A_ident(hT[:, c, b * 128:(b + 1) * 128], pb[:, j * 128:(j + 1) * 128], sc1p[0][:, c:c + 1], mods[0][:, c:c + 1],
                        [pb, sc1p[0], mods[0]], [hT])
        if 'nostore' not in DBGF:
            S.dma("sp", xT_d[0].ap[:, :, b * 128:(b + 1) * 128].rearrange("c p t -> p c t"), xs[:, :, :], r=[xs], w=[xT_d[0]])
    S.barrier(new_epoch=True)
    A.top = work0
    ckpt("p0", [("hT", hT.ap, [128, KC, T], BF16, [hT])] + ([] if 'nodumpx' in DBGF else [("xT", xT_d[0].ap, [KC, 128, T], F32, [xT_d[0]])]) + [
                ("mods0", mods[0][:, :], [128, 96], F32, [mods[0]]), ("mods1", mods[1][:, :], [128, 96], F32, [mods[1]])])

    def load_w(tl, dram_cols_ap):
        S.dma("pool", tl.ap, dram_cols_ap.rearrange("(kc p) n -> p kc n", p=128), w=[tl])

    def proj_fm(wt, col, tt, ps_tl):
        mm_group(ps_tl[:, :], [(wt[:, kc, col * 128:(col + 1) * 128], hT[:, kc, tt * 512:(tt + 1) * 512]) for kc in range(KC)],
                 [wt, hT], [ps_tl])


    def mk_chunks(tl3, n):
        return [Tl(tl3.ap[:, c, :], f"{tl3.name}_{c}") for c in range(n)]

    def ln_tile(zc, tb, lw, lb, out_x, out_h):
        sqb, zbb, mean, rstd, s1p, s2p = tb
        for c in range(KC):
            sqc, zbc = sqb[c % 2], zbb[c % 2]
            act(lambda E, sqc=sqc, zt=zc[c]: E.activation(out=sqc[:, :], in_=zt[:, :], func=AF.Square), [zc[c]], [sqc])
            dve(lambda E, zbc=zbc, zt=zc[c]: E.tensor_copy(out=zbc[:, :], in_=zt[:, :]), [zc[c]], [zbc])
            pe(lambda E, c=c, zbc=zbc: E.matmul(s1p[:, :], ones_b[:, :], zbc[:, :], start=(c == 0), stop=(c == KC - 1)), [zbc, ones_b], [s1p])
            pe(lambda E, c=c, sqc=sqc: E.matmul(s2p[:, :], ones_b[:, :], sqc[:, :], start=(c == 0), stop=(c == KC - 1)), [sqc, ones_b], [s2p])
        make_ln_finish(S, act, dve, A_ident, None, mean, rstd, s1p, s2p)()
        for c in range(KC):
            zt = zc[c]
            dve(lambda E, zt=zt: E.tensor_tensor(out=zt[:, :], in0=zt[:, :], in1=mean[:, :], op=ALU.subtract), [zt, mean], [zt])
            dve(lambda E, zt=zt: E.tensor_tensor(out=zt[:, :], in0=zt[:, :], in1=rstd[:, :], op=ALU.mult), [zt, rstd], [zt])
            if out_x is not None:
                out_x(c, zt)
            if out_h is not None:
                out_h(c, zt)

    def layer(l):
        xin = xT_d[0]
        xmid = xT_d[1]
        win = w_in_d[l]
        p1 = A.top
        S.label = f"l{l}_attn"
        wbuf = [A.bf16([128, KC, 128], f"wbuf{i}") for i in range(3)]
        wi = [0]

        def next_w(c0, n=128):
            tl = wbuf[wi[0] % 3]
            wi[0] += 1
            load_w(tl, win[:, c0:c0 + n])
            return tl

        qr = A.bf16([128, 16 * 4 * 128], "qr", ww=True)
        kr = A.bf16([128, T], "kr", ww=True)
        vtok = A.bf16([128, 16, 128], "vtok", ww=True)
        kst = A.f32([128, 16, 128], "kst", ww=True)
        vst = A.f32([128, 16, 128], "vst", ww=True)
        ckT = A.bf16([128, 256], "ckT")
        cvt = A.bf16([128, 2, 128], "cvt")
        ckst = A.f32([128, 2, 128], "ckst")
        cosb = [A.f32([128, 512], f"cos{i}") for i in range(2)]
        sinb = [A.f32([128, 512], f"sin{i}") for i in range(2)]
        qb = [A.bf16([128, 512], f"qb{i}") for i in range(2)]
        t1 = [A.f32([128, 512], f"t1{i}") for i in range(2)]
        t2 = [A.f32([128, 512], f"t2{i}") for i in range(2)]
        Eb = [A.bf16([128, 512], f"E{i}") for i in range(3)]
        den = A.f32([128, 512], "den")
        oatt = A.bf16([128, 4, T], "oatt", ww=True)
        qrv = qr.ap.rearrange("p (b h q) -> p b h q", b=16, h=4)
        cnt = [0]
        for kh in range(2):
            for hi in range(5):
                c0 = (Q0 + (4 * kh + hi) * 128) if hi < 4 else (K0 + kh * 128)
                wt = next_w(c0)
                for tt in range(4):
                    n = cnt[0]
                    cnt[0] += 1
                    pq = PS[n % 2]
                    pr = PS[2 + n % 2]
                    cb, sb_, qbb, t1b, t2b = cosb[n % 2], sinb[n % 2], qb[n % 2], t1[n % 2], t2[n % 2]
                    S.dma("sp", cb[:, :], cos_d[:, tt * 512:(tt + 1) * 512], w=[cb])
                    S.dma("sp", sb_[:, :], sin_d[:, tt * 512:(tt + 1) * 512], w=[sb_])
                    proj_fm(wt, 0, tt, pq)
                    A_copy(qbb[:, :], pq[:, :], [pq], [qbb])
                    pe(lambda E, pr=pr, qbb=qbb: E.matmul(pr[:, :], ropeP_b[:, :], qbb[:, :], start=True, stop=True), [qbb, ropeP_b], [pr])
                    dve(lambda E, t1b=t1b, qbb=qbb, cb=cb: E.tensor_tensor(out=t1b[:, :], in0=qbb[:, :], in1=cb[:, :], op=ALU.mult), [qbb, cb], [t1b])
                    dve(lambda E, t2b=t2b, pr=pr, sb_=sb_: E.tensor_tensor(out=t2b[:, :], in0=pr[:, :], in1=sb_[:, :], op=ALU.mult), [pr, sb_], [t2b])
                    if hi < 4:
                        dve(lambda E, t1b=t1b, t2b=t2b, tt=tt, hi=hi: E.tensor_tensor(
                            out=qrv[:, tt * 4:(tt + 1) * 4, hi, :], in0=t1b[:, :].rearrange("p (a b) -> p a b", a=4),
                            in1=t2b[:, :].rearrange("p (a b) -> p a b", a=4), op=ALU.add), [t1b, t2b], [qr])
                    else:
                        dve(lambda E, t1b=t1b, t2b=t2b, tt=tt: E.tensor_tensor(
                            out=kr[:, tt * 512:(tt + 1) * 512], in0=t1b[:, :], in1=t2b[:, :], op=ALU.add), [t1b, t2b], [kr])
            for which in range(2):
                c0 = (K0 if which == 0 else V0) + kh * 128
                wt = next_w(c0)
                st_ = kst if which == 0 else vst
                for b in range(16):
                    pb = PS[4 + b % 2]
                    mm_group(pb[:, 0:128], [(hT[:, kc, b * 128:(b + 1) * 128], wt[:, kc, :]) for kc in range(KC)], [wt, hT], [pb])
                    A_copy(st_[:, b, :], pb[:, 0:128], [pb], [st_])
                    if which == 1:
                        dve(lambda E, b=b: E.tensor_copy(out=vtok[:, b, :], in_=vst[:, b, :]), [vst], [vtok])
                od = kc_o if which == 0 else vc_o
                S.dma("sp", od[l, :, kh, :].rearrange("(b p) d -> p b d", p=128), st_[:, :, :], r=[st_])
            S.dma("sp", ckst[:, :, :], ck_d[l, :, kh, :].rearrange("(j p) d -> p j d", p=128), w=[ckst])
            for j in range(2):
                pe(lambda E, j=j: E.transpose(PS[6][:, j * 128:(j + 1) * 128], ckst[:, j, :], ident_f[:, :]), [ckst, ident_f], [PS[6]])
            A_copy(ckT[:, :], PS[6][:, 0:256], [PS[6]], [ckT])
            S.dma("pool", cvt[:, :, :], cv_d[l, :, kh, :].rearrange("(j p) d -> p j d", p=128), w=[cvt])
            scale = 128.0 ** -0.5
            ecnt = 0
            for i in range(16):
                chunks = []
                ev = i % 2
                if i > 0:
                    chunks.append((kr[:, (i - 1) * 128:i * 128], vtok[:, i - 1, :], abias[:, ev:ev + 1], 0, [kr], [vtok]))
                chunks.append((kr[:, i * 128:(i + 1) * 128], vtok[:, i, :], 0.0, None, [kr], [vtok]))
                if i < 15:
                    chunks.append((kr[:, (i + 1) * 128:(i + 2) * 128], vtok[:, i + 1, :], abias[:, 2 + ev:3 + ev], 1, [kr], [vtok]))
                for j in range(2):
                    chunks.append((ckT[:, j * 128:(j + 1) * 128], cvt[:, j, :], abias[:, 4:5], None, [ckT], [cvt]))
                po = PS[4 + i % 2]
                pl = PS[6 + i % 2]
                qi = qrv[:, i, :, :].rearrange("p h q -> p (h q)")
                nchk = len(chunks)
                pend = None
                for ci, (kT, vv, bias, mk, kdep, vdep) in enumerate(chunks):
                    psb = PS[ecnt % 2]
                    eb = Eb[ecnt % 3]
                    ecnt += 1
                    pe(lambda E, psb=psb, kT=kT, qi=qi: E.matmul(psb[:, :], kT, qi, start=True, stop=True), kdep + [qr], [psb])
                    act(lambda E, eb=eb, psb=psb, bias=bias: E.activation(out=eb[:, :], in_=psb[:, :], func=AF.Exp, scale=scale, bias=bias),
                        [psb, abias], [eb])
                    if mk is not None:
                        dve(lambda E, eb=eb, mk=mk: E.tensor_tensor(
                            out=eb[:, :].rearrange("p (h q) -> p h q", h=4), in0=eb[:, :].rearrange("p (h q) -> p h q", h=4),
                            in1=masks_b[:, mk, :].unsqueeze(1).to_broadcast([128, 4, 128]), op=ALU.mult), [eb, masks_b], [eb])
                    if pend is not None:
                        pci, peb, pvv, pvdep = pend
                        pe(lambda E, po=po, pvv=pvv, peb=peb, pci=pci: E.matmul(po[:, :], pvv, peb[:, :], start=(pci == 0), stop=False), pvdep + [peb], [po])
                        pe(lambda E, pl=pl, peb=peb, pci=pci: E.matmul(pl[:, :], ones_b[:, :], peb[:, :], start=(pci == 0), stop=False), [peb, ones_b], [pl])
                    pend = (ci, eb, vv, vdep)
                pci, peb, pvv, pvdep = pend
                pe(lambda E, po=po, pvv=pvv, peb=peb, pci=pci: E.matmul(po[:, :], pvv, peb[:, :], start=(pci == 0), stop=True), pvdep + [peb], [po])
                pe(lambda E, pl=pl, peb=peb, pci=pci: E.matmul(pl[:, :], ones_b[:, :], peb[:, :], start=(pci == 0), stop=True), [peb, ones_b], [pl])
                dve(lambda E, pl=pl, kh=kh: E.tensor_tensor(
                    out=den[:, :].rearrange("p (h q) -> p h q", h=4), in0=pl[:, :].rearrange("p (h q) -> p h q", h=4),
                    in1=esink[l][:, 4 * kh:4 * kh + 4].unsqueeze(2).to_broadcast([128, 4, 128]), op=ALU.add), [pl, esink[l]], [den])
                dve(lambda E: E.reciprocal(out=den[:, :], in_=den[:, :]), [den], [den])
                dve(lambda E, po=po, i=i: E.tensor_tensor(
                    out=oatt[:, :, i * 128:(i + 1) * 128], in0=po[:, :].rearrange("p (h q) -> p h q", h=4),
                    in1=den[:, :].rearrange("p (h q) -> p h q", h=4), op=ALU.mult), [po, den], [oatt])
            for g in range(4):
                S.dma("sp", catT_d.ap[4 * kh + g, :, :], oatt[:, g, :], r=[oatt], w=[catT_d])
        S.barrier()
        A.top = p1
        ckpt(f"l{l}a", [("catTa", catT_d.ap[0:8], [8, 128, T], BF16, [catT_d])])

        S.label = f"l{l}_gla"
        wbuf = [A.bf16([128, KC, 128], f"gwbuf{i}") for i in range(3)]
        wi[0] = 0
        lrT = A.bf16([32, T], "lrT", ww=True)
        gq = A.bf16([128, T], "gq", ww=True)
        gk = A.bf16([128, T], "gk", ww=True)
        gvt = A.bf16([128, 16, 256], "gvt", ww=True)
        o_sb = A.f32([128, 2, T], "o_sb")
        gt = A.f32([128, T], "gt")
        cum = A.f32([128, T], "cum")
        qe = A.bf16([128, T], "qe")
        ke = A.bf16([128, T], "ke")
        kd = A.bf16([128, T], "kd")
        kdt = A.bf16([128, 16, 128], "kdt", ww=True)
        nle = A.f32([128, 16], "nle")
        dec = A.f32([128, 16], "dec")
        NCH = 6
        chain = [A.f32([128, 256], f"chain{i}") for i in range(NCH)]
        Sb_all = A.bf16([128, 16, 256], "Sb_all", ww=True)
        At4 = [A.bf16([128, 512], f"At4{i}") for i in range(2)]
        PSU = [PS[2], PS[3]]
        sq = [A.bf16([128, 512], f"gsq{i}") for i in range(2)]
        rs = A.f32([128, 512], "grs")
        sg = [A.f32([128, 512], f"gsg{i}") for i in range(2)]
        tq = A.f32([128, 512], "gtq")
        catg = [A.bf16([128, T], f"catg{i}", ww=True) for i in range(2)]
        wlr = A.bf16([128, KC, 32], "wlr")
        load_w(wlr, win[:, LR0:LR0 + 32])
        for tt in range(4):
            pb = PS[tt % 2]
            mm_group(pb[0:32, :], [(wlr[:, kc, :], hT[:, kc, tt * 512:(tt + 1) * 512]) for kc in range(KC)], [wlr, hT], [pb])
            A_copy(lrT[:, tt * 512:(tt + 1) * 512], pb[0:32, :], [pb], [lrT])
        for hd in range(4):
            for which, dst in ((0, gq), (1, gk)):
                wt = next_w((GQ0 if which == 0 else GK0) + hd * 128)
                for tt in range(4):
                    pb = PS[tt % 2]
                    proj_fm(wt, 0, tt, pb)
                    A_copy(dst[:, tt * 512:(tt + 1) * 512], pb[:, :], [pb], [dst])
            wv = [next_w(GV0 + hd * 256), next_w(GV0 + hd * 256 + 128)]
            for b in range(16):
                pb = PS[2 + b % 2]
                for vc in range(2):
                    mm_group(pb[:, vc * 128:(vc + 1) * 128], [(hT[:, kc, b * 128:(b + 1) * 128], wv[vc][:, kc, :]) for kc in range(KC)],
                             [wv[vc], hT], [pb])
                A_copy(gvt[:, b, :], pb[:, 0:256], [pb], [gvt])
            for dr in range(2):
                for tt in range(4):
                    pb = PS[tt % 2]
                    col = dr * 512 + hd * 128
                    pe(lambda E, pb=pb, col=col, tt=tt: E.matmul(pb[:, :], wgbd[l][:, col:col + 128], lrT[:, tt * 512:(tt + 1) * 512], start=True, stop=True),
                       [wgbd[l], lrT], [pb])
                    act(lambda E, pb=pb, tt=tt, dr=dr, hd=hd: E.activation(out=gt[:, tt * 512:(tt + 1) * 512], in_=pb[:, :], func=AF.Exp, scale=-1.0,
                                                                          bias=nbg[l][:, dr * 4 + hd:dr * 4 + hd + 1]), [pb, nbg[l]], [gt])
                act(lambda E: E.activation(out=gt[:, :], in_=gt[:, :], func=AF.Ln, bias=1.0, scale=1.0), [gt], [gt])
                for c in range(16):
                    dve(lambda E, c=c: E.tensor_tensor_scan(out=cum[:, c * 128:(c + 1) * 128], data0=ones_b[:, :], data1=gt[:, c * 128:(c + 1) * 128],
                                                            initial=0.0, op0=ALU.mult, op1=ALU.add), [gt, ones_b], [cum])
                cum3 = cum.ap.rearrange("p (c t) -> p c t", c=16)
                gt3 = gt.ap.rearrange("p (c t) -> p c t", c=16)
                if dr == 1:
                    dve(lambda E: E.tensor_tensor(out=gt[:, :], in0=gt[:, :], in1=cum[:, :], op=ALU.subtract), [gt, cum], [gt])
                    dve(lambda E: E.tensor_tensor(out=gt3, in0=gt3, in1=cum3[:, :, 127:128].to_broadcast([128, 16, 128]), op=ALU.add), [gt, cum], [gt])
                    Lc, Lc3, tmpb, endcol = gt, gt3, cum, 0
                else:
                    Lc, Lc3, tmpb, endcol = cum, cum3, gt, 127
                dve(lambda E, Lc3=Lc3, endcol=endcol: E.tensor_scalar(out=nle[:, :].unsqueeze(2), in0=Lc3[:, :, endcol:endcol + 1], scalar1=-1.0 / 16, scalar2=None, op0=ALU.mult),
                    [Lc], [nle])
                act(lambda E: E.activation(out=dec[:, :], in_=nle[:, :], func=AF.Exp), [nle], [dec])
                for c in range(16):
                    act(lambda E, c=c, Lc=Lc, tmpb=tmpb: E.activation(out=tmpb[:, c * 128:(c + 1) * 128], in_=Lc[:, c * 128:(c + 1) * 128], func=AF.Exp,
                                                                      scale=1.0 / 16, bias=nle[:, c:c + 1]), [Lc, nle], [tmpb])
                dve(lambda E, tmpb=tmpb: E.tensor_tensor(out=kd[:, :], in0=gk[:, :], in1=tmpb[:, :], op=ALU.mult), [gk, tmpb], [kd])
                act(lambda E, Lc=Lc, tmpb=tmpb: E.activation(out=tmpb[:, :], in_=Lc[:, :], func=AF.Exp, scale=1.0 / 16), [Lc], [tmpb])
                dve(lambda E, tmpb=tmpb: E.tensor_tensor(out=ke[:, :], in0=gk[:, :], in1=tmpb[:, :], op=ALU.mult), [gk, tmpb], [ke])
                act(lambda E, Lc=Lc, tmpb=tmpb: E.activation(out=tmpb[:, :], in_=Lc[:, :], func=AF.Exp, scale=-1.0 / 16), [Lc], [tmpb])
                dve(lambda E, tmpb=tmpb: E.scalar_tensor_tensor(out=qe[:, :], in0=gq[:, :], scalar=128.0 ** -0.5, in1=tmpb[:, :], op0=ALU.mult, op1=ALU.mult),
                    [gq, tmpb], [qe])
                for c4 in range(4):
                    pb = PS[2 + c4 % 2]
                    pbb = pb[:, 0:256].bitcast(BF16)

                    def fn(E, pbb=pbb, c4=c4):
                        ins = None
                        for j in range(4):
                            c = c4 * 4 + j
                            ins = E.transpose(pbb[:, j * 128:(j + 1) * 128], kd[:, c * 128:(c + 1) * 128], ident_b[:, :])
                        return ins
                    pe(fn, [kd, ident_b], [pb])
                    A_copy(kdt[:, c4 * 4:(c4 + 1) * 4, :], pbb.rearrange("p (a b) -> p a b", a=4), [pb], [kdt])
                s0 = (s0f_d if dr == 0 else s0b_d)
                so = (sf_o if dr == 0 else sb_o)
                order = list(range(16)) if dr == 0 else list(range(15, -1, -1))
                ci = 0
                cur = chain[0]
                S.dma("sp", cur[:, :], s0[l, hd], w=[cur])
                for n, c in enumerate(order):
                    boundary = (n > 0) and (n % 2 == 0)
                    if boundary:
                        nxt = chain[(ci + 1) % NCH]
                        ci += 1
                        dve(lambda E, nxt=nxt, cur=cur: E.tensor_scalar(out=nxt[:, :], in0=cur[:, :], scalar1=cont, scalar2=None, op0=ALU.mult), [cur, flags], [nxt])
                        cur = nxt
                    A_copy(Sb_all[:, c, :], cur[:, :], [cur], [Sb_all])
                    pu = PSU[n % 2]
                    pe(lambda E, pu=pu, c=c: E.matmul(pu[:, 0:256], kdt[:, c, :], gvt[:, c, :], start=True, stop=True), [kdt, gvt], [pu])
                    nxt = chain[(ci + 1) % NCH]
                    ci += 1
                    dve(lambda E, nxt=nxt, cur=cur, pu=pu, c=c: E.scalar_tensor_tensor(out=nxt[:, :], in0=cur[:, :], scalar=dec[:, c:c + 1], in1=pu[:, 0:256],
                                                                                       op0=ALU.mult, op1=ALU.add), [cur, dec, pu], [nxt])
                    cur = nxt
                    if n % 2 == 1:
                        S.dma("sp", so[l, c // 2, hd], cur[:, :], r=[cur])
                for g in range(4):
                    cs = order[4 * g:4 * g + 4]
                    g4 = min(cs) // 4
                    pa = PS[0] if g % 2 == 0 else PS[6]
                    att = At4[g % 2]

                    def fnA(E, pa=pa, cs=cs):
                        ins = None
                        for c in cs:
                            k = c % 4
                            ins = E.matmul(pa[:, k * 128:(k + 1) * 128], ke[:, c * 128:(c + 1) * 128], qe[:, c * 128:(c + 1) * 128], start=True, stop=True)
                        return ins
                    pe(fnA, [ke, qe], [pa])
                    dve(lambda E, att=att, pa=pa, dr=dr: E.tensor_tensor(
                        out=att[:, :].rearrange("p (k t) -> p k t", k=4), in0=pa[:, :].rearrange("p (k t) -> p k t", k=4),
                        in1=masks_b[:, 2 + dr, :].unsqueeze(1).to_broadcast([128, 4, 128]), op=ALU.mult), [pa, masks_b], [att])
                    pov = (PS[1], PS[4]) if g % 2 == 0 else (PS[5], PS[7])
                    for vc in range(2):
                        po = pov[vc]

                        def fnO(E, po=po, cs=cs, vc=vc, att=att):
                            ins = None
                            for c in cs:
                                k = c % 4
                                E.matmul(po[:, k * 128:(k + 1) * 128], gvt[:, c, vc * 128:(vc + 1) * 128], att[:, k * 128:(k + 1) * 128], start=True, stop=False)
                                ins = E.matmul(po[:, k * 128:(k + 1) * 128], Sb_all[:, c, vc * 128:(vc + 1) * 128], qe[:, c * 128:(c + 1) * 128], start=False, stop=True)
                            return ins
                        pe(fnO, [gvt, att, Sb_all, qe], [po])
                        if dr == 0:
                            A_copy(o_sb[:, vc, g4 * 512:(g4 + 1) * 512], po[:, :], [po], [o_sb])
                        else:
                            dve(lambda E, po=po, vc=vc, g4=g4: E.tensor_tensor(out=o_sb[:, vc, g4 * 512:(g4 + 1) * 512], in0=o_sb[:, vc, g4 * 512:(g4 + 1) * 512],
                                                                              in1=po[:, :], op=ALU.add), [po, o_sb], [o_sb])
            wg = [next_w(GO0 + hd * 256), next_w(GO0 + hd * 256 + 128)]
            for tt in range(4):
                pm = PS[tt % 2]
                for vc in range(2):
                    sqv = sq[vc]
                    act(lambda E, sqv=sqv, vc=vc, tt=tt: E.activation(out=sqv[:, :], in_=o_sb[:, vc, tt * 512:(tt + 1) * 512], func=AF.Square), [o_sb], [sqv])
                mm_group(pm[:, :], [(ones_b[:, :], sq[0][:, :]), (ones_b[:, :], sq[1][:, :])], [ones_b, sq[0], sq[1]], [pm])
                act(lambda E, pm=pm: E.activation(out=rs[:, :], in_=pm[:, :], func=AF.Sqrt, scale=1.0 / 256, bias=1e-5), [pm], [rs])
                dve(lambda E: E.reciprocal(out=rs[:, :], in_=rs[:, :]), [rs], [rs])
                for vc in range(2):
                    pg = PS[2 + vc]
                    proj_fm(wg[vc], 0, tt, pg)
                    sgv = sg[vc]
                    act(lambda E, sgv=sgv, pg=pg: E.activation(out=sgv[:, :], in_=pg[:, :], func=AF.Silu), [pg], [sgv])
                    dve(lambda E, vc=vc, tt=tt: E.tensor_tensor(out=tq[:, :], in0=o_sb[:, vc, tt * 512:(tt + 1) * 512], in1=rs[:, :], op=ALU.mult), [o_sb, rs], [tq])
                    dve(lambda E, vc=vc, tt=tt, sgv=sgv: E.scalar_tensor_tensor(out=catg[vc][:, tt * 512:(tt + 1) * 512], in0=tq[:, :], scalar=gnw[l][:, vc:vc + 1],
                                                                               in1=sgv[:, :], op0=ALU.mult, op1=ALU.mult), [tq, gnw[l], sgv], [catg[vc]])
            for vc in range(2):
                S.dma("sp", catT_d.ap[8 + 2 * hd + vc, :, :], catg[vc][:, :], r=[catg[vc]], w=[catT_d])
        S.barrier()
        A.top = hT_base
        ckpt(f"l{l}b", [("catT", catT_d.ap, [KC, 128, T], BF16, [catT_d])])

        S.label = f"l{l}_p3"
        wout = A.bf16([128, KC, D], "wout")
        for q4 in range(4):
            S.dma("pool", wout[:, :, q4 * 512:(q4 + 1) * 512], w_out_d[l][:, q4 * 512:(q4 + 1) * 512].rearrange("(kc p) n -> p kc n", p=128), w=[wout])
        catt = [A.bf16([128, KC, 512], f"catt{i}") for i in range(2)]
        zbuf = [A.f32([128, KC, 512], f"z{i}") for i in range(2)]
        zch = [mk_chunks(zb_, KC) for zb_ in zbuf]
        xc = [A.f32([128, 512], f"xc{i}") for i in range(3)]
        tb3 = ([A.bf16([128, 512], f"lsq{i}") for i in range(2)], [A.bf16([128, 512], f"lzb{i}") for i in range(2)],
               A.f32([128, 512], "lmean"), A.f32([128, 512], "lrstd"), PS[4], PS[5])
        xo = [A.f32([128, 512], f"lxo{i}") for i in range(3)]
        ho = [A.bf16([128, 512], f"ho{i}") for i in range(3)]
        for tt in range(4):
            ct = catt[tt % 2]
            zc = zch[tt % 2]
            S.dma("sp", ct[:, :, :], catT_d.ap[:, :, tt * 512:(tt + 1) * 512].rearrange("c p t -> p c t"), r=[catT_d], w=[ct])
            for oc in range(KC):
                pb = PS[oc % 4]
                xct = xc[oc % 3]
                S.dma("sp", xct[:, :], xin.ap[oc, :, tt * 512:(tt + 1) * 512], r=[xin], w=[xct])
                mm_group(pb[:, :], [(wout[:, kc, oc * 128:(oc + 1) * 128], ct[:, kc, :]) for kc in range(KC)], [wout, ct], [pb])
                dve(lambda E, pb=pb, oc=oc, xct=xct, zt=zc[oc]: E.scalar_tensor_tensor(out=zt[:, :], in0=pb[:, :], scalar=g1a[l][:, oc:oc + 1], in1=xct[:, :],
                                                                                     op0=ALU.mult, op1=ALU.add), [pb, g1a[l], xct], [zc[oc]])

            def out_x(c, zt, tt=tt):
                xoc = xo[c % 3]
                A_ident(xoc[:, :], zt[:, :], ln1w[l][:, c:c + 1], ln1b[l][:, c:c + 1], [zt, ln1w[l], ln1b[l]], [xoc])
                S.dma("sp", xmid.ap[c, :, tt * 512:(tt + 1) * 512], xoc[:, :], r=[xoc], w=[xmid])

            def out_h(c, zt, tt=tt):
                hoc = ho[c % 3]
                A_ident(hoc[:, :], zt[:, :], A2[l][:, c:c + 1], B2[l][:, c:c + 1], [zt, A2[l], B2[l]], [hoc])
                S.dma("sp", h2T_d.ap[c, :, tt * 512:(tt + 1) * 512], hoc[:, :], r=[hoc], w=[h2T_d])
            ln_tile(zc, tb3, ln1w[l], ln1b[l], out_x, out_h)
        S.barrier()
        A.top = hT_base
        ckpt(f"l{l}p3", [("x1T", xmid.ap, [KC, 128, T], F32, [xmid]), ("h2T", h2T_d.ap, [KC, 128, T], BF16, [h2T_d])])

        actT = A.bf16([128, NH, 1024], "actT", ww=True)
        p4 = A.top
        last = (l == L - 1)
        for hf in range(2):
            A.top = p4
            S.label = f"l{l}_up{hf}"
            t0 = hf * 1024
            h2h = A.bf16([128, KC, 1024], "h2h")
            halo = A.bf16([128, KC, 2], "halo")
            up_top = A.top
            S.dma("sp", h2h[:, :, :], h2T_d.ap[:, :, t0:t0 + 1024].rearrange("c p t -> p c t"), r=[h2T_d], w=[h2h])
            ht = 1024 if hf == 0 else 1023
            S.dma("sp", halo[:, :, 0:1], h2T_d.ap[:, :, ht:ht + 1].rearrange("c p t -> p c t"), r=[h2T_d], w=[halo], slow=True)
            wup = [[A.bf16([128, KC, 256], f"wupa{i}") for i in range(2)], [A.bf16([128, KC, 256], f"wupg{i}") for i in range(2)]]
            U = [[A.f32([128, 4, 258], f"Ua{i}") for i in range(2)], [A.f32([128, 4, 258], f"Ug{i}") for i in range(2)]]
            acc = [[A.f32([128, 1024], f"acca{i}") for i in range(2)], [A.f32([128, 1024], f"accg{i}") for i in range(2)]]
            for ag in range(2):
                for i in range(2):
                    dve(lambda E, ag=ag, i=i: E.memset(U[ag][i][:, :, :], 0.0), [], [U[ag][i]])
            for j in range(NH):
                if j % 2 == 0:
                    for ag in range(2):
                        wt = wup[ag][(j // 2) % 2]
                        load_w(wt, w_up_d[l][:, ag * DFF + j * 128: ag * DFF + j * 128 + 256])
                for ag in range(2):
                    wt = wup[ag][(j // 2) % 2]
                    wcol = (j % 2)
                    Ut = U[ag][j % 2]
                    at = acc[ag][j % 2]
                    idx = ag * NH + j
                    for t2_ in range(2):
                        pb = PS[ag * 2 + t2_]
                        mm_group(pb[:, :], [(wt[:, kc, wcol * 128:(wcol + 1) * 128], h2h[:, kc, t2_ * 512:(t2_ + 1) * 512]) for kc in range(KC)], [wt, h2h], [pb])
                        A_copy(Ut[:, 2 * t2_:2 * t2_ + 2, 1:257], pb[:, :].rearrange("p (s t) -> p s t", s=2), [pb], [Ut])
                    pmn = PS[4 + j % 2]
                    mcol = ag * 4
                    mm_group(pmn[:, mcol:mcol + 1], [(wt[:, kc, wcol * 128:(wcol + 1) * 128], halo[:, kc, 0:1]) for kc in range(KC)], [wt, halo], [pmn])
                    dve(lambda E, Ut=Ut: E.tensor_scalar(out=Ut[:, 1:4, 0:1], in0=Ut[:, 0:3, 256:257], scalar1=cont, scalar2=None, op0=ALU.mult), [Ut, flags], [Ut])
                    dve(lambda E, Ut=Ut: E.tensor_scalar(out=Ut[:, 0:3, 257:258], in0=Ut[:, 1:4, 1:2], scalar1=cont, scalar2=None, op0=ALU.mult), [Ut, flags], [Ut])
                    if hf == 0:
                        dve(lambda E, Ut=Ut, pmn=pmn, mcol=mcol: E.tensor_scalar(out=Ut[:, 3, 257:258], in0=pmn[:, mcol:mcol + 1], scalar1=cont, scalar2=None, op0=ALU.mult),
                            [pmn, flags], [Ut])
                    else:
                        dve(lambda E, Ut=Ut, pmn=pmn, mcol=mcol: E.tensor_scalar(out=Ut[:, 0, 0:1], in0=pmn[:, mcol:mcol + 1], scalar1=cont, scalar2=None, op0=ALU.mult),
                            [pmn, flags], [Ut])
                    a3 = at[:, :].rearrange("p (s t) -> p s t", s=4)
                    A_ident(a3, Ut[:, :, 1:257], convw[l][:, 88 + idx:88 + idx + 1], convb[l][:, idx:idx + 1], [Ut, convw[l], convb[l]], [at])
                    dve(lambda E, a3=a3, Ut=Ut, idx=idx: E.scalar_tensor_tensor(out=a3, in0=Ut[:, :, 0:256], scalar=convw[l][:, idx:idx + 1], in1=a3, op0=ALU.mult, op1=ALU.add),
                        [Ut, convw[l], at], [at])
                    dve(lambda E, a3=a3, Ut=Ut, idx=idx: E.scalar_tensor_tensor(out=a3, in0=Ut[:, :, 2:258], scalar=convw[l][:, 176 + idx:176 + idx + 1], in1=a3, op0=ALU.mult, op1=ALU.add),
                        [Ut, convw[l], at], [at])
                aa, gg = acc[0][j % 2], acc[1][j % 2]
                act(lambda E, gg=gg: E.activation(out=gg[:, :], in_=gg[:, :], func=AF.Silu), [gg], [gg])
                dve(lambda E, aa=aa, gg=gg, j=j: E.tensor_tensor(out=actT[:, j, :], in0=aa[:, :], in1=gg[:, :], op=ALU.mult), [aa, gg], [actT])
            S.barrier()
            S.label = f"l{l}_dn{hf}"
            A.top = p4
            wdn = [A.bf16([128, NH, 256], f"wdn{i}") for i in range(2)]
            z2 = A.f32([128, KC, 512], "z2")
            z2c = mk_chunks(z2, KC)
            xc2 = [A.f32([128, 512], f"xc2{i}") for i in range(2)]
            tb4 = ([A.bf16([128, 512], f"l2sq{i}") for i in range(2)], [A.bf16([128, 512], f"l2zb{i}") for i in range(2)],
                   A.f32([128, 512], "l2mean"), A.f32([128, 512], "l2rstd"), PS[4], PS[5])
            xo2 = [A.f32([128, 512], f"l2xo{i}") for i in range(2)]
            xTo = [A.f32([128, 512], f"xTo{i}") for i in range(2)]
            for t2_ in range(2):
                tt = hf * 2 + t2_
                for op_ in range(8):
                    wt = wdn[op_ % 2]
                    load_w(wt, w_down_d[l][:, op_ * 256:(op_ + 1) * 256])
                    for o2 in range(2):
                        oc = op_ * 2 + o2
                        pb = PS[oc % 4]
                        xct = xc2[oc % 2]
                        S.dma("sp", xct[:, :], xmid.ap[oc, :, tt * 512:(tt + 1) * 512], r=[xmid], w=[xct])
                        mm_group(pb[:, :], [(wt[:, j, o2 * 128:(o2 + 1) * 128], actT[:, j, t2_ * 512:(t2_ + 1) * 512]) for j in range(NH)], [wt, actT], [pb])
                        dve(lambda E, pb=pb, oc=oc, xct=xct, zt=z2c[oc]: E.scalar_tensor_tensor(out=zt[:, :], in0=pb[:, :], scalar=g2a[l][:, oc:oc + 1], in1=xct[:, :],
                                                                                              op0=ALU.mult, op1=ALU.add), [pb, g2a[l], xct], [z2c[oc]])

                def out_x2(c, zt, tt=tt):
                    xoc = xo2[c % 2]
                    A_ident(xoc[:, :], zt[:, :], ln2w[l][:, c:c + 1], ln2b[l][:, c:c + 1], [zt, ln2w[l], ln2b[l]], [xoc])
                    if not last:
                        S.dma("sp", xin.ap[c, :, tt * 512:(tt + 1) * 512], xoc[:, :], r=[xoc], w=[xin])
                    else:
                        pt = PS[6 + c % 2]

                        def fn(E, pt=pt, xoc=xoc):
                            ins = None
                            for j in range(4):
                                ins = E.transpose(pt[:, j * 128:(j + 1) * 128], xoc[:, j * 128:(j + 1) * 128], ident_f[:, :])
                            return ins
                        pe(fn, [xoc, ident_f], [pt])
                        yo = xTo[c % 2]
                        A_copy(yo[:, :], pt[:, :], [pt], [yo])
                        S.dma("sp", y_o[tt * 512:(tt + 1) * 512, c * 128:(c + 1) * 128].rearrange("(j p) f -> p j f", p=128),
                              yo[:, :].rearrange("p (j f) -> p j f", j=4), r=[yo])

                def out_h2(c, zt, tt=tt):
                    hoc = xTo[c % 2]
                    hob = hoc[:, 0:256].bitcast(BF16)
                    A_ident(hob, zt[:, :], A1n[:, c:c + 1], B1n[:, c:c + 1], [zt, A1n, B1n], [hoc])
                    S.dma("sp", catT_d.ap[c, :, tt * 512:(tt + 1) * 512], hob, r=[hoc], w=[catT_d])
                ln_tile(z2c, tb4, ln2w[l], ln2b[l], out_x2, None if last else out_h2)
            S.barrier()
        A.top = work0
        ckpt(f"l{l}p4", [("x2T", xin.ap, [KC, 128, T], F32, [xin])])
        if not last:
            for c in range(KC):
                S.dma("sp", hT[:, c, :], catT_d.ap[c, :, :], r=[catT_d], w=[hT])
            S.barrier(new_epoch=True)

    for l_ in range(L):
        layer(l_)
    S.barrier()
    S.emit()


def make_ln_finish(S, act, dve, A_ident, z, mean, rstd, s1p, s2p):
    def fin():
        act(lambda E: E.activation(out=mean[:, :], in_=s1p[:, :], func=AF.Identity, scale=1.0 / D), [s1p], [mean])
        dve(lambda E: E.tensor_tensor(out=rstd[:, :], in0=mean[:, :], in1=mean[:, :], op=ALU.mult), [mean], [rstd])
        dve(lambda E: E.scalar_tensor_tensor(out=rstd[:, :], in0=s2p[:, :], scalar=1.0 / D, in1=rstd[:, :], op0=ALU.mult, op1=ALU.subtract), [s2p, rstd], [rstd])
        dve(lambda E: E.tensor_scalar(out=rstd[:, :], in0=rstd[:, :], scalar1=EPS1, scalar2=None, op0=ALU.add), [rstd], [rstd])
        act(lambda E: E.activation(out=rstd[:, :], in_=rstd[:, :], func=AF.Sqrt), [rstd], [rstd])
        dve(lambda E: E.reciprocal(out=rstd[:, :], in_=rstd[:, :]), [rstd], [rstd])
    return fin


def _consts(is_sample):
    ident = np.eye(128, dtype=np.float32)
    P = np.zeros((128, 128), np.float32)
    for d in range(128):
        if (d % 64) < 32:
            P[d + 32, d] = -1.0
        else:
            P[d - 32, d] = 1.0
    cosT = np.ones((128, T), np.float32)
    sinT = np.zeros((128, T), np.float32)
    if is_sample:
        t = np.arange(T)
        row = (t // 64).astype(np.float32)
        col = (t % 64).astype(np.float32)
        freqs = (np.float32(10000.0) ** (-np.arange(32, dtype=np.float32) / np.float32(32))).astype(np.float32)
        for d in range(128):
            pos = row if d < 64 else col
            ang = (pos * freqs[d % 32]).astype(np.float32)
            cosT[d] = np.cos(ang)
            sinT[d] = np.sin(ang)
    kj = np.arange(128)[:, None]
    qi = np.arange(128)[None, :]
    masks = np.ones((128, 4, 128), np.float32)
    if is_sample:
        masks[:, 0, :] = (kj >= qi)
        masks[:, 1, :] = (kj <= qi)
    masks[:, 2, :] = (kj <= qi)
    masks[:, 3, :] = (kj >= qi)
    abias = np.zeros((128, 8), np.float32)
    if not is_sample:
        abias[:, 0] = -30000.0
        abias[:, 3] = -30000.0
        abias[:, 4] = -30000.0
    flags = np.zeros((128, 2), np.float32)
    flags[:, 0] = 1.0 if is_sample else 0.0
    return dict(ident=ident, ropeP=P, cosT=cosT, sinT=sinT, masks=masks, abias=abias, flags=flags)


_NC_CACHE = {}


def make_in_maps(x_prompt, x_sample, cache_k, cache_v, state_gla_fwd, state_gla_bwd, c, c_ctx,
                 w_ada, b_ada, w_in, attn_sink, w_gate_f, b_gate_f, w_gate_b, b_gate_b, gla_norm_w,
                 w_out, ln1_w, ln1_b, w_up, conv_w, conv_b, w_down, ln2_w, ln2_b):
    f = lambda a: np.ascontiguousarray(np.asarray(a, dtype=np.float32))
    shared = dict(
        w_ada=f(w_ada), b_ada=f(b_ada).reshape(L, 96, 128), w_in=f(w_in), attn_sink=f(attn_sink),
        w_gate_f=f(w_gate_f), b_gate_f=f(b_gate_f).reshape(L, 4, 128), w_gate_b=f(w_gate_b), b_gate_b=f(b_gate_b).reshape(L, 4, 128),
        gla_norm_w=f(gla_norm_w).reshape(L, 2, 128), w_out=f(w_out), ln1_w=f(ln1_w).reshape(L, KC, 128), ln1_b=f(ln1_b).reshape(L, KC, 128),
        w_up=f(w_up), conv_w=f(conv_w).reshape(L, 3 * 88, 128), conv_b=f(conv_b).reshape(L, 88, 128), w_down=f(w_down),
        ln2_w=f(ln2_w).reshape(L, KC, 128), ln2_b=f(ln2_b).reshape(L, KC, 128))
    x_prompt = f(x_prompt)
    x_sample = f(x_sample)
    cs = [_consts(False), _consts(True)]
    in_maps = []
    for core in range(8):
        m = dict(shared)
        if core < 4:
            m.update(cs[0])
            m["x"] = x_prompt[core * 8:(core + 1) * 8].reshape(T, D)
            m["cvec"] = f(c_ctx).reshape(KC, 128)
            m["ck"] = np.zeros((L, 256, 2, 128), np.float32)
            m["cv"] = np.zeros((L, 256, 2, 128), np.float32)
            m["s0f"] = np.zeros((L, 4, 128, 256), np.float32)
            m["s0b"] = np.zeros((L, 4, 128, 256), np.float32)
        else:
            b = core - 4
            m.update(cs[1])
            m["x"] = x_sample[b]
            m["cvec"] = f(c)[b].reshape(KC, 128)
            m["ck"] = f(cache_k)[b]
            m["cv"] = f(cache_v)[b]
            m["s0f"] = f(state_gla_fwd)[b]
            m["s0b"] = f(state_gla_bwd)[b]
        in_maps.append(m)
    return in_maps


def kernel(**inputs):
    in_maps = make_in_maps(**inputs)
    if "nc" not in _NC_CACHE:
        _NC_CACHE["nc"] = build_program()
    nc = _NC_CACHE["nc"]
    res = run_bass_kernel_spmd(nc, in_maps, core_ids=list(range(8)))
    R = res.results
    y_prompt = np.stack([R[cr]["y"] for cr in range(4)]).reshape(32, 256, D)
    y_sample = np.stack([R[cr]["y"] for cr in range(4, 8)]).reshape(4, T, D)
    nk = np.concatenate([R[cr]["kc_o"].reshape(L, 8, 256, 2, 128).transpose(1, 0, 2, 3, 4) for cr in range(4)], axis=0)
    nv = np.concatenate([R[cr]["vc_o"].reshape(L, 8, 256, 2, 128).transpose(1, 0, 2, 3, 4) for cr in range(4)], axis=0)
    sf = np.concatenate([R[cr]["sf_o"].transpose(1, 0, 2, 3, 4) for cr in range(4)], axis=0)
    sb = np.concatenate([R[cr]["sb_o"].transpose(1, 0, 2, 3, 4) for cr in range(4)], axis=0)
    return (np.ascontiguousarray(y_prompt), np.ascontiguousarray(y_sample), np.ascontiguousarray(nk), np.ascontiguousarray(nv),
            np.ascontiguousarray(sf), np.ascontiguousarray(sb))
```

```python
import numpy as np
import concourse.bass as bass
import concourse.mybir as mybir
from concourse.bass_utils import run_bass_kernel_spmd

F32 = mybir.dt.float32
BF16 = mybir.dt.bfloat16
AF = mybir.ActivationFunctionType
ALU = mybir.AluOpType

T = 2048
D = 2048
KC = 16
L = 2
IN_W = 4640
DFF = 5632
NH = 44
Q0, K0, V0, GQ0, GK0, GV0, GO0, LR0 = 0, 1024, 1280, 1536, 2048, 2560, 3584, 4608
ALPHA = 4.0 ** 0.25
EPS1 = 1e-5 / (ALPHA * ALPHA)
SELF_SYNC = True
RAW_ONLY_SELF = True
OVERLAP_MODS = True
SCRATCH_AS_OUTPUT = True
SCOPES = False
import os as _os
DBGF = set(_os.environ.get('KDBG', '').split(','))
RING = 16


class Ev:
    __slots__ = ("sem", "val", "key")

    def __init__(self, sem, val, key):
        self.sem, self.val, self.key = sem, val, key


class Tl:
    def __init__(self, ap, name="", ww=False):
        self.ap = ap
        self.w = {}
        self.r = {}
        self.name = name
        self.ww = ww

    def __getitem__(self, idx):
        return self.ap[idx]


class Sched:
    def __init__(self, nc):
        self.nc = nc
        self.q = {e: [] for e in ("pe", "act", "dve", "pool", "sp")}
        self.seen = {e: {} for e in self.q}
        self.esem = {}
        self.epoch = 0
        self.ring = {qn: [[nc.alloc_semaphore(f"r_{qn}_{i}"), 0] for i in range(RING)] for qn in ("sp", "pool")}
        self.rr = {"sp": 0, "pool": 0}
        self.label = "init"
        self.new_epoch()

    def new_epoch(self):
        self.epoch += 1
        for e in ("pe", "act", "dve"):
            self.esem[e] = [self.nc.alloc_semaphore(f"e_{e}_{self.epoch}"), 0]

    def _wait(self, eng, ev):
        if ev.key[0] == eng and (eng == "pe" or not SELF_SYNC):
            return
        if self.seen[eng].get(ev.key, 0) >= ev.val:
            return
        self.seen[eng][ev.key] = ev.val
        sem, val = ev.sem, ev.val
        self.q[eng].append((self.label, lambda E, sem=sem, val=val: E.wait_ge(sem, val)))

    def _deps(self, eng, r, w):
        for t in r:
            for ev in t.w.values():
                self._wait(eng, ev)
        for t in w:
            if not t.ww:
                for ev in t.w.values():
                    if ev.key[0] == eng and RAW_ONLY_SELF:
                        continue
                    self._wait(eng, ev)
            for ev in t.r.values():
                if ev.key[0] == eng and RAW_ONLY_SELF:
                    continue
                self._wait(eng, ev)

    def _record(self, ev, r, w):
        for t in r:
            t.r[ev.key] = ev
        for t in w:
            if t.ww:
                t.w[ev.key] = ev
            else:
                t.w = {ev.key: ev}
                t.r = {}

    def op(self, eng, fn, r=(), w=()):
        self._deps(eng, r, w)
        s = self.esem[eng]
        s[1] += 1
        sem = s[0]
        ev = Ev(sem, s[1], (eng, self.epoch))
        self.q[eng].append((self.label, lambda E, fn=fn, sem=sem: fn(E).then_inc(sem, 1)))
        self._record(ev, r, w)

    def dma(self, qn, out, in_, r=(), w=(), slow=False):
        self._deps(qn, r, w)
        i = self.rr[qn] % RING
        self.rr[qn] += 1
        slot = self.ring[qn][i]
        slot[1] += 16
        sem = slot[0]
        ev = Ev(sem, slot[1], ("dma", qn, i))
        if slow:
            self.q[qn].append((self.label, lambda E, out=out, in_=in_, sem=sem: E.dma_start(out=out, in_=in_, allow_slow_non_contiguous=True).then_inc(sem, 16)))
        else:
            self.q[qn].append((self.label, lambda E, out=out, in_=in_, sem=sem: E.dma_start(out=out, in_=in_).then_inc(sem, 16)))
        self._record(ev, r, w)

    def barrier(self, new_epoch=False):
        evs = []
        for e in ("pe", "act", "dve"):
            s = self.esem[e]
            if s[1] > 0:
                evs.append(Ev(s[0], s[1], (e, self.epoch)))
        for qn in ("sp", "pool"):
            for i, (sem, val) in enumerate(self.ring[qn]):
                if val > 0:
                    evs.append(Ev(sem, val, ("dma", qn, i)))
        for e in self.q:
            for ev in evs:
                if ev.key[0] == e and e == "pe":
                    continue
                if self.seen[e].get(ev.key, 0) >= ev.val:
                    continue
                self.seen[e][ev.key] = ev.val
                self.q[e].append((self.label, lambda E, sem=ev.sem, val=ev.val: E.wait_ge(sem, val)))
        if new_epoch:
            self.new_epoch()

    def emit(self):
        nc = self.nc
        q = self.q

        def run(E, items):
            if not SCOPES:
                for _, f in items:
                    f(E)
                return
            cur = None
            sid = None
            for lab, f in items:
                if lab != cur:
                    if cur is not None:
                        nc.leave_named_scope(cur, sid, False)
                    sid, _ = nc.enter_named_scope(lab, False)
                    cur = lab
                f(E)
            if cur is not None:
                nc.leave_named_scope(cur, sid, False)

        with nc.Block() as block:
            @block.tensor
            def _(E):
                run(E, q["pe"])

            @block.scalar
            def _(E):
                run(E, q["act"])

            @block.vector
            def _(E):
                run(E, q["dve"])

            @block.gpsimd
            def _(E):
                run(E, q["pool"])

            @block.sync
            def _(E):
                run(E, q["sp"])


class Arena:
    def __init__(self, nc, ncols):
        self.t = nc.alloc_sbuf_tensor("arena", [128, ncols], F32)
        self.n = ncols
        self.top = 0

    def _take(self, cols):
        cols = (cols + 7) // 8 * 8
        a = self.top
        assert a + cols <= self.n, f"arena overflow {a}+{cols}>{self.n}"
        self.top = a + cols
        return a, cols

    def f32(self, shape, name="", ww=False):
        n = int(np.prod(shape[1:]))
        a, c = self._take(n)
        ap = self.t[0:shape[0], a:a + n]
        if len(shape) == 3:
            ap = ap.rearrange("p (a b) -> p a b", a=shape[1])
        return Tl(ap, name, ww)

    def bf16(self, shape, name="", ww=False):
        n = int(np.prod(shape[1:]))
        a, c = self._take((n + 1) // 2)
        ap = self.t[0:shape[0], a:a + (n + 1) // 2].bitcast(BF16)[:, 0:n]
        if len(shape) == 3:
            ap = ap.rearrange("p (a b) -> p a b", a=shape[1])
        return Tl(ap, name, ww)


class _Stop(Exception):
    pass


def build_program(stop_after=None):
    nc = bass.Bass("TRN2", target_bir_lowering=False)
    try:
        _build_body(nc, stop_after)
    except _Stop:
        pass
    return nc


def _build_body(nc, stop_after):

    def din(name, shape):
        return nc.dram_tensor(name, list(shape), F32, kind="ExternalInput").ap()

    def dout(name, shape):
        return nc.dram_tensor(name, list(shape), F32, kind="ExternalOutput").ap()

    x_d = din("x", [T, D])
    cvec_d = din("cvec", [KC, 128])
    ck_d = din("ck", [L, 256, 2, 128])
    cv_d = din("cv", [L, 256, 2, 128])
    s0f_d = din("s0f", [L, 4, 128, 256])
    s0b_d = din("s0b", [L, 4, 128, 256])
    ident_d = din("ident", [128, 128])
    ropeP_d = din("ropeP", [128, 128])
    cos_d = din("cosT", [128, T])
    sin_d = din("sinT", [128, T])
    masks_d = din("masks", [128, 4, 128])
    abias_d = din("abias", [128, 8])
    flags_d = din("flags", [128, 2])
    w_ada_d = din("w_ada", [L, D, 6 * D])
    b_ada_d = din("b_ada", [L, 96, 128])
    w_in_d = din("w_in", [L, D, IN_W])
    sink_d = din("attn_sink", [L, 8])
    wgf_d = din("w_gate_f", [L, 16, 512])
    bgf_d = din("b_gate_f", [L, 4, 128])
    wgb_d = din("w_gate_b", [L, 16, 512])
    bgb_d = din("b_gate_b", [L, 4, 128])
    gnw_d = din("gla_norm_w", [L, 2, 128])
    w_out_d = din("w_out", [L, D, D])
    ln1w_d = din("ln1_w", [L, KC, 128])
    ln1b_d = din("ln1_b", [L, KC, 128])
    w_up_d = din("w_up", [L, D, 2 * DFF])
    convw_d = din("conv_w", [L, 3 * 88, 128])
    convb_d = din("conv_b", [L, 88, 128])
    w_down_d = din("w_down", [L, DFF, D])
    ln2w_d = din("ln2_w", [L, KC, 128])
    ln2b_d = din("ln2_b", [L, KC, 128])

    y_o = dout("y", [T, D])
    kc_o = dout("kc_o", [L, T, 2, 128])
    vc_o = dout("vc_o", [L, T, 2, 128])
    sf_o = dout("sf_o", [L, 8, 4, 128, 256])
    sb_o = dout("sb_o", [L, 8, 4, 128, 256])

    SCR = dict(kind="ExternalOutput") if SCRATCH_AS_OUTPUT else {}
    xT_d = [Tl(nc.dram_tensor(f"xT_s{i}", [KC, 128, T], F32, **SCR).ap(), f"xT{i}", True) for i in range(2)]
    catT_d = Tl(nc.dram_tensor("catT_s", [KC, 128, T], BF16, **SCR).ap(), "catT", True)
    h2T_d = Tl(nc.dram_tensor("h2T_s", [KC, 128, T], BF16, **SCR).ap(), "h2T", True)

    S = Sched(nc)
    A = Arena(nc, 51800)
    PS = [Tl(nc.alloc_psum_tensor(f"ps{i}", [128, 512], F32)[:, :], f"ps{i}") for i in range(8)]

    def ckpt(name, dumps=()):
        if stop_after != name:
            return
        S.barrier()
        for label, src_ap, shape, dt, deps in dumps:
            o = nc.dram_tensor("dbg_" + label, list(shape), dt, kind="ExternalOutput").ap()
            S.dma("sp", o, src_ap, r=deps)
        S.barrier()
        S.emit()
        raise _Stop

    def act(fn, r=(), w=()):
        S.op("act", fn, r, w)

    def dve(fn, r=(), w=()):
        S.op("dve", fn, r, w)

    def pe(fn, r=(), w=()):
        S.op("pe", fn, r, w)

    def A_copy(out, in_, r, w):
        act(lambda E: E.activation(out=out, in_=in_, func=AF.Copy), r, w)

    def A_ident(out, in_, scale, bias, r, w):
        act(lambda E: E.activation(out=out, in_=in_, func=AF.Identity, scale=scale, bias=bias), r, w)

    def mm_group(out_ap, pairs, r, w):
        n = len(pairs)

        def fn(E):
            ins = None
            for i, (lt, rh) in enumerate(pairs):
                ins = E.matmul(out_ap, lt, rh, start=(i == 0), stop=(i == n - 1))
            return ins
        pe(fn, r, w)

    ident_f = A.f32([128, 128], "ident_f")
    ident_b = A.bf16([128, 128], "ident_b")
    ropeP_b = A.bf16([128, 128], "ropeP_b")
    masks_b = A.bf16([128, 4, 128], "masks_b")
    ones_b = A.bf16([128, 128], "ones_b")
    ones_f = A.f32([128, 128], "ones_f")
    abias = A.f32([128, 8], "abias")
    flags = A.f32([128, 2], "flags")
    mods = [A.f32([128, 96], f"mods{l}") for l in range(L)]
    sc1p = [A.f32([128, 16], f"sc1p{l}") for l in range(L)]
    sc2p = [A.f32([128, 16], f"sc2p{l}") for l in range(L)]
    g1a = [A.f32([128, 16], f"g1a{l}") for l in range(L)]
    g2a = [A.f32([128, 16], f"g2a{l}") for l in range(L)]
    ln1w = [A.f32([128, 16], f"ln1w{l}") for l in range(L)]
    ln1b = [A.f32([128, 16], f"ln1b{l}") for l in range(L)]
    ln2w = [A.f32([128, 16], f"ln2w{l}") for l in range(L)]
    ln2b = [A.f32([128, 16], f"ln2b{l}") for l in range(L)]
    A2 = [A.f32([128, 16], f"A2{l}") for l in range(L)]
    B2 = [A.f32([128, 16], f"B2{l}") for l in range(L)]
    A1n = A.f32([128, 16], "A1n")
    B1n = A.f32([128, 16], "B1n")
    convw = [A.f32([128, 3 * 88], f"convw{l}") for l in range(L)]
    convb = [A.f32([128, 88], f"convb{l}") for l in range(L)]
    nbg = [A.f32([128, 8], f"nbg{l}") for l in range(L)]
    gnw = [A.f32([128, 2], f"gnw{l}") for l in range(L)]
    esink = [A.f32([128, 8], f"esink{l}") for l in range(L)]
    wgbd = [A.bf16([32, 1024], f"wgbd{l}") for l in range(L)]
    cT_b = A.bf16([128, 16], "cT_b")
    stage = A.f32([128, 128], "stage")
    stage2 = A.f32([128, 128], "stage2")
    base_top = A.top

    tmpc = A.f32([128, 512], "tmpc")
    S.dma("sp", ident_f[:, :], ident_d[:, :], w=[ident_f])
    dve(lambda E: E.tensor_copy(out=ident_b[:, :], in_=ident_f[:, :]), [ident_f], [ident_b])
    S.dma("sp", tmpc[:, 0:128], ropeP_d[:, :], w=[tmpc])
    dve(lambda E: E.tensor_copy(out=ropeP_b[:, :], in_=tmpc[:, 0:128]), [tmpc], [ropeP_b])
    S.dma("sp", tmpc[:, :], masks_d.rearrange("p a b -> p (a b)"), w=[tmpc])
    dve(lambda E: E.tensor_copy(out=masks_b.ap.rearrange("p a b -> p (a b)"), in_=tmpc[:, :]), [tmpc], [masks_b])
    dve(lambda E: E.memset(ones_b[:, :], 1.0), [], [ones_b])
    dve(lambda E: E.memset(ones_f[:, :], 1.0), [], [ones_f])
    S.dma("sp", abias[:, :], abias_d[:, :], w=[abias])
    S.dma("sp", flags[:, :], flags_d[:, :], w=[flags])
    cont = flags.ap[:, 0:1]

    def vec_fm(dst_tl, dst_ap, src_ap, n, st=None):
        st = st or stage
        S.dma("sp", st[0:n, :], src_ap, w=[st])
        pe(lambda E: E.transpose(PS[7][:, 0:n], st[0:n, :], ident_f[0:n, 0:n]), [st, ident_f], [PS[7]])
        A_copy(dst_ap, PS[7][:, 0:n], [PS[7]], [dst_tl])

    for l in range(L):
        vec_fm(ln1w[l], ln1w[l][:, :], ln1w_d[l], 16)
        vec_fm(ln1b[l], ln1b[l][:, :], ln1b_d[l], 16, stage2)
        vec_fm(ln2w[l], ln2w[l][:, :], ln2w_d[l], 16)
        vec_fm(ln2b[l], ln2b[l][:, :], ln2b_d[l], 16, stage2)
        for k in range(3):
            vec_fm(convw[l], convw[l][:, k * 88:(k + 1) * 88], convw_d[l, k * 88:(k + 1) * 88, :], 88, stage if k % 2 == 0 else stage2)
        vec_fm(convb[l], convb[l][:, :], convb_d[l], 88, stage2)
        vec_fm(nbg[l], nbg[l][:, 0:4], bgf_d[l], 4)
        vec_fm(nbg[l], nbg[l][:, 4:8], bgb_d[l], 4, stage2)
        dve(lambda E, l=l: E.tensor_scalar(out=nbg[l][:, :], in0=nbg[l][:, :], scalar1=-1.0, scalar2=None, op0=ALU.mult), [nbg[l]], [nbg[l]])
        vec_fm(gnw[l], gnw[l][:, :], gnw_d[l], 2)
        S.dma("sp", esink[l][:, :], sink_d[l].partition_broadcast(128), w=[esink[l]])
        act(lambda E, l=l: E.activation(out=esink[l][:, :], in_=esink[l][:, :], func=AF.Exp), [esink[l]], [esink[l]])
        dve(lambda E, l=l: E.memset(wgbd[l][:, :], 0.0), [], [wgbd[l]])
        S.dma("pool", wgbd[l][0:16, 0:512], wgf_d[l], w=[wgbd[l]])
        S.dma("pool", wgbd[l][16:32, 512:1024], wgb_d[l], w=[wgbd[l]])

    ckpt("consts", [("ln1w0", ln1w[0][:, :], [128, 16], F32, [ln1w[0]]), ("convw0", convw[0][:, :], [128, 264], F32, [convw[0]]),
                    ("nbg0", nbg[0][:, :], [128, 8], F32, [nbg[0]]), ("esink0", esink[0][:, :], [128, 8], F32, [esink[0]]),
                    ("wgbd0", wgbd[0][:, :], [32, 1024], BF16, [wgbd[0]]), ("masks", masks_b.ap, [128, 4, 128], BF16, [masks_b])])
    S.label = "M"
    S.dma("sp", stage[0:16, :], cvec_d[:, :], w=[stage])
    pe(lambda E: E.transpose(PS[7][:, 0:16], stage[0:16, :], ident_f[0:16, 0:16]), [stage, ident_f], [PS[7]])
    act(lambda E: E.activation(out=cT_b[:, :], in_=PS[7][:, 0:16], func=AF.Silu), [PS[7]], [cT_b])
    m0 = A.top
    bada = A.f32([128, 96], "bada")
    m1 = A.top

    def mods_steps(l, wada, psm):
        steps = []
        nsp = 12288 // wada[0].ap.shape[1]
        jn = 96 // nsp
        cnt = 0
        for kc in range(KC):
            for sp_ in range(nsp):
                def step(kc=kc, sp_=sp_, cnt=cnt):
                    wb = wada[cnt % 2]
                    S.dma("pool", wb[:, :], w_ada_d[l, kc * 128:(kc + 1) * 128, sp_ * jn * 128:(sp_ + 1) * jn * 128], w=[wb])

                    def fn(E, wb=wb, kc=kc, sp_=sp_):
                        ins = None
                        for jj in range(jn):
                            j = sp_ * jn + jj
                            ins = E.matmul(psm[:, j:j + 1], wb[:, jj * 128:(jj + 1) * 128], cT_b[:, kc:kc + 1],
                                           start=(kc == 0 and j == 0), stop=(kc == KC - 1), skip_group_check=True)
                        return ins
                    pe(fn, [wb, cT_b], [psm])
                steps.append(step)
                cnt += 1

        def fin():
            S.dma("sp", stage[0:96, :], b_ada_d[l], w=[stage])
            pe(lambda E: E.transpose(psm[:, 128:224], stage[0:96, :], ident_f[0:96, 0:96]), [stage, ident_f], [psm])
            A_copy(bada[:, :], psm[:, 128:224], [psm], [bada])
            dve(lambda E: E.tensor_tensor(out=mods[l][:, :], in0=psm[:, 0:96], in1=bada[:, :], op=ALU.add), [psm, bada], [mods[l]])
            dve(lambda E: E.tensor_scalar(out=sc1p[l][:, :], in0=mods[l][:, 16:32], scalar1=1.0, scalar2=None, op0=ALU.add), [mods[l]], [sc1p[l]])
            dve(lambda E: E.tensor_scalar(out=sc2p[l][:, :], in0=mods[l][:, 64:80], scalar1=1.0, scalar2=None, op0=ALU.add), [mods[l]], [sc2p[l]])
            dve(lambda E: E.tensor_scalar(out=g1a[l][:, :], in0=mods[l][:, 32:48], scalar1=1.0 / ALPHA, scalar2=None, op0=ALU.mult), [mods[l]], [g1a[l]])
            dve(lambda E: E.tensor_scalar(out=g2a[l][:, :], in0=mods[l][:, 80:96], scalar1=1.0 / ALPHA, scalar2=None, op0=ALU.mult), [mods[l]], [g2a[l]])
            dve(lambda E: E.tensor_tensor(out=A2[l][:, :], in0=ln1w[l][:, :], in1=sc2p[l][:, :], op=ALU.mult), [ln1w[l], sc2p[l]], [A2[l]])
            dve(lambda E: E.tensor_tensor(out=B2[l][:, :], in0=ln1b[l][:, :], in1=sc2p[l][:, :], op=ALU.mult), [ln1b[l], sc2p[l]], [B2[l]])
            dve(lambda E: E.tensor_tensor(out=B2[l][:, :], in0=B2[l][:, :], in1=mods[l][:, 48:64], op=ALU.add), [B2[l], mods[l]], [B2[l]])
            if l == 1:
                dve(lambda E: E.tensor_tensor(out=A1n[:, :], in0=ln2w[0][:, :], in1=sc1p[1][:, :], op=ALU.mult), [ln2w[0], sc1p[1]], [A1n])
                dve(lambda E: E.tensor_tensor(out=B1n[:, :], in0=ln2b[0][:, :], in1=sc1p[1][:, :], op=ALU.mult), [ln2b[0], sc1p[1]], [B1n])
                dve(lambda E: E.tensor_tensor(out=B1n[:, :], in0=B1n[:, :], in1=mods[1][:, 0:16], op=ALU.add), [B1n, mods[1]], [B1n])
        steps.append(fin)
        return steps

    wada0 = [A.bf16([128, 6 * D], f"wada{i}") for i in range(2)]
    for st_ in mods_steps(0, wada0, PS[6]):
        st_()
    if not OVERLAP_MODS:
        for st_ in mods_steps(1, wada0, PS[6]):
            st_()
    S.barrier()
    A.top = m1
    ckpt("M", [("mods0", mods[0][:, :], [128, 96], F32, [mods[0]])])

    hT_base = A.top
    hT = A.bf16([128, KC, T], "hT", ww=True)
    work0 = A.top

    S.label = "p0"
    xb = [A.f32([128, D], f"xb{i}") for i in range(2)]
    xTs = [A.f32([128, KC, 128], f"xTs{i}", ww=True) for i in range(2)]
    for b in range(16):
        xbt = xb[b % 2]
        xs = xTs[b % 2]
        S.dma("sp", xbt[:, :], x_d[b * 128:(b + 1) * 128, :], w=[xbt])
        for c4 in range(4):
            pb = PS[c4 % 4]

            def fn(E, pb=pb, xbt=xbt, c4=c4):
                ins = None
                for j in range(4):
                    c = c4 * 4 + j
                    ins = E.transpose(pb[:, j * 128:(j + 1) * 128], xbt[:, c * 128:(c + 1) * 128], ident_f[:, :])
                return ins
            if 'nope' in DBGF:
                continue
            pe(fn, [xbt, ident_f], [pb])
            if 'noact' not in DBGF:
                A_copy(xs[:, c4 * 4:(c4 + 1) * 4, :], pb[:, :].rearrange("p (a b) -> p a b", a=4), [pb], [xs])
            for j in range(4):
                c = c4 * 4 + j
                A_ident(hT[:, c, b * 128:(b + 1) * 128], pb[:, j * 128:(j + 1) * 128], sc1p[0][:, c:c + 1], mods[0][:, c:c + 1],
                        [pb, sc1p[0], mods[0]], [hT])
        if 'nostore' not in DBGF:
            S.dma("sp", xT_d[0].ap[:, :, b * 128:(b + 1) * 128].rearrange("c p t -> p c t"), xs[:, :, :], r=[xs], w=[xT_d[0]])
    S.barrier(new_epoch=True)
    A.top = work0
    ckpt("p0", [("hT", hT.ap, [128, KC, T], BF16, [hT])] + ([] if 'nodumpx' in DBGF else [("xT", xT_d[0].ap, [KC, 128, T], F32, [xT_d[0]])]) + [
                ("mods0", mods[0][:, :], [128, 96], F32, [mods[0]])])

    def load_w(tl, dram_cols_ap):
        S.dma("pool", tl.ap, dram_cols_ap.rearrange("(kc p) n -> p kc n", p=128), w=[tl])

    def proj_fm(wt, col, tt, ps_tl):
        mm_group(ps_tl[:, :], [(wt[:, kc, col * 128:(col + 1) * 128], hT[:, kc, tt * 512:(tt + 1) * 512]) for kc in range(KC)],
                 [wt, hT], [ps_tl])


    def mk_chunks(tl3, n):
        return [Tl(tl3.ap[:, c, :], f"{tl3.name}_{c}") for c in range(n)]

    def ln_tile(zc, tb, lw, lb, out_x, out_h):
        sqb, zbb, mean, rstd, s1p, s2p = tb
        for c in range(KC):
            sqc, zbc = sqb[c % 2], zbb[c % 2]
            act(lambda E, sqc=sqc, zt=zc[c]: E.activation(out=sqc[:, :], in_=zt[:, :], func=AF.Square), [zc[c]], [sqc])
            dve(lambda E, zbc=zbc, zt=zc[c]: E.tensor_copy(out=zbc[:, :], in_=zt[:, :]), [zc[c]], [zbc])
            pe(lambda E, c=c, zbc=zbc: E.matmul(s1p[:, :], ones_b[:, :], zbc[:, :], start=(c == 0), stop=(c == KC - 1)), [zbc, ones_b], [s1p])
            pe(lambda E, c=c, sqc=sqc: E.matmul(s2p[:, :], ones_b[:, :], sqc[:, :], start=(c == 0), stop=(c == KC - 1)), [sqc, ones_b], [s2p])
        make_ln_finish(S, act, dve, A_ident, None, mean, rstd, s1p, s2p)()
        for c in range(KC):
            zt = zc[c]
            dve(lambda E, zt=zt: E.tensor_tensor(out=zt[:, :], in0=zt[:, :], in1=mean[:, :], op=ALU.subtract), [zt, mean], [zt])
            dve(lambda E, zt=zt: E.tensor_tensor(out=zt[:, :], in0=zt[:, :], in1=rstd[:, :], op=ALU.mult), [zt, rstd], [zt])
            if out_x is not None:
                out_x(c, zt)
            if out_h is not None:
                out_h(c, zt)

    def layer(l):
        xin = xT_d[0]
        xmid = xT_d[1]
        win = w_in_d[l]
        p1 = A.top
        S.label = f"l{l}_attn"
        wbuf = [A.bf16([128, KC, 128], f"wbuf{i}") for i in range(3)]
        wi = [0]

        def next_w(c0, n=128):
            tl = wbuf[wi[0] % 3]
            wi[0] += 1
            load_w(tl, win[:, c0:c0 + n])
            return tl

        qr = A.bf16([128, 16 * 4 * 128], "qr", ww=True)
        kr = A.bf16([128, T], "kr", ww=True)
        vtok = A.bf16([128, 16, 128], "vtok", ww=True)
        kst = A.f32([128, 16, 128], "kst", ww=True)
        vst = A.f32([128, 16, 128], "vst", ww=True)
        ckT = A.bf16([128, 256], "ckT")
        cvt = A.bf16([128, 2, 128], "cvt")
        ckst = A.f32([128, 2, 128], "ckst")
        cosb = [A.f32([128, 512], f"cos{i}") for i in range(2)]
        sinb = [A.f32([128, 512], f"sin{i}") for i in range(2)]
        qb = [A.bf16([128, 512], f"qb{i}") for i in range(2)]
        t1 = [A.f32([128, 512], f"t1{i}") for i in range(2)]
        t2 = [A.f32([128, 512], f"t2{i}") for i in range(2)]
        Eb = [A.bf16([128, 512], f"E{i}") for i in range(3)]
        den = A.f32([128, 512], "den")
        oatt = A.bf16([128, 4, T], "oatt", ww=True)
        extra = []
        if l == 0 and OVERLAP_MODS:
            wada1 = [A.bf16([128, 3 * D], f"wadb{i}") for i in range(2)]
            extra = mods_steps(1, wada1, PS[7])
        qrv = qr.ap.rearrange("p (b h q) -> p b h q", b=16, h=4)
        cnt = [0]
        for kh in range(2):
            for hi in range(5):
                c0 = (Q0 + (4 * kh + hi) * 128) if hi < 4 else (K0 + kh * 128)
                wt = next_w(c0)
                for tt in range(4):
                    n = cnt[0]
                    cnt[0] += 1
                    pq = PS[n % 2]
                    pr = PS[2 + n % 2]
                    cb, sb_, qbb, t1b, t2b = cosb[n % 2], sinb[n % 2], qb[n % 2], t1[n % 2], t2[n % 2]
                    S.dma("sp", cb[:, :], cos_d[:, tt * 512:(tt + 1) * 512], w=[cb])
                    S.dma("sp", sb_[:, :], sin_d[:, tt * 512:(tt + 1) * 512], w=[sb_])
                    proj_fm(wt, 0, tt, pq)
                    A_copy(qbb[:, :], pq[:, :], [pq], [qbb])
                    pe(lambda E, pr=pr, qbb=qbb: E.matmul(pr[:, :], ropeP_b[:, :], qbb[:, :], start=True, stop=True), [qbb, ropeP_b], [pr])
                    dve(lambda E, t1b=t1b, qbb=qbb, cb=cb: E.tensor_tensor(out=t1b[:, :], in0=qbb[:, :], in1=cb[:, :], op=ALU.mult), [qbb, cb], [t1b])
                    dve(lambda E, t2b=t2b, pr=pr, sb_=sb_: E.tensor_tensor(out=t2b[:, :], in0=pr[:, :], in1=sb_[:, :], op=ALU.mult), [pr, sb_], [t2b])
                    if hi < 4:
                        dve(lambda E, t1b=t1b, t2b=t2b, tt=tt, hi=hi: E.tensor_tensor(
                            out=qrv[:, tt * 4:(tt + 1) * 4, hi, :], in0=t1b[:, :].rearrange("p (a b) -> p a b", a=4),
                            in1=t2b[:, :].rearrange("p (a b) -> p a b", a=4), op=ALU.add), [t1b, t2b], [qr])
                    else:
                        dve(lambda E, t1b=t1b, t2b=t2b, tt=tt: E.tensor_tensor(
                            out=kr[:, tt * 512:(tt + 1) * 512], in0=t1b[:, :], in1=t2b[:, :], op=ALU.add), [t1b, t2b], [kr])
            for which in range(2):
                c0 = (K0 if which == 0 else V0) + kh * 128
                wt = next_w(c0)
                st_ = kst if which == 0 else vst
                for b in range(16):
                    pb = PS[4 + b % 2]
                    mm_group(pb[:, 0:128], [(hT[:, kc, b * 128:(b + 1) * 128], wt[:, kc, :]) for kc in range(KC)], [wt, hT], [pb])
                    A_copy(st_[:, b, :], pb[:, 0:128], [pb], [st_])
                    if which == 1:
                        dve(lambda E, b=b: E.tensor_copy(out=vtok[:, b, :], in_=vst[:, b, :]), [vst], [vtok])
                od = kc_o if which == 0 else vc_o
                S.dma("sp", od[l, :, kh, :].rearrange("(b p) d -> p b d", p=128), st_[:, :, :], r=[st_])
            S.dma("sp", ckst[:, :, :], ck_d[l, :, kh, :].rearrange("(j p) d -> p j d", p=128), w=[ckst])
            for j in range(2):
                pe(lambda E, j=j: E.transpose(PS[6][:, j * 128:(j + 1) * 128], ckst[:, j, :], ident_f[:, :]), [ckst, ident_f], [PS[6]])
            A_copy(ckT[:, :], PS[6][:, 0:256], [PS[6]], [ckT])
            S.dma("pool", cvt[:, :, :], cv_d[l, :, kh, :].rearrange("(j p) d -> p j d", p=128), w=[cvt])
            scale = 128.0 ** -0.5
            ecnt = 0
            for i in range(16):
                chunks = []
                ev = i % 2
                if i > 0:
                    chunks.append((kr[:, (i - 1) * 128:i * 128], vtok[:, i - 1, :], abias[:, ev:ev + 1], 0, [kr], [vtok]))
                chunks.append((kr[:, i * 128:(i + 1) * 128], vtok[:, i, :], 0.0, None, [kr], [vtok]))
                if i < 15:
                    chunks.append((kr[:, (i + 1) * 128:(i + 2) * 128], vtok[:, i + 1, :], abias[:, 2 + ev:3 + ev], 1, [kr], [vtok]))
                for j in range(2):
                    chunks.append((ckT[:, j * 128:(j + 1) * 128], cvt[:, j, :], abias[:, 4:5], None, [ckT], [cvt]))
                po = PS[4 + i % 2]
                pl = PS[6]
                qi = qrv[:, i, :, :].rearrange("p h q -> p (h q)")
                nchk = len(chunks)
                pend = None
                for ci, (kT, vv, bias, mk, kdep, vdep) in enumerate(chunks):
                    psb = PS[ecnt % 2]
                    eb = Eb[ecnt % 3]
                    ecnt += 1
                    pe(lambda E, psb=psb, kT=kT, qi=qi: E.matmul(psb[:, :], kT, qi, start=True, stop=True), kdep + [qr], [psb])
                    act(lambda E, eb=eb, psb=psb, bias=bias: E.activation(out=eb[:, :], in_=psb[:, :], func=AF.Exp, scale=scale, bias=bias),
                        [psb, abias], [eb])
                    if mk is not None:
                        dve(lambda E, eb=eb, mk=mk: E.tensor_tensor(
                            out=eb[:, :].rearrange("p (h q) -> p h q", h=4), in0=eb[:, :].rearrange("p (h q) -> p h q", h=4),
                            in1=masks_b[:, mk, :].unsqueeze(1).to_broadcast([128, 4, 128]), op=ALU.mult), [eb, masks_b], [eb])
                    if pend is not None:
                        pci, peb, pvv, pvdep = pend
                        pe(lambda E, po=po, pvv=pvv, peb=peb, pci=pci: E.matmul(po[:, :], pvv, peb[:, :], start=(pci == 0), stop=False), pvdep + [peb], [po])
                        pe(lambda E, pl=pl, peb=peb, pci=pci: E.matmul(pl[:, :], ones_b[:, :], peb[:, :], start=(pci == 0), stop=False), [peb, ones_b], [pl])
                    pend = (ci, eb, vv, vdep)
                pci, peb, pvv, pvdep = pend
                pe(lambda E, po=po, pvv=pvv, peb=peb, pci=pci: E.matmul(po[:, :], pvv, peb[:, :], start=(pci == 0), stop=True), pvdep + [peb], [po])
                pe(lambda E, pl=pl, peb=peb, pci=pci: E.matmul(pl[:, :], ones_b[:, :], peb[:, :], start=(pci == 0), stop=True), [peb, ones_b], [pl])
                dve(lambda E, pl=pl, kh=kh: E.tensor_tensor(
                    out=den[:, :].rearrange("p (h q) -> p h q", h=4), in0=pl[:, :].rearrange("p (h q) -> p h q", h=4),
                    in1=esink[l][:, 4 * kh:4 * kh + 4].unsqueeze(2).to_broadcast([128, 4, 128]), op=ALU.add), [pl, esink[l]], [den])
                dve(lambda E: E.reciprocal(out=den[:, :], in_=den[:, :]), [den], [den])
                dve(lambda E, po=po, i=i: E.tensor_tensor(
                    out=oatt[:, :, i * 128:(i + 1) * 128], in0=po[:, :].rearrange("p (h q) -> p h q", h=4),
                    in1=den[:, :].rearrange("p (h q) -> p h q", h=4), op=ALU.mult), [po, den], [oatt])
                if extra:
                    extra.pop(0)()
            for g in range(4):
                S.dma("sp", catT_d.ap[4 * kh + g, :, :], oatt[:, g, :], r=[oatt], w=[catT_d])
        while extra:
            extra.pop(0)()
        S.barrier()
        A.top = p1
        ckpt(f"l{l}a", [("catTa", catT_d.ap[0:8], [8, 128, T], BF16, [catT_d])])

        S.label = f"l{l}_gla"
        wbuf = [A.bf16([128, KC, 128], f"gwbuf{i}") for i in range(3)]
        wi[0] = 0
        lrT = A.bf16([32, T], "lrT", ww=True)
        gq = A.bf16([128, T], "gq", ww=True)
        gk = A.bf16([128, T], "gk", ww=True)
        gvt = A.bf16([128, 16, 256], "gvt", ww=True)
        o_sb = A.f32([128, 2, T], "o_sb")
        gt = A.f32([128, T], "gt")
        cum = A.f32([128, T], "cum")
        qe = A.bf16([128, T], "qe")
        ke = A.bf16([128, T], "ke")
        kd = A.bf16([128, T], "kd")
        kdt = A.bf16([128, 16, 128], "kdt", ww=True)
        nle = A.f32([128, 16], "nle")
        dec = A.f32([128, 16], "dec")
        NCH = 6
        chain = [A.f32([128, 256], f"chain{i}") for i in range(NCH)]
        Sb_all = A.bf16([128, 16, 256], "Sb_all", ww=True)
        At4 = [A.bf16([128, 512], f"At4{i}") for i in range(2)]
        PSU = [PS[2], PS[3]]
        sq = [A.bf16([128, 512], f"gsq{i}") for i in range(2)]
        rs = A.f32([128, 512], "grs")
        sg = [A.f32([128, 512], f"gsg{i}") for i in range(2)]
        tq = A.f32([128, 512], "gtq")
        catg = [A.bf16([128, T], f"catg{i}", ww=True) for i in range(2)]
        wlr = A.bf16([128, KC, 32], "wlr")
        load_w(wlr, win[:, LR0:LR0 + 32])
        for tt in range(4):
            pb = PS[tt % 2]
            mm_group(pb[0:32, :], [(wlr[:, kc, :], hT[:, kc, tt * 512:(tt + 1) * 512]) for kc in range(KC)], [wlr, hT], [pb])
            A_copy(lrT[:, tt * 512:(tt + 1) * 512], pb[0:32, :], [pb], [lrT])
        for hd in range(4):
            for which, dst in ((0, gq), (1, gk)):
                wt = next_w((GQ0 if which == 0 else GK0) + hd * 128)
                for tt in range(4):
                    pb = PS[tt % 2]
                    proj_fm(wt, 0, tt, pb)
                    A_copy(dst[:, tt * 512:(tt + 1) * 512], pb[:, :], [pb], [dst])
            wv = [next_w(GV0 + hd * 256), next_w(GV0 + hd * 256 + 128)]
            for b in range(16):
                pb = PS[2 + b % 2]
                for vc in range(2):
                    mm_group(pb[:, vc * 128:(vc + 1) * 128], [(hT[:, kc, b * 128:(b + 1) * 128], wv[vc][:, kc, :]) for kc in range(KC)],
                             [wv[vc], hT], [pb])
                A_copy(gvt[:, b, :], pb[:, 0:256], [pb], [gvt])
            for dr in range(2):
                for tt in range(4):
                    pb = PS[tt % 2]
                    col = dr * 512 + hd * 128
                    pe(lambda E, pb=pb, col=col, tt=tt: E.matmul(pb[:, :], wgbd[l][:, col:col + 128], lrT[:, tt * 512:(tt + 1) * 512], start=True, stop=True),
                       [wgbd[l], lrT], [pb])
                    act(lambda E, pb=pb, tt=tt, dr=dr, hd=hd: E.activation(out=gt[:, tt * 512:(tt + 1) * 512], in_=pb[:, :], func=AF.Exp, scale=-1.0,
                                                                          bias=nbg[l][:, dr * 4 + hd:dr * 4 + hd + 1]), [pb, nbg[l]], [gt])
                act(lambda E: E.activation(out=gt[:, :], in_=gt[:, :], func=AF.Ln, bias=1.0, scale=1.0), [gt], [gt])
                for c in range(16):
                    dve(lambda E, c=c: E.tensor_tensor_scan(out=cum[:, c * 128:(c + 1) * 128], data0=ones_b[:, :], data1=gt[:, c * 128:(c + 1) * 128],
                                                            initial=0.0, op0=ALU.mult, op1=ALU.add), [gt, ones_b], [cum])
                cum3 = cum.ap.rearrange("p (c t) -> p c t", c=16)
                gt3 = gt.ap.rearrange("p (c t) -> p c t", c=16)
                if dr == 1:
                    dve(lambda E: E.tensor_tensor(out=gt[:, :], in0=gt[:, :], in1=cum[:, :], op=ALU.subtract), [gt, cum], [gt])
                    dve(lambda E: E.tensor_tensor(out=gt3, in0=gt3, in1=cum3[:, :, 127:128].to_broadcast([128, 16, 128]), op=ALU.add), [gt, cum], [gt])
                    Lc, Lc3, tmpb, endcol = gt, gt3, cum, 0
                else:
                    Lc, Lc3, tmpb, endcol = cum, cum3, gt, 127
                dve(lambda E, Lc3=Lc3, endcol=endcol: E.tensor_scalar(out=nle[:, :].unsqueeze(2), in0=Lc3[:, :, endcol:endcol + 1], scalar1=-1.0 / 16, scalar2=None, op0=ALU.mult),
                    [Lc], [nle])
                act(lambda E: E.activation(out=dec[:, :], in_=nle[:, :], func=AF.Exp), [nle], [dec])
                for c in range(16):
                    act(lambda E, c=c, Lc=Lc, tmpb=tmpb: E.activation(out=tmpb[:, c * 128:(c + 1) * 128], in_=Lc[:, c * 128:(c + 1) * 128], func=AF.Exp,
                                                                      scale=1.0 / 16, bias=nle[:, c:c + 1]), [Lc, nle], [tmpb])
                dve(lambda E, tmpb=tmpb: E.tensor_tensor(out=kd[:, :], in0=gk[:, :], in1=tmpb[:, :], op=ALU.mult), [gk, tmpb], [kd])
                act(lambda E, Lc=Lc, tmpb=tmpb: E.activation(out=tmpb[:, :], in_=Lc[:, :], func=AF.Exp, scale=1.0 / 16), [Lc], [tmpb])
                dve(lambda E, tmpb=tmpb: E.tensor_tensor(out=ke[:, :], in0=gk[:, :], in1=tmpb[:, :], op=ALU.mult), [gk, tmpb], [ke])
                act(lambda E, Lc=Lc, tmpb=tmpb: E.activation(out=tmpb[:, :], in_=Lc[:, :], func=AF.Exp, scale=-1.0 / 16), [Lc], [tmpb])
                dve(lambda E, tmpb=tmpb: E.scalar_tensor_tensor(out=qe[:, :], in0=gq[:, :], scalar=128.0 ** -0.5, in1=tmpb[:, :], op0=ALU.mult, op1=ALU.mult),
                    [gq, tmpb], [qe])
                for c4 in range(4):
                    pb = PS[2 + c4 % 2]
                    pbb = pb[:, 0:256].bitcast(BF16)

                    def fn(E, pbb=pbb, c4=c4):
                        ins = None
                        for j in range(4):
                            c = c4 * 4 + j
                            ins = E.transpose(pbb[:, j * 128:(j + 1) * 128], kd[:, c * 128:(c + 1) * 128], ident_b[:, :])
                        return ins
                    pe(fn, [kd, ident_b], [pb])
                    A_copy(kdt[:, c4 * 4:(c4 + 1) * 4, :], pbb.rearrange("p (a b) -> p a b", a=4), [pb], [kdt])
                s0 = (s0f_d if dr == 0 else s0b_d)
                so = (sf_o if dr == 0 else sb_o)
                order = list(range(16)) if dr == 0 else list(range(15, -1, -1))
                ci = 0
                cur = chain[0]
                S.dma("sp", cur[:, :], s0[l, hd], w=[cur])
                for n, c in enumerate(order):
                    boundary = (n > 0) and (n % 2 == 0)
                    if boundary:
                        nxt = chain[(ci + 1) % NCH]
                        ci += 1
                        dve(lambda E, nxt=nxt, cur=cur: E.tensor_scalar(out=nxt[:, :], in0=cur[:, :], scalar1=cont, scalar2=None, op0=ALU.mult), [cur, flags], [nxt])
                        cur = nxt
                    A_copy(Sb_all[:, c, :], cur[:, :], [cur], [Sb_all])
                    pu = PSU[n % 2]
                    pe(lambda E, pu=pu, c=c: E.matmul(pu[:, 0:256], kdt[:, c, :], gvt[:, c, :], start=True, stop=True), [kdt, gvt], [pu])
                    nxt = chain[(ci + 1) % NCH]
                    ci += 1
                    dve(lambda E, nxt=nxt, cur=cur, pu=pu, c=c: E.scalar_tensor_tensor(out=nxt[:, :], in0=cur[:, :], scalar=dec[:, c:c + 1], in1=pu[:, 0:256],
                                                                                       op0=ALU.mult, op1=ALU.add), [cur, dec, pu], [nxt])
                    cur = nxt
                    if n % 2 == 1:
                        S.dma("sp", so[l, c // 2, hd], cur[:, :], r=[cur])
                for g in range(4):
                    cs = order[4 * g:4 * g + 4]
                    g4 = min(cs) // 4
                    pa = PS[0] if g % 2 == 0 else PS[6]
                    att = At4[g % 2]

                    def fnA(E, pa=pa, cs=cs):
                        ins = None
                        for c in cs:
                            k = c % 4
                            ins = E.matmul(pa[:, k * 128:(k + 1) * 128], ke[:, c * 128:(c + 1) * 128], qe[:, c * 128:(c + 1) * 128], start=True, stop=True)
                        return ins
                    pe(fnA, [ke, qe], [pa])
                    dve(lambda E, att=att, pa=pa, dr=dr: E.tensor_tensor(
                        out=att[:, :].rearrange("p (k t) -> p k t", k=4), in0=pa[:, :].rearrange("p (k t) -> p k t", k=4),
                        in1=masks_b[:, 2 + dr, :].unsqueeze(1).to_broadcast([128, 4, 128]), op=ALU.mult), [pa, masks_b], [att])
                    pov = (PS[1], PS[4]) if g % 2 == 0 else (PS[5], PS[7])
                    for vc in range(2):
                        po = pov[vc]

                        def fnO(E, po=po, cs=cs, vc=vc, att=att):
                            ins = None
                            for c in cs:
                                k = c % 4
                                E.matmul(po[:, k * 128:(k + 1) * 128], gvt[:, c, vc * 128:(vc + 1) * 128], att[:, k * 128:(k + 1) * 128], start=True, stop=False)
                                ins = E.matmul(po[:, k * 128:(k + 1) * 128], Sb_all[:, c, vc * 128:(vc + 1) * 128], qe[:, c * 128:(c + 1) * 128], start=False, stop=True)
                            return ins
                        pe(fnO, [gvt, att, Sb_all, qe], [po])
                        if dr == 0:
                            A_copy(o_sb[:, vc, g4 * 512:(g4 + 1) * 512], po[:, :], [po], [o_sb])
                        else:
                            dve(lambda E, po=po, vc=vc, g4=g4: E.tensor_tensor(out=o_sb[:, vc, g4 * 512:(g4 + 1) * 512], in0=o_sb[:, vc, g4 * 512:(g4 + 1) * 512],
                                                                              in1=po[:, :], op=ALU.add), [po, o_sb], [o_sb])
            wg = [next_w(GO0 + hd * 256), next_w(GO0 + hd * 256 + 128)]
            for tt in range(4):
                pm = PS[tt % 2]
                for vc in range(2):
                    sqv = sq[vc]
                    act(lambda E, sqv=sqv, vc=vc, tt=tt: E.activation(out=sqv[:, :], in_=o_sb[:, vc, tt * 512:(tt + 1) * 512], func=AF.Square), [o_sb], [sqv])
                mm_group(pm[:, :], [(ones_b[:, :], sq[0][:, :]), (ones_b[:, :], sq[1][:, :])], [ones_b, sq[0], sq[1]], [pm])
                act(lambda E, pm=pm: E.activation(out=rs[:, :], in_=pm[:, :], func=AF.Sqrt, scale=1.0 / 256, bias=1e-5), [pm], [rs])
                dve(lambda E: E.reciprocal(out=rs[:, :], in_=rs[:, :]), [rs], [rs])
                for vc in range(2):
                    pg = PS[2 + vc]
                    proj_fm(wg[vc], 0, tt, pg)
                    sgv = sg[vc]
                    act(lambda E, sgv=sgv, pg=pg: E.activation(out=sgv[:, :], in_=pg[:, :], func=AF.Silu), [pg], [sgv])
                    dve(lambda E, vc=vc, tt=tt: E.tensor_tensor(out=tq[:, :], in0=o_sb[:, vc, tt * 512:(tt + 1) * 512], in1=rs[:, :], op=ALU.mult), [o_sb, rs], [tq])
                    dve(lambda E, vc=vc, tt=tt, sgv=sgv: E.scalar_tensor_tensor(out=catg[vc][:, tt * 512:(tt + 1) * 512], in0=tq[:, :], scalar=gnw[l][:, vc:vc + 1],
                                                                               in1=sgv[:, :], op0=ALU.mult, op1=ALU.mult), [tq, gnw[l], sgv], [catg[vc]])
            for vc in range(2):
                S.dma("sp", catT_d.ap[8 + 2 * hd + vc, :, :], catg[vc][:, :], r=[catg[vc]], w=[catT_d])
        S.barrier()
        A.top = hT_base
        ckpt(f"l{l}b", [("catT", catT_d.ap, [KC, 128, T], BF16, [catT_d])])

        S.label = f"l{l}_p3"
        wob = [A.bf16([128, KC, 256], f"wob{i}") for i in range(3)]
        wocnt = [0]
        catt = [A.bf16([128, KC, 512], f"catt{i}") for i in range(2)]
        zbuf = [A.f32([128, KC, 512], f"z{i}") for i in range(2)]
        zch = [mk_chunks(zb_, KC) for zb_ in zbuf]
        xc = [A.f32([128, 512], f"xc{i}") for i in range(3)]
        tb3 = ([A.bf16([128, 512], f"lsq{i}") for i in range(2)], [A.bf16([128, 512], f"lzb{i}") for i in range(2)],
               A.f32([128, 512], "lmean"), A.f32([128, 512], "lrstd"), PS[4], PS[5])
        xo = [A.f32([128, 512], f"lxo{i}") for i in range(3)]
        ho = [A.bf16([128, 512], f"ho{i}") for i in range(3)]
        for tt in range(4):
            ct = catt[tt % 2]
            zc = zch[tt % 2]
            S.dma("sp", ct[:, :, :], catT_d.ap[:, :, tt * 512:(tt + 1) * 512].rearrange("c p t -> p c t"), r=[catT_d], w=[ct])
            for oc in range(KC):
                pb = PS[oc % 4]
                xct = xc[oc % 3]
                S.dma("sp", xct[:, :], xin.ap[oc, :, tt * 512:(tt + 1) * 512], r=[xin], w=[xct])
                if oc % 2 == 0:
                    wo_t = wob[wocnt[0] % 3]
                    wocnt[0] += 1
                    load_w(wo_t, w_out_d[l][:, oc * 128:oc * 128 + 256])
                o2 = oc % 2
                mm_group(pb[:, :], [(wo_t[:, kc, o2 * 128:(o2 + 1) * 128], ct[:, kc, :]) for kc in range(KC)], [wo_t, ct], [pb])
                dve(lambda E, pb=pb, oc=oc, xct=xct, zt=zc[oc]: E.scalar_tensor_tensor(out=zt[:, :], in0=pb[:, :], scalar=g1a[l][:, oc:oc + 1], in1=xct[:, :],
                                                                                     op0=ALU.mult, op1=ALU.add), [pb, g1a[l], xct], [zc[oc]])

            def out_x(c, zt, tt=tt):
                xoc = xo[c % 3]
                A_ident(xoc[:, :], zt[:, :], ln1w[l][:, c:c + 1], ln1b[l][:, c:c + 1], [zt, ln1w[l], ln1b[l]], [xoc])
                S.dma("sp", xmid.ap[c, :, tt * 512:(tt + 1) * 512], xoc[:, :], r=[xoc], w=[xmid])

            def out_h(c, zt, tt=tt):
                hoc = ho[c % 3]
                A_ident(hoc[:, :], zt[:, :], A2[l][:, c:c + 1], B2[l][:, c:c + 1], [zt, A2[l], B2[l]], [hoc])
                S.dma("sp", h2T_d.ap[c, :, tt * 512:(tt + 1) * 512], hoc[:, :], r=[hoc], w=[h2T_d])
            ln_tile(zc, tb3, ln1w[l], ln1b[l], out_x, out_h)
        S.barrier()
        A.top = hT_base
        ckpt(f"l{l}p3", [("x1T", xmid.ap, [KC, 128, T], F32, [xmid]), ("h2T", h2T_d.ap, [KC, 128, T], BF16, [h2T_d])])

        actT = A.bf16([128, NH, 1024], "actT", ww=True)
        p4 = A.top
        last = (l == L - 1)
        for hf in range(2):
            A.top = p4
            S.label = f"l{l}_up{hf}"
            t0 = hf * 1024
            h2h = A.bf16([128, KC, 1024], "h2h")
            halo = A.bf16([128, KC, 2], "halo")
            up_top = A.top
            S.dma("sp", h2h[:, :, :], h2T_d.ap[:, :, t0:t0 + 1024].rearrange("c p t -> p c t"), r=[h2T_d], w=[h2h])
            ht = 1024 if hf == 0 else 1023
            S.dma("sp", halo[:, :, 0:1], h2T_d.ap[:, :, ht:ht + 1].rearrange("c p t -> p c t"), r=[h2T_d], w=[halo], slow=True)
            wup = [[A.bf16([128, KC, 256], f"wupa{i}") for i in range(2)], [A.bf16([128, KC, 256], f"wupg{i}") for i in range(2)]]
            U = [[A.f32([128, 4, 258], f"Ua{i}") for i in range(2)], [A.f32([128, 4, 258], f"Ug{i}") for i in range(2)]]
            acc = [[A.f32([128, 1024], f"acca{i}") for i in range(2)], [A.f32([128, 1024], f"accg{i}") for i in range(2)]]
            for ag in range(2):
                for i in range(2):
                    dve(lambda E, ag=ag, i=i: E.memset(U[ag][i][:, :, :], 0.0), [], [U[ag][i]])
            for j in range(NH):
                if j % 2 == 0:
                    for ag in range(2):
                        wt = wup[ag][(j // 2) % 2]
                        load_w(wt, w_up_d[l][:, ag * DFF + j * 128: ag * DFF + j * 128 + 256])
                for ag in range(2):
                    wt = wup[ag][(j // 2) % 2]
                    wcol = (j % 2)
                    Ut = U[ag][j % 2]
                    at = acc[ag][j % 2]
                    idx = ag * NH + j
                    for t2_ in range(2):
                        pb = PS[ag * 2 + t2_]
                        mm_group(pb[:, :], [(wt[:, kc, wcol * 128:(wcol + 1) * 128], h2h[:, kc, t2_ * 512:(t2_ + 1) * 512]) for kc in range(KC)], [wt, h2h], [pb])
                        A_copy(Ut[:, 2 * t2_:2 * t2_ + 2, 1:257], pb[:, :].rearrange("p (s t) -> p s t", s=2), [pb], [Ut])
                    pmn = PS[4 + j % 2]
                    mcol = ag * 4
                    mm_group(pmn[:, mcol:mcol + 1], [(wt[:, kc, wcol * 128:(wcol + 1) * 128], halo[:, kc, 0:1]) for kc in range(KC)], [wt, halo], [pmn])
                    dve(lambda E, Ut=Ut: E.tensor_scalar(out=Ut[:, 1:4, 0:1], in0=Ut[:, 0:3, 256:257], scalar1=cont, scalar2=None, op0=ALU.mult), [Ut, flags], [Ut])
                    dve(lambda E, Ut=Ut: E.tensor_scalar(out=Ut[:, 0:3, 257:258], in0=Ut[:, 1:4, 1:2], scalar1=cont, scalar2=None, op0=ALU.mult), [Ut, flags], [Ut])
                    if hf == 0:
                        dve(lambda E, Ut=Ut, pmn=pmn, mcol=mcol: E.tensor_scalar(out=Ut[:, 3, 257:258], in0=pmn[:, mcol:mcol + 1], scalar1=cont, scalar2=None, op0=ALU.mult),
                            [pmn, flags], [Ut])
                    else:
                        dve(lambda E, Ut=Ut, pmn=pmn, mcol=mcol: E.tensor_scalar(out=Ut[:, 0, 0:1], in0=pmn[:, mcol:mcol + 1], scalar1=cont, scalar2=None, op0=ALU.mult),
                            [pmn, flags], [Ut])
                    a3 = at[:, :].rearrange("p (s t) -> p s t", s=4)
                    A_ident(a3, Ut[:, :, 1:257], convw[l][:, 88 + idx:88 + idx + 1], convb[l][:, idx:idx + 1], [Ut, convw[l], convb[l]], [at])
                    dve(lambda E, a3=a3, Ut=Ut, idx=idx: E.scalar_tensor_tensor(out=a3, in0=Ut[:, :, 0:256], scalar=convw[l][:, idx:idx + 1], in1=a3, op0=ALU.mult, op1=ALU.add),
                        [Ut, convw[l], at], [at])
                    dve(lambda E, a3=a3, Ut=Ut, idx=idx: E.scalar_tensor_tensor(out=a3, in0=Ut[:, :, 2:258], scalar=convw[l][:, 176 + idx:176 + idx + 1], in1=a3, op0=ALU.mult, op1=ALU.add),
                        [Ut, convw[l], at], [at])
                aa, gg = acc[0][j % 2], acc[1][j % 2]
                act(lambda E, gg=gg: E.activation(out=gg[:, :], in_=gg[:, :], func=AF.Silu), [gg], [gg])
                dve(lambda E, aa=aa, gg=gg, j=j: E.tensor_tensor(out=actT[:, j, :], in0=aa[:, :], in1=gg[:, :], op=ALU.mult), [aa, gg], [actT])
            S.barrier()
            S.label = f"l{l}_dn{hf}"
            A.top = p4
            wdn = [A.bf16([128, NH, 256], f"wdn{i}") for i in range(2)]
            z2 = A.f32([128, KC, 512], "z2")
            z2c = mk_chunks(z2, KC)
            xc2 = [A.f32([128, 512], f"xc2{i}") for i in range(2)]
            tb4 = ([A.bf16([128, 512], f"l2sq{i}") for i in range(2)], [A.bf16([128, 512], f"l2zb{i}") for i in range(2)],
                   A.f32([128, 512], "l2mean"), A.f32([128, 512], "l2rstd"), PS[4], PS[5])
            xo2 = [A.f32([128, 512], f"l2xo{i}") for i in range(2)]
            xTo = [A.f32([128, 512], f"xTo{i}") for i in range(2)]
            for t2_ in range(2):
                tt = hf * 2 + t2_
                for op_ in range(8):
                    wt = wdn[op_ % 2]
                    load_w(wt, w_down_d[l][:, op_ * 256:(op_ + 1) * 256])
                    for o2 in range(2):
                        oc = op_ * 2 + o2
                        pb = PS[oc % 4]
                        xct = xc2[oc % 2]
                        S.dma("sp", xct[:, :], xmid.ap[oc, :, tt * 512:(tt + 1) * 512], r=[xmid], w=[xct])
                        mm_group(pb[:, :], [(wt[:, j, o2 * 128:(o2 + 1) * 128], actT[:, j, t2_ * 512:(t2_ + 1) * 512]) for j in range(NH)], [wt, actT], [pb])
                        dve(lambda E, pb=pb, oc=oc, xct=xct, zt=z2c[oc]: E.scalar_tensor_tensor(out=zt[:, :], in0=pb[:, :], scalar=g2a[l][:, oc:oc + 1], in1=xct[:, :],
                                                                                              op0=ALU.mult, op1=ALU.add), [pb, g2a[l], xct], [z2c[oc]])

                def out_x2(c, zt, tt=tt):
                    xoc = xo2[c % 2]
                    A_ident(xoc[:, :], zt[:, :], ln2w[l][:, c:c + 1], ln2b[l][:, c:c + 1], [zt, ln2w[l], ln2b[l]], [xoc])
                    if not last:
                        S.dma("sp", xin.ap[c, :, tt * 512:(tt + 1) * 512], xoc[:, :], r=[xoc], w=[xin])
                    else:
                        pt = PS[6 + c % 2]

                        def fn(E, pt=pt, xoc=xoc):
                            ins = None
                            for j in range(4):
                                ins = E.transpose(pt[:, j * 128:(j + 1) * 128], xoc[:, j * 128:(j + 1) * 128], ident_f[:, :])
                            return ins
                        pe(fn, [xoc, ident_f], [pt])
                        yo = xTo[c % 2]
                        A_copy(yo[:, :], pt[:, :], [pt], [yo])
                        S.dma("sp", y_o[tt * 512:(tt + 1) * 512, c * 128:(c + 1) * 128].rearrange("(j p) f -> p j f", p=128),
                              yo[:, :].rearrange("p (j f) -> p j f", j=4), r=[yo])

                def out_h2(c, zt, tt=tt):
                    hoc = xTo[c % 2]
                    hob = hoc[:, 0:256].bitcast(BF16)
                    A_ident(hob, zt[:, :], A1n[:, c:c + 1], B1n[:, c:c + 1], [zt, A1n, B1n], [hoc])
                    S.dma("sp", catT_d.ap[c, :, tt * 512:(tt + 1) * 512], hob, r=[hoc], w=[catT_d])
                ln_tile(z2c, tb4, ln2w[l], ln2b[l], out_x2, None if last else out_h2)
            S.barrier()
        A.top = work0
        ckpt(f"l{l}p4", [("x2T", xin.ap, [KC, 128, T], F32, [xin])])
        if not last:
            for c in range(KC):
                S.dma("sp", hT[:, c, :], catT_d.ap[c, :, :], r=[catT_d], w=[hT])
            S.barrier(new_epoch=True)

    for l_ in range(L):
        layer(l_)
    S.barrier()
    S.emit()


def make_ln_finish(S, act, dve, A_ident, z, mean, rstd, s1p, s2p):
    def fin():
        act(lambda E: E.activation(out=mean[:, :], in_=s1p[:, :], func=AF.Identity, scale=1.0 / D), [s1p], [mean])
        dve(lambda E: E.tensor_tensor(out=rstd[:, :], in0=mean[:, :], in1=mean[:, :], op=ALU.mult), [mean], [rstd])
        dve(lambda E: E.scalar_tensor_tensor(out=rstd[:, :], in0=s2p[:, :], scalar=1.0 / D, in1=rstd[:, :], op0=ALU.mult, op1=ALU.subtract), [s2p, rstd], [rstd])
        dve(lambda E: E.tensor_scalar(out=rstd[:, :], in0=rstd[:, :], scalar1=EPS1, scalar2=None, op0=ALU.add), [rstd], [rstd])
        act(lambda E: E.activation(out=rstd[:, :], in_=rstd[:, :], func=AF.Sqrt), [rstd], [rstd])
        dve(lambda E: E.reciprocal(out=rstd[:, :], in_=rstd[:, :]), [rstd], [rstd])
    return fin


def _consts(is_sample):
    ident = np.eye(128, dtype=np.float32)
    P = np.zeros((128, 128), np.float32)
    for d in range(128):
        if (d % 64) < 32:
            P[d + 32, d] = -1.0
        else:
            P[d - 32, d] = 1.0
    cosT = np.ones((128, T), np.float32)
    sinT = np.zeros((128, T), np.float32)
    if is_sample:
        t = np.arange(T)
        row = (t // 64).astype(np.float32)
        col = (t % 64).astype(np.float32)
        freqs = (np.float32(10000.0) ** (-np.arange(32, dtype=np.float32) / np.float32(32))).astype(np.float32)
        for d in range(128):
            pos = row if d < 64 else col
            ang = (pos * freqs[d % 32]).astype(np.float32)
            cosT[d] = np.cos(ang)
            sinT[d] = np.sin(ang)
    kj = np.arange(128)[:, None]
    qi = np.arange(128)[None, :]
    masks = np.ones((128, 4, 128), np.float32)
    if is_sample:
        masks[:, 0, :] = (kj >= qi)
        masks[:, 1, :] = (kj <= qi)
    masks[:, 2, :] = (kj <= qi)
    masks[:, 3, :] = (kj >= qi)
    abias = np.zeros((128, 8), np.float32)
    if not is_sample:
        abias[:, 0] = -30000.0
        abias[:, 3] = -30000.0
        abias[:, 4] = -30000.0
    flags = np.zeros((128, 2), np.float32)
    flags[:, 0] = 1.0 if is_sample else 0.0
    return dict(ident=ident, ropeP=P, cosT=cosT, sinT=sinT, masks=masks, abias=abias, flags=flags)


_NC_CACHE = {}


def make_in_maps(x_prompt, x_sample, cache_k, cache_v, state_gla_fwd, state_gla_bwd, c, c_ctx,
                 w_ada, b_ada, w_in, attn_sink, w_gate_f, b_gate_f, w_gate_b, b_gate_b, gla_norm_w,
                 w_out, ln1_w, ln1_b, w_up, conv_w, conv_b, w_down, ln2_w, ln2_b):
    f = lambda a: np.ascontiguousarray(np.asarray(a, dtype=np.float32))
    shared = dict(
        w_ada=f(w_ada), b_ada=f(b_ada).reshape(L, 96, 128), w_in=f(w_in), attn_sink=f(attn_sink),
        w_gate_f=f(w_gate_f), b_gate_f=f(b_gate_f).reshape(L, 4, 128), w_gate_b=f(w_gate_b), b_gate_b=f(b_gate_b).reshape(L, 4, 128),
        gla_norm_w=f(gla_norm_w).reshape(L, 2, 128), w_out=f(w_out), ln1_w=f(ln1_w).reshape(L, KC, 128), ln1_b=f(ln1_b).reshape(L, KC, 128),
        w_up=f(w_up), conv_w=f(conv_w).reshape(L, 3 * 88, 128), conv_b=f(conv_b).reshape(L, 88, 128), w_down=f(w_down),
        ln2_w=f(ln2_w).reshape(L, KC, 128), ln2_b=f(ln2_b).reshape(L, KC, 128))
    x_prompt = f(x_prompt)
    x_sample = f(x_sample)
    cs = [_consts(False), _consts(True)]
    in_maps = []
    for core in range(8):
        m = dict(shared)
        if core < 4:
            m.update(cs[0])
            m["x"] = x_prompt[core * 8:(core + 1) * 8].reshape(T, D)
            m["cvec"] = f(c_ctx).reshape(KC, 128)
            m["ck"] = np.zeros((L, 256, 2, 128), np.float32)
            m["cv"] = np.zeros((L, 256, 2, 128), np.float32)
            m["s0f"] = np.zeros((L, 4, 128, 256), np.float32)
            m["s0b"] = np.zeros((L, 4, 128, 256), np.float32)
        else:
            b = core - 4
            m.update(cs[1])
            m["x"] = x_sample[b]
            m["cvec"] = f(c)[b].reshape(KC, 128)
            m["ck"] = f(cache_k)[b]
            m["cv"] = f(cache_v)[b]
            m["s0f"] = f(state_gla_fwd)[b]
            m["s0b"] = f(state_gla_bwd)[b]
        in_maps.append(m)
    return in_maps


def kernel(**inputs):
    in_maps = make_in_maps(**inputs)
    if "nc" not in _NC_CACHE:
        _NC_CACHE["nc"] = build_program()
    nc = _NC_CACHE["nc"]
    res = run_bass_kernel_spmd(nc, in_maps, core_ids=list(range(8)))
    R = res.results
    y_prompt = np.stack([R[cr]["y"] for cr in range(4)]).reshape(32, 256, D)
    y_sample = np.stack([R[cr]["y"] for cr in range(4, 8)]).reshape(4, T, D)
    nk = np.concatenate([R[cr]["kc_o"].reshape(L, 8, 256, 2, 128).transpose(1, 0, 2, 3, 4) for cr in range(4)], axis=0)
    nv = np.concatenate([R[cr]["vc_o"].reshape(L, 8, 256, 2, 128).transpose(1, 0, 2, 3, 4) for cr in range(4)], axis=0)
    sf = np.concatenate([R[cr]["sf_o"].transpose(1, 0, 2, 3, 4) for cr in range(4)], axis=0)
    sb = np.concatenate([R[cr]["sb_o"].transpose(1, 0, 2, 3, 4) for cr in range(4)], axis=0)
    return (np.ascontiguousarray(y_prompt), np.ascontiguousarray(y_sample), np.ascontiguousarray(nk), np.ascontiguousarray(nv),
            np.ascontiguousarray(sf), np.ascontiguousarray(sb))
```

```python
import numpy as np
import concourse.bass as bass
import concourse.mybir as mybir
from concourse.bass_utils import run_bass_kernel_spmd

F32 = mybir.dt.float32
BF16 = mybir.dt.bfloat16
AF = mybir.ActivationFunctionType
ALU = mybir.AluOpType

T = 2048
D = 2048
KC = 16
L = 2
IN_W = 4640
DFF = 5632
NH = 44
Q0, K0, V0, GQ0, GK0, GV0, GO0, LR0 = 0, 1024, 1280, 1536, 2048, 2560, 3584, 4608
ALPHA = 4.0 ** 0.25
EPS1 = 1e-5 / (ALPHA * ALPHA)
SELF_SYNC = True
RAW_ONLY_SELF = True
OVERLAP_MODS = True
SCRATCH_AS_OUTPUT = True
SCOPES = False
import os as _os
DBGF = set(_os.environ.get('KDBG', '').split(','))
RING = 16


class Ev:
    __slots__ = ("sem", "val", "key")

    def __init__(self, sem, val, key):
        self.sem, self.val, self.key = sem, val, key


class Tl:
    def __init__(self, ap, name="", ww=False):
        self.ap = ap
        self.w = {}
        self.r = {}
        self.name = name
        self.ww = ww

    def __getitem__(self, idx):
        return self.ap[idx]


class Sched:
    def __init__(self, nc):
        self.nc = nc
        self.q = {e: [] for e in ("pe", "act", "dve", "pool", "sp")}
        self.seen = {e: {} for e in self.q}
        self.esem = {}
        self.epoch = 0
        self.ring = {qn: [[nc.alloc_semaphore(f"r_{qn}_{i}"), 0] for i in range(RING)] for qn in ("sp", "pool")}
        self.rr = {"sp": 0, "pool": 0}
        self.label = "init"
        self.new_epoch()

    def new_epoch(self):
        self.epoch += 1
        for e in ("pe", "act", "dve"):
            self.esem[e] = [self.nc.alloc_semaphore(f"e_{e}_{self.epoch}"), 0]

    def _wait(self, eng, ev):
        if ev.key[0] == eng and (eng == "pe" or not SELF_SYNC):
            return
        if self.seen[eng].get(ev.key, 0) >= ev.val:
            return
        self.seen[eng][ev.key] = ev.val
        sem, val = ev.sem, ev.val
        self.q[eng].append((self.label, lambda E, sem=sem, val=val: E.wait_ge(sem, val)))

    def _deps(self, eng, r, w):
        for t in r:
            for ev in t.w.values():
                self._wait(eng, ev)
        for t in w:
            if not t.ww:
                for ev in t.w.values():
                    if ev.key[0] == eng and RAW_ONLY_SELF:
                        continue
                    self._wait(eng, ev)
            for ev in t.r.values():
                if ev.key[0] == eng and RAW_ONLY_SELF:
                    continue
                self._wait(eng, ev)

    def _record(self, ev, r, w):
        for t in r:
            t.r[ev.key] = ev
        for t in w:
            if t.ww:
                t.w[ev.key] = ev
            else:
                t.w = {ev.key: ev}
                t.r = {}

    def op(self, eng, fn, r=(), w=()):
        self._deps(eng, r, w)
        s = self.esem[eng]
        s[1] += 1
        sem = s[0]
        ev = Ev(sem, s[1], (eng, self.epoch))
        self.q[eng].append((self.label, lambda E, fn=fn, sem=sem: fn(E).then_inc(sem, 1)))
        self._record(ev, r, w)

    def dma(self, qn, out, in_, r=(), w=(), slow=False):
        self._deps(qn, r, w)
        i = self.rr[qn] % RING
        self.rr[qn] += 1
        slot = self.ring[qn][i]
        slot[1] += 16
        sem = slot[0]
        ev = Ev(sem, slot[1], ("dma", qn, i))
        if slow:
            self.q[qn].append((self.label, lambda E, out=out, in_=in_, sem=sem: E.dma_start(out=out, in_=in_, allow_slow_non_contiguous=True).then_inc(sem, 16)))
        else:
            self.q[qn].append((self.label, lambda E, out=out, in_=in_, sem=sem: E.dma_start(out=out, in_=in_).then_inc(sem, 16)))
        self._record(ev, r, w)

    def barrier(self, new_epoch=False):
        evs = []
        for e in ("pe", "act", "dve"):
            s = self.esem[e]
            if s[1] > 0:
                evs.append(Ev(s[0], s[1], (e, self.epoch)))
        for qn in ("sp", "pool"):
            for i, (sem, val) in enumerate(self.ring[qn]):
                if val > 0:
                    evs.append(Ev(sem, val, ("dma", qn, i)))
        for e in self.q:
            for ev in evs:
                if ev.key[0] == e and e == "pe":
                    continue
                if self.seen[e].get(ev.key, 0) >= ev.val:
                    continue
                self.seen[e][ev.key] = ev.val
                self.q[e].append((self.label, lambda E, sem=ev.sem, val=ev.val: E.wait_ge(sem, val)))
        if new_epoch:
            self.new_epoch()

    def emit(self):
        nc = self.nc
        q = self.q

        def run(E, items):
            if not SCOPES:
                for _, f in items:
                    f(E)
                return
            cur = None
            sid = None
            for lab, f in items:
                if lab != cur:
                    if cur is not None:
                        nc.leave_named_scope(cur, sid, False)
                    sid, _ = nc.enter_named_scope(lab, False)
                    cur = lab
                f(E)
            if cur is not None:
                nc.leave_named_scope(cur, sid, False)

        with nc.Block() as block:
            @block.tensor
            def _(E):
                run(E, q["pe"])

            @block.scalar
            def _(E):
                run(E, q["act"])

            @block.vector
            def _(E):
                run(E, q["dve"])

            @block.gpsimd
            def _(E):
                run(E, q["pool"])

            @block.sync
            def _(E):
                run(E, q["sp"])


class Arena:
    def __init__(self, nc, ncols):
        self.t = nc.alloc_sbuf_tensor("arena", [128, ncols], F32)
        self.n = ncols
        self.top = 0

    def _take(self, cols):
        cols = (cols + 7) // 8 * 8
        a = self.top
        assert a + cols <= self.n, f"arena overflow {a}+{cols}>{self.n}"
        self.top = a + cols
        return a, cols

    def f32(self, shape, name="", ww=False):
        n = int(np.prod(shape[1:]))
        a, c = self._take(n)
        ap = self.t[0:shape[0], a:a + n]
        if len(shape) == 3:
            ap = ap.rearrange("p (a b) -> p a b", a=shape[1])
        return Tl(ap, name, ww)

    def bf16(self, shape, name="", ww=False):
        n = int(np.prod(shape[1:]))
        a, c = self._take((n + 1) // 2)
        ap = self.t[0:shape[0], a:a + (n + 1) // 2].bitcast(BF16)[:, 0:n]
        if len(shape) == 3:
            ap = ap.rearrange("p (a b) -> p a b", a=shape[1])
        return Tl(ap, name, ww)


class _Stop(Exception):
    pass


def build_program(stop_after=None):
    nc = bass.Bass("TRN2", target_bir_lowering=False)
    try:
        _build_body(nc, stop_after)
    except _Stop:
        pass
    return nc


def _build_body(nc, stop_after):

    def din(name, shape):
        return nc.dram_tensor(name, list(shape), F32, kind="ExternalInput").ap()

    def dout(name, shape):
        return nc.dram_tensor(name, list(shape), F32, kind="ExternalOutput").ap()

    x_d = din("x", [T, D])
    cvec_d = din("cvec", [KC, 128])
    ck_d = din("ck", [L, 256, 2, 128])
    cv_d = din("cv", [L, 256, 2, 128])
    s0f_d = din("s0f", [L, 4, 128, 256])
    s0b_d = din("s0b", [L, 4, 128, 256])
    ident_d = din("ident", [128, 128])
    ropeP_d = din("ropeP", [128, 128])
    cos_d = din("cosT", [128, T])
    sin_d = din("sinT", [128, T])
    masks_d = din("masks", [128, 4, 128])
    abias_d = din("abias", [128, 8])
    flags_d = din("flags", [128, 2])
    w_ada_d = din("w_ada", [L, D, 6 * D])
    b_ada_d = din("b_ada", [L, 96, 128])
    w_in_d = din("w_in", [L, D, IN_W])
    sink_d = din("attn_sink", [L, 8])
    wgf_d = din("w_gate_f", [L, 16, 512])
    bgf_d = din("b_gate_f", [L, 4, 128])
    wgb_d = din("w_gate_b", [L, 16, 512])
    bgb_d = din("b_gate_b", [L, 4, 128])
    gnw_d = din("gla_norm_w", [L, 2, 128])
    w_out_d = din("w_out", [L, D, D])
    ln1w_d = din("ln1_w", [L, KC, 128])
    ln1b_d = din("ln1_b", [L, KC, 128])
    w_up_d = din("w_up", [L, D, 2 * DFF])
    convw_d = din("conv_w", [L, 3 * 88, 128])
    convb_d = din("conv_b", [L, 88, 128])
    w_down_d = din("w_down", [L, DFF, D])
    ln2w_d = din("ln2_w", [L, KC, 128])
    ln2b_d = din("ln2_b", [L, KC, 128])

    y_o = dout("y", [T, D])
    kc_o = dout("kc_o", [L, T, 2, 128])
    vc_o = dout("vc_o", [L, T, 2, 128])
    sf_o = dout("sf_o", [L, 8, 4, 128, 256])
    sb_o = dout("sb_o", [L, 8, 4, 128, 256])

    SCR = dict(kind="ExternalOutput") if SCRATCH_AS_OUTPUT else {}
    xT_d = [Tl(nc.dram_tensor(f"xT_s{i}", [KC, 128, T], F32, **SCR).ap(), f"xT{i}", True) for i in range(2)]
    catT_d = Tl(nc.dram_tensor("catT_s", [KC, 128, T], BF16, **SCR).ap(), "catT", True)
    h2T_d = Tl(nc.dram_tensor("h2T_s", [KC, 128, T], BF16, **SCR).ap(), "h2T", True)

    S = Sched(nc)
    A = Arena(nc, 53000)
    PS = [Tl(nc.alloc_psum_tensor(f"ps{i}", [128, 512], F32)[:, :], f"ps{i}") for i in range(8)]

    def ckpt(name, dumps=()):
        if stop_after != name:
            return
        S.barrier()
        for label, src_ap, shape, dt, deps in dumps:
            o = nc.dram_tensor("dbg_" + label, list(shape), dt, kind="ExternalOutput").ap()
            S.dma("sp", o, src_ap, r=deps)
        S.barrier()
        S.emit()
        raise _Stop

    def act(fn, r=(), w=()):
        S.op("act", fn, r, w)

    def dve(fn, r=(), w=()):
        S.op("dve", fn, r, w)

    def pe(fn, r=(), w=()):
        S.op("pe", fn, r, w)

    def A_copy(out, in_, r, w):
        act(lambda E: E.activation(out=out, in_=in_, func=AF.Copy), r, w)

    def A_ident(out, in_, scale, bias, r, w):
        act(lambda E: E.activation(out=out, in_=in_, func=AF.Identity, scale=scale, bias=bias), r, w)

    def mm_group(out_ap, pairs, r, w):
        n = len(pairs)

        def fn(E):
            ins = None
            for i, (lt, rh) in enumerate(pairs):
                ins = E.matmul(out_ap, lt, rh, start=(i == 0), stop=(i == n - 1))
            return ins
        pe(fn, r, w)

    ident_f = A.f32([128, 128], "ident_f")
    ident_b = A.bf16([128, 128], "ident_b")
    ropeP_b = A.bf16([128, 128], "ropeP_b")
    masks_b = A.bf16([128, 4, 128], "masks_b")
    ones_b = A.bf16([128, 128], "ones_b")
    ones_f = A.f32([128, 128], "ones_f")
    abias = A.f32([128, 8], "abias")
    flags = A.f32([128, 2], "flags")
    mods = [A.f32([128, 96], f"mods{l}") for l in range(L)]
    sc1p = [A.f32([128, 16], f"sc1p{l}") for l in range(L)]
    sc2p = [A.f32([128, 16], f"sc2p{l}") for l in range(L)]
    g1a = [A.f32([128, 16], f"g1a{l}") for l in range(L)]
    g2a = [A.f32([128, 16], f"g2a{l}") for l in range(L)]
    ln1w = [A.f32([128, 16], f"ln1w{l}") for l in range(L)]
    ln1b = [A.f32([128, 16], f"ln1b{l}") for l in range(L)]
    ln2w = [A.f32([128, 16], f"ln2w{l}") for l in range(L)]
    ln2b = [A.f32([128, 16], f"ln2b{l}") for l in range(L)]
    A2 = [A.f32([128, 16], f"A2{l}") for l in range(L)]
    B2 = [A.f32([128, 16], f"B2{l}") for l in range(L)]
    A1n = A.f32([128, 16], "A1n")
    B1n = A.f32([128, 16], "B1n")
    convw = [A.f32([128, 3 * 88], f"convw{l}") for l in range(L)]
    convb = [A.f32([128, 88], f"convb{l}") for l in range(L)]
    nbg = [A.f32([128, 8], f"nbg{l}") for l in range(L)]
    gnw = [A.f32([128, 2], f"gnw{l}") for l in range(L)]
    esink = [A.f32([128, 8], f"esink{l}") for l in range(L)]
    cT_b = A.bf16([128, 16], "cT_b")
    stage = A.f32([128, 128], "stage")
    stage2 = A.f32([128, 128], "stage2")
    bada = A.f32([128, 96], "bada")
    base_top = A.top
    m1 = A.top
    tmpc = A.f32([128, 512], "tmpc")
    S.dma("sp", ident_f[:, :], ident_d[:, :], w=[ident_f])
    dve(lambda E: E.tensor_copy(out=ident_b[:, :], in_=ident_f[:, :]), [ident_f], [ident_b])
    S.dma("sp", tmpc[:, 0:128], ropeP_d[:, :], w=[tmpc])
    dve(lambda E: E.tensor_copy(out=ropeP_b[:, :], in_=tmpc[:, 0:128]), [tmpc], [ropeP_b])
    S.dma("sp", tmpc[:, :], masks_d.rearrange("p a b -> p (a b)"), w=[tmpc])
    dve(lambda E: E.tensor_copy(out=masks_b.ap.rearrange("p a b -> p (a b)"), in_=tmpc[:, :]), [tmpc], [masks_b])
    dve(lambda E: E.memset(ones_b[:, :], 1.0), [], [ones_b])
    dve(lambda E: E.memset(ones_f[:, :], 1.0), [], [ones_f])
    S.dma("sp", abias[:, :], abias_d[:, :], w=[abias])
    S.dma("sp", flags[:, :], flags_d[:, :], w=[flags])
    cont = flags.ap[:, 0:1]

    def vec_fm(dst_tl, dst_ap, src_ap, n, st=None):
        st = st or stage
        S.dma("sp", st[0:n, :], src_ap, w=[st])
        pe(lambda E: E.transpose(PS[7][:, 0:n], st[0:n, :], ident_f[0:n, 0:n]), [st, ident_f], [PS[7]])
        A_copy(dst_ap, PS[7][:, 0:n], [PS[7]], [dst_tl])

    for l in range(L):
        vec_fm(ln1w[l], ln1w[l][:, :], ln1w_d[l], 16)
        vec_fm(ln1b[l], ln1b[l][:, :], ln1b_d[l], 16, stage2)
        vec_fm(ln2w[l], ln2w[l][:, :], ln2w_d[l], 16)
        vec_fm(ln2b[l], ln2b[l][:, :], ln2b_d[l], 16, stage2)
        for k in range(3):
            vec_fm(convw[l], convw[l][:, k * 88:(k + 1) * 88], convw_d[l, k * 88:(k + 1) * 88, :], 88, stage if k % 2 == 0 else stage2)
        vec_fm(convb[l], convb[l][:, :], convb_d[l], 88, stage2)
        vec_fm(nbg[l], nbg[l][:, 0:4], bgf_d[l], 4)
        vec_fm(nbg[l], nbg[l][:, 4:8], bgb_d[l], 4, stage2)
        dve(lambda E, l=l: E.tensor_scalar(out=nbg[l][:, :], in0=nbg[l][:, :], scalar1=-1.0, scalar2=None, op0=ALU.mult), [nbg[l]], [nbg[l]])
        vec_fm(gnw[l], gnw[l][:, :], gnw_d[l], 2)
        S.dma("sp", esink[l][:, :], sink_d[l].partition_broadcast(128), w=[esink[l]])
        act(lambda E, l=l: E.activation(out=esink[l][:, :], in_=esink[l][:, :], func=AF.Exp), [esink[l]], [esink[l]])

    ckpt("consts", [("ln1w0", ln1w[0][:, :], [128, 16], F32, [ln1w[0]]), ("convw0", convw[0][:, :], [128, 264], F32, [convw[0]]),
                    ("nbg0", nbg[0][:, :], [128, 8], F32, [nbg[0]]), ("esink0", esink[0][:, :], [128, 8], F32, [esink[0]]),
                    ("masks", masks_b.ap, [128, 4, 128], BF16, [masks_b])])
    S.label = "M"
    S.dma("sp", stage[0:16, :], cvec_d[:, :], w=[stage])
    pe(lambda E: E.transpose(PS[7][:, 0:16], stage[0:16, :], ident_f[0:16, 0:16]), [stage, ident_f], [PS[7]])
    act(lambda E: E.activation(out=cT_b[:, :], in_=PS[7][:, 0:16], func=AF.Silu), [PS[7]], [cT_b])
    m0 = A.top

    def mods_steps(l, wada, psm):
        steps = []
        nsp = 12288 // wada[0].ap.shape[1]
        jn = 96 // nsp
        cnt = 0
        for kc in range(KC):
            for sp_ in range(nsp):
                def step(kc=kc, sp_=sp_, cnt=cnt):
                    wb = wada[cnt % 2]
                    S.dma("pool", wb[:, :], w_ada_d[l, kc * 128:(kc + 1) * 128, sp_ * jn * 128:(sp_ + 1) * jn * 128], w=[wb])

                    def fn(E, wb=wb, kc=kc, sp_=sp_):
                        ins = None
                        for jj in range(jn):
                            j = sp_ * jn + jj
                            ins = E.matmul(psm[:, j:j + 1], wb[:, jj * 128:(jj + 1) * 128], cT_b[:, kc:kc + 1],
                                           start=(kc == 0 and j == 0), stop=(kc == KC - 1), skip_group_check=True)
                        return ins
                    pe(fn, [wb, cT_b], [psm])
                steps.append(step)
                cnt += 1

        def fin():
            S.dma("sp", stage[0:96, :], b_ada_d[l], w=[stage])
            pe(lambda E: E.transpose(psm[:, 128:224], stage[0:96, :], ident_f[0:96, 0:96]), [stage, ident_f], [psm])
            A_copy(bada[:, :], psm[:, 128:224], [psm], [bada])
            dve(lambda E: E.tensor_tensor(out=mods[l][:, :], in0=psm[:, 0:96], in1=bada[:, :], op=ALU.add), [psm, bada], [mods[l]])
            dve(lambda E: E.tensor_scalar(out=sc1p[l][:, :], in0=mods[l][:, 16:32], scalar1=1.0, scalar2=None, op0=ALU.add), [mods[l]], [sc1p[l]])
            dve(lambda E: E.tensor_scalar(out=sc2p[l][:, :], in0=mods[l][:, 64:80], scalar1=1.0, scalar2=None, op0=ALU.add), [mods[l]], [sc2p[l]])
            dve(lambda E: E.tensor_scalar(out=g1a[l][:, :], in0=mods[l][:, 32:48], scalar1=1.0 / ALPHA, scalar2=None, op0=ALU.mult), [mods[l]], [g1a[l]])
            dve(lambda E: E.tensor_scalar(out=g2a[l][:, :], in0=mods[l][:, 80:96], scalar1=1.0 / ALPHA, scalar2=None, op0=ALU.mult), [mods[l]], [g2a[l]])
            dve(lambda E: E.tensor_tensor(out=A2[l][:, :], in0=ln1w[l][:, :], in1=sc2p[l][:, :], op=ALU.mult), [ln1w[l], sc2p[l]], [A2[l]])
            dve(lambda E: E.tensor_tensor(out=B2[l][:, :], in0=ln1b[l][:, :], in1=sc2p[l][:, :], op=ALU.mult), [ln1b[l], sc2p[l]], [B2[l]])
            dve(lambda E: E.tensor_tensor(out=B2[l][:, :], in0=B2[l][:, :], in1=mods[l][:, 48:64], op=ALU.add), [B2[l], mods[l]], [B2[l]])
            if l == 1:
                dve(lambda E: E.tensor_tensor(out=A1n[:, :], in0=ln2w[0][:, :], in1=sc1p[1][:, :], op=ALU.mult), [ln2w[0], sc1p[1]], [A1n])
                dve(lambda E: E.tensor_tensor(out=B1n[:, :], in0=ln2b[0][:, :], in1=sc1p[1][:, :], op=ALU.mult), [ln2b[0], sc1p[1]], [B1n])
                dve(lambda E: E.tensor_tensor(out=B1n[:, :], in0=B1n[:, :], in1=mods[1][:, 0:16], op=ALU.add), [B1n, mods[1]], [B1n])
        steps.append(fin)
        return steps

    wada0 = [A.bf16([128, 6 * D], f"wada{i}") for i in range(2)]
    for st_ in mods_steps(0, wada0, PS[6]):
        st_()
    if not OVERLAP_MODS:
        for st_ in mods_steps(1, wada0, PS[6]):
            st_()
    S.barrier()
    A.top = m1
    ckpt("M", [("mods0", mods[0][:, :], [128, 96], F32, [mods[0]])])

    hT_base = A.top
    hT = A.bf16([128, KC, T], "hT", ww=True)
    work0 = A.top

    S.label = "p0"
    xb = [A.f32([128, D], f"xb{i}") for i in range(2)]
    xTs = [A.f32([128, KC, 128], f"xTs{i}", ww=True) for i in range(2)]
    for b in range(16):
        xbt = xb[b % 2]
        xs = xTs[b % 2]
        S.dma("sp", xbt[:, :], x_d[b * 128:(b + 1) * 128, :], w=[xbt])
        for c4 in range(4):
            pb = PS[c4 % 4]

            def fn(E, pb=pb, xbt=xbt, c4=c4):
                ins = None
                for j in range(4):
                    c = c4 * 4 + j
                    ins = E.transpose(pb[:, j * 128:(j + 1) * 128], xbt[:, c * 128:(c + 1) * 128], ident_f[:, :])
                return ins
            if 'nope' in DBGF:
                continue
            pe(fn, [xbt, ident_f], [pb])
            if 'noact' not in DBGF:
                A_copy(xs[:, c4 * 4:(c4 + 1) * 4, :], pb[:, :].rearrange("p (a b) -> p a b", a=4), [pb], [xs])
            for j in range(4):
                c = c4 * 4 + j
                A_ident(hT[:, c, b * 128:(b + 1) * 128], pb[:, j * 128:(j + 1) * 128], sc1p[0][:, c:c + 1], mods[0][:, c:c + 1],
                        [pb, sc1p[0], mods[0]], [hT])
        if 'nostore' not in DBGF:
            S.dma("sp", xT_d[0].ap[:, :, b * 128:(b + 1) * 128].rearrange("c p t -> p c t"), xs[:, :, :], r=[xs], w=[xT_d[0]])
    S.barrier(new_epoch=True)
    A.top = work0
    ckpt("p0", [("hT", hT.ap, [128, KC, T], BF16, [hT])] + ([] if 'nodumpx' in DBGF else [("xT", xT_d[0].ap, [KC, 128, T], F32, [xT_d[0]])]) + [
                ("mods0", mods[0][:, :], [128, 96], F32, [mods[0]])])

    def load_w(tl, dram_cols_ap):
        S.dma("pool", tl.ap, dram_cols_ap.rearrange("(kc p) n -> p kc n", p=128), w=[tl])

    def proj_fm(wt, col, tt, ps_tl):
        mm_group(ps_tl[:, :], [(wt[:, kc, col * 128:(col + 1) * 128], hT[:, kc, tt * 512:(tt + 1) * 512]) for kc in range(KC)],
                 [wt, hT], [ps_tl])


    def mk_chunks(tl3, n):
        return [Tl(tl3.ap[:, c, :], f"{tl3.name}_{c}") for c in range(n)]

    def ln_stats_chunk(zc, tb, c):
        sqb, zbb, mean, rstd, s1p, s2p = tb
        sqc, zbc = sqb[c % 2], zbb[c % 2]
        act(lambda E, sqc=sqc, zt=zc[c]: E.activation(out=sqc[:, :], in_=zt[:, :], func=AF.Square), [zc[c]], [sqc])
        dve(lambda E, zbc=zbc, zt=zc[c]: E.tensor_copy(out=zbc[:, :], in_=zt[:, :]), [zc[c]], [zbc])
        pe(lambda E, c=c, zbc=zbc: E.matmul(s1p[:, :], ones_b[:, :], zbc[:, :], start=(c == 0), stop=(c == KC - 1)), [zbc, ones_b], [s1p])
        pe(lambda E, c=c, sqc=sqc: E.matmul(s2p[:, :], ones_b[:, :], sqc[:, :], start=(c == 0), stop=(c == KC - 1)), [sqc, ones_b], [s2p])

    def ln_norm(zc, tb, out_x, out_h):
        sqb, zbb, mean, rstd, s1p, s2p = tb
        make_ln_finish(S, act, dve, A_ident, None, mean, rstd, s1p, s2p)()
        for c in range(KC):
            zt = zc[c]
            dve(lambda E, zt=zt: E.tensor_tensor(out=zt[:, :], in0=zt[:, :], in1=mean[:, :], op=ALU.subtract), [zt, mean], [zt])
            dve(lambda E, zt=zt: E.tensor_tensor(out=zt[:, :], in0=zt[:, :], in1=rstd[:, :], op=ALU.mult), [zt, rstd], [zt])
            if out_x is not None:
                out_x(c, zt)
            if out_h is not None:
                out_h(c, zt)

    def ln_tile(zc, tb, lw, lb, out_x, out_h):
        for c in range(KC):
            ln_stats_chunk(zc, tb, c)
        ln_norm(zc, tb, out_x, out_h)

    def layer(l):
        xin = xT_d[0]
        xmid = xT_d[1]
        win = w_in_d[l]
        p1 = A.top
        S.label = f"l{l}_attn"
        wbuf = [A.bf16([128, KC, 128], f"wbuf{i}") for i in range(3)]
        wi = [0]

        def next_w(c0, n=128):
            tl = wbuf[wi[0] % 3]
            wi[0] += 1
            load_w(tl, win[:, c0:c0 + n])
            return tl

        qr = A.bf16([128, 16 * 4 * 128], "qr", ww=True)
        kr = A.bf16([128, T], "kr", ww=True)
        vtok = A.bf16([128, 16, 128], "vtok", ww=True)
        kst = A.f32([128, 16, 128], "kst", ww=True)
        vst = A.f32([128, 16, 128], "vst", ww=True)
        ckT = A.bf16([128, 256], "ckT")
        cvt = A.bf16([128, 2, 128], "cvt")
        ckst = A.f32([128, 2, 128], "ckst")
        cosb = [A.f32([128, 512], f"cos{i}") for i in range(2)]
        sinb = [A.f32([128, 512], f"sin{i}") for i in range(2)]
        qb = [A.bf16([128, 512], f"qb{i}") for i in range(2)]
        t1 = [A.f32([128, 512], f"t1{i}") for i in range(2)]
        t2 = [A.f32([128, 512], f"t2{i}") for i in range(2)]
        Eb = [A.bf16([128, 512], f"E{i}") for i in range(3)]
        den = A.f32([128, 512], "den")
        oatt = A.bf16([128, 4, T], "oatt", ww=True)
        extra = []
        if l == 0 and OVERLAP_MODS:
            wada1 = [A.bf16([128, 3 * D], f"wadb{i}") for i in range(2)]
            extra = mods_steps(1, wada1, PS[7])
        qrv = qr.ap.rearrange("p (b h q) -> p b h q", b=16, h=4)
        cnt = [0]
        for kh in range(2):
            for hi in range(5):
                c0 = (Q0 + (4 * kh + hi) * 128) if hi < 4 else (K0 + kh * 128)
                wt = next_w(c0)
                for tt in range(4):
                    n = cnt[0]
                    cnt[0] += 1
                    pq = PS[n % 2]
                    pr = PS[2 + n % 2]
                    cb, sb_, qbb, t1b, t2b = cosb[n % 2], sinb[n % 2], qb[n % 2], t1[n % 2], t2[n % 2]
                    S.dma("sp", cb[:, :], cos_d[:, tt * 512:(tt + 1) * 512], w=[cb])
                    S.dma("sp", sb_[:, :], sin_d[:, tt * 512:(tt + 1) * 512], w=[sb_])
                    proj_fm(wt, 0, tt, pq)
                    A_copy(qbb[:, :], pq[:, :], [pq], [qbb])
                    pe(lambda E, pr=pr, qbb=qbb: E.matmul(pr[:, :], ropeP_b[:, :], qbb[:, :], start=True, stop=True), [qbb, ropeP_b], [pr])
                    dve(lambda E, t1b=t1b, qbb=qbb, cb=cb: E.tensor_tensor(out=t1b[:, :], in0=qbb[:, :], in1=cb[:, :], op=ALU.mult), [qbb, cb], [t1b])
                    dve(lambda E, t2b=t2b, pr=pr, sb_=sb_: E.tensor_tensor(out=t2b[:, :], in0=pr[:, :], in1=sb_[:, :], op=ALU.mult), [pr, sb_], [t2b])
                    if hi < 4:
                        dve(lambda E, t1b=t1b, t2b=t2b, tt=tt, hi=hi: E.tensor_tensor(
                            out=qrv[:, tt * 4:(tt + 1) * 4, hi, :], in0=t1b[:, :].rearrange("p (a b) -> p a b", a=4),
                            in1=t2b[:, :].rearrange("p (a b) -> p a b", a=4), op=ALU.add), [t1b, t2b], [qr])
                    else:
                        dve(lambda E, t1b=t1b, t2b=t2b, tt=tt: E.tensor_tensor(
                            out=kr[:, tt * 512:(tt + 1) * 512], in0=t1b[:, :], in1=t2b[:, :], op=ALU.add), [t1b, t2b], [kr])
            for which in range(2):
                c0 = (K0 if which == 0 else V0) + kh * 128
                wt = next_w(c0)
                st_ = kst if which == 0 else vst
                for b in range(16):
                    pb = PS[4 + b % 2]
                    mm_group(pb[:, 0:128], [(hT[:, kc, b * 128:(b + 1) * 128], wt[:, kc, :]) for kc in range(KC)], [wt, hT], [pb])
                    A_copy(st_[:, b, :], pb[:, 0:128], [pb], [st_])
                    if which == 1:
                        dve(lambda E, b=b: E.tensor_copy(out=vtok[:, b, :], in_=vst[:, b, :]), [vst], [vtok])
                od = kc_o if which == 0 else vc_o
                S.dma("sp", od[l, :, kh, :].rearrange("(b p) d -> p b d", p=128), st_[:, :, :], r=[st_])
            S.dma("sp", ckst[:, :, :], ck_d[l, :, kh, :].rearrange("(j p) d -> p j d", p=128), w=[ckst])
            for j in range(2):
                pe(lambda E, j=j: E.transpose(PS[6][:, j * 128:(j + 1) * 128], ckst[:, j, :], ident_f[:, :]), [ckst, ident_f], [PS[6]])
            A_copy(ckT[:, :], PS[6][:, 0:256], [PS[6]], [ckT])
            S.dma("pool", cvt[:, :, :], cv_d[l, :, kh, :].rearrange("(j p) d -> p j d", p=128), w=[cvt])
            scale = 128.0 ** -0.5
            ecnt = 0
            for i in range(16):
                chunks = []
                ev = i % 2
                if i > 0:
                    chunks.append((kr[:, (i - 1) * 128:i * 128], vtok[:, i - 1, :], abias[:, ev:ev + 1], 0, [kr], [vtok]))
                chunks.append((kr[:, i * 128:(i + 1) * 128], vtok[:, i, :], 0.0, None, [kr], [vtok]))
                if i < 15:
                    chunks.append((kr[:, (i + 1) * 128:(i + 2) * 128], vtok[:, i + 1, :], abias[:, 2 + ev:3 + ev], 1, [kr], [vtok]))
                for j in range(2):
                    chunks.append((ckT[:, j * 128:(j + 1) * 128], cvt[:, j, :], abias[:, 4:5], None, [ckT], [cvt]))
                po = PS[4 + i % 2]
                pl = PS[6]
                qi = qrv[:, i, :, :].rearrange("p h q -> p (h q)")
                nchk = len(chunks)
                pend = None
                for ci, (kT, vv, bias, mk, kdep, vdep) in enumerate(chunks):
                    psb = PS[ecnt % 2]
                    eb = Eb[ecnt % 3]
                    ecnt += 1
                    pe(lambda E, psb=psb, kT=kT, qi=qi: E.matmul(psb[:, :], kT, qi, start=True, stop=True), kdep + [qr], [psb])
                    act(lambda E, eb=eb, psb=psb, bias=bias: E.activation(out=eb[:, :], in_=psb[:, :], func=AF.Exp, scale=scale, bias=bias),
                        [psb, abias], [eb])
                    if mk is not None:
                        dve(lambda E, eb=eb, mk=mk: E.tensor_tensor(
                            out=eb[:, :].rearrange("p (h q) -> p h q", h=4), in0=eb[:, :].rearrange("p (h q) -> p h q", h=4),
                            in1=masks_b[:, mk, :].unsqueeze(1).to_broadcast([128, 4, 128]), op=ALU.mult), [eb, masks_b], [eb])
                    if pend is not None:
                        pci, peb, pvv, pvdep = pend
                        pe(lambda E, po=po, pvv=pvv, peb=peb, pci=pci: E.matmul(po[:, :], pvv, peb[:, :], start=(pci == 0), stop=False), pvdep + [peb], [po])
                        pe(lambda E, pl=pl, peb=peb, pci=pci: E.matmul(pl[:, :], ones_b[:, :], peb[:, :], start=(pci == 0), stop=False), [peb, ones_b], [pl])
                    pend = (ci, eb, vv, vdep)
                pci, peb, pvv, pvdep = pend
                pe(lambda E, po=po, pvv=pvv, peb=peb, pci=pci: E.matmul(po[:, :], pvv, peb[:, :], start=(pci == 0), stop=True), pvdep + [peb], [po])
                pe(lambda E, pl=pl, peb=peb, pci=pci: E.matmul(pl[:, :], ones_b[:, :], peb[:, :], start=(pci == 0), stop=True), [peb, ones_b], [pl])
                dve(lambda E, pl=pl, kh=kh: E.tensor_tensor(
                    out=den[:, :].rearrange("p (h q) -> p h q", h=4), in0=pl[:, :].rearrange("p (h q) -> p h q", h=4),
                    in1=esink[l][:, 4 * kh:4 * kh + 4].unsqueeze(2).to_broadcast([128, 4, 128]), op=ALU.add), [pl, esink[l]], [den])
                dve(lambda E: E.reciprocal(out=den[:, :], in_=den[:, :]), [den], [den])
                dve(lambda E, po=po, i=i: E.tensor_tensor(
                    out=oatt[:, :, i * 128:(i + 1) * 128], in0=po[:, :].rearrange("p (h q) -> p h q", h=4),
                    in1=den[:, :].rearrange("p (h q) -> p h q", h=4), op=ALU.mult), [po, den], [oatt])
                if extra:
                    extra.pop(0)()
            for g in range(4):
                S.dma("sp", catT_d.ap[4 * kh + g, :, :], oatt[:, g, :], r=[oatt], w=[catT_d])
        while extra:
            extra.pop(0)()
        S.barrier()
        A.top = p1
        ckpt(f"l{l}a", [("catTa", catT_d.ap[0:8], [8, 128, T], BF16, [catT_d])])

        S.label = f"l{l}_gla"
        wbuf = [A.bf16([128, KC, 128], f"gwbuf{i}") for i in range(3)]
        wi[0] = 0
        wgbd_l = A.bf16([32, 1024], "wgbd")
        dve(lambda E: E.memset(wgbd_l[:, :], 0.0), [], [wgbd_l])
        S.dma("pool", wgbd_l[0:16, 0:512], wgf_d[l], w=[wgbd_l])
        S.dma("pool", wgbd_l[16:32, 512:1024], wgb_d[l], w=[wgbd_l])
        lrT = A.bf16([32, T], "lrT", ww=True)
        gq = A.bf16([128, T], "gq", ww=True)
        gk = A.bf16([128, T], "gk", ww=True)
        gvt = A.bf16([128, 16, 256], "gvt", ww=True)
        o_sb = A.f32([128, 2, T], "o_sb")
        gt = A.f32([128, T], "gt")
        cum = A.f32([128, T], "cum")
        qe = A.bf16([128, T], "qe")
        ke = A.bf16([128, T], "ke")
        kd = A.bf16([128, T], "kd")
        kdt = A.bf16([128, 16, 128], "kdt", ww=True)
        nle = A.f32([128, 16], "nle")
        dec = A.f32([128, 16], "dec")
        NCH = 6
        chain = [A.f32([128, 256], f"chain{i}") for i in range(NCH)]
        Sb_all = A.bf16([128, 16, 256], "Sb_all", ww=True)
        At4 = [A.bf16([128, 512], f"At4{i}") for i in range(2)]
        PSU = [PS[2], PS[3]]
        sq = [A.bf16([128, 512], f"gsq{i}") for i in range(2)]
        rs = A.f32([128, 512], "grs")
        sg = [A.f32([128, 512], f"gsg{i}") for i in range(2)]
        tq = A.f32([128, 512], "gtq")
        catg = [A.bf16([128, T], f"catg{i}", ww=True) for i in range(2)]
        wlr = A.bf16([128, KC, 32], "wlr")
        load_w(wlr, win[:, LR0:LR0 + 32])
        for tt in range(4):
            pb = PS[tt % 2]
            mm_group(pb[0:32, :], [(wlr[:, kc, :], hT[:, kc, tt * 512:(tt + 1) * 512]) for kc in range(KC)], [wlr, hT], [pb])
            A_copy(lrT[:, tt * 512:(tt + 1) * 512], pb[0:32, :], [pb], [lrT])
        for hd in range(4):
            for which, dst in ((0, gq), (1, gk)):
                wt = next_w((GQ0 if which == 0 else GK0) + hd * 128)
                for tt in range(4):
                    pb = PS[tt % 2]
                    proj_fm(wt, 0, tt, pb)
                    A_copy(dst[:, tt * 512:(tt + 1) * 512], pb[:, :], [pb], [dst])
            wv = [next_w(GV0 + hd * 256), next_w(GV0 + hd * 256 + 128)]
            for b in range(16):
                pb = PS[2 + b % 2]
                for vc in range(2):
                    mm_group(pb[:, vc * 128:(vc + 1) * 128], [(hT[:, kc, b * 128:(b + 1) * 128], wv[vc][:, kc, :]) for kc in range(KC)],
                             [wv[vc], hT], [pb])
                A_copy(gvt[:, b, :], pb[:, 0:256], [pb], [gvt])
            for dr in range(2):
                for tt in range(4):
                    pb = PS[tt % 2]
                    col = dr * 512 + hd * 128
                    pe(lambda E, pb=pb, col=col, tt=tt: E.matmul(pb[:, :], wgbd_l[:, col:col + 128], lrT[:, tt * 512:(tt + 1) * 512], start=True, stop=True),
                       [wgbd_l, lrT], [pb])
                    act(lambda E, pb=pb, tt=tt, dr=dr, hd=hd: E.activation(out=gt[:, tt * 512:(tt + 1) * 512], in_=pb[:, :], func=AF.Exp, scale=-1.0,
                                                                          bias=nbg[l][:, dr * 4 + hd:dr * 4 + hd + 1]), [pb, nbg[l]], [gt])
                act(lambda E: E.activation(out=gt[:, :], in_=gt[:, :], func=AF.Ln, bias=1.0, scale=1.0), [gt], [gt])
                for c in range(16):
                    dve(lambda E, c=c: E.tensor_tensor_scan(out=cum[:, c * 128:(c + 1) * 128], data0=ones_b[:, :], data1=gt[:, c * 128:(c + 1) * 128],
                                                            initial=0.0, op0=ALU.mult, op1=ALU.add), [gt, ones_b], [cum])
                cum3 = cum.ap.rearrange("p (c t) -> p c t", c=16)
                gt3 = gt.ap.rearrange("p (c t) -> p c t", c=16)
                if dr == 1:
                    dve(lambda E: E.tensor_tensor(out=gt[:, :], in0=gt[:, :], in1=cum[:, :], op=ALU.subtract), [gt, cum], [gt])
                    dve(lambda E: E.tensor_tensor(out=gt3, in0=gt3, in1=cum3[:, :, 127:128].to_broadcast([128, 16, 128]), op=ALU.add), [gt, cum], [gt])
                    Lc, Lc3, tmpb, endcol = gt, gt3, cum, 0
                else:
                    Lc, Lc3, tmpb, endcol = cum, cum3, gt, 127
                dve(lambda E, Lc3=Lc3, endcol=endcol: E.tensor_scalar(out=nle[:, :].unsqueeze(2), in0=Lc3[:, :, endcol:endcol + 1], scalar1=-1.0 / 16, scalar2=None, op0=ALU.mult),
                    [Lc], [nle])
                act(lambda E: E.activation(out=dec[:, :], in_=nle[:, :], func=AF.Exp), [nle], [dec])
                for c in range(16):
                    act(lambda E, c=c, Lc=Lc, tmpb=tmpb: E.activation(out=tmpb[:, c * 128:(c + 1) * 128], in_=Lc[:, c * 128:(c + 1) * 128], func=AF.Exp,
                                                                      scale=1.0 / 16, bias=nle[:, c:c + 1]), [Lc, nle], [tmpb])
                dve(lambda E, tmpb=tmpb: E.tensor_tensor(out=kd[:, :], in0=gk[:, :], in1=tmpb[:, :], op=ALU.mult), [gk, tmpb], [kd])
                act(lambda E, Lc=Lc, tmpb=tmpb: E.activation(out=tmpb[:, :], in_=Lc[:, :], func=AF.Exp, scale=1.0 / 16), [Lc], [tmpb])
                dve(lambda E, tmpb=tmpb: E.tensor_tensor(out=ke[:, :], in0=gk[:, :], in1=tmpb[:, :], op=ALU.mult), [gk, tmpb], [ke])
                act(lambda E, Lc=Lc, tmpb=tmpb: E.activation(out=tmpb[:, :], in_=Lc[:, :], func=AF.Exp, scale=-1.0 / 16), [Lc], [tmpb])
                dve(lambda E, tmpb=tmpb: E.scalar_tensor_tensor(out=qe[:, :], in0=gq[:, :], scalar=128.0 ** -0.5, in1=tmpb[:, :], op0=ALU.mult, op1=ALU.mult),
                    [gq, tmpb], [qe])
                for c4 in range(4):
                    pb = PS[2 + c4 % 2]
                    pbb = pb[:, 0:256].bitcast(BF16)

                    def fn(E, pbb=pbb, c4=c4):
                        ins = None
                        for j in range(4):
                            c = c4 * 4 + j
                            ins = E.transpose(pbb[:, j * 128:(j + 1) * 128], kd[:, c * 128:(c + 1) * 128], ident_b[:, :])
                        return ins
                    pe(fn, [kd, ident_b], [pb])
                    A_copy(kdt[:, c4 * 4:(c4 + 1) * 4, :], pbb.rearrange("p (a b) -> p a b", a=4), [pb], [kdt])
                s0 = (s0f_d if dr == 0 else s0b_d)
                so = (sf_o if dr == 0 else sb_o)
                order = list(range(16)) if dr == 0 else list(range(15, -1, -1))
                ci = 0
                cur = chain[0]
                S.dma("sp", cur[:, :], s0[l, hd], w=[cur])
                for n, c in enumerate(order):
                    boundary = (n > 0) and (n % 2 == 0)
                    if boundary:
                        nxt = chain[(ci + 1) % NCH]
                        ci += 1
                        dve(lambda E, nxt=nxt, cur=cur: E.tensor_scalar(out=nxt[:, :], in0=cur[:, :], scalar1=cont, scalar2=None, op0=ALU.mult), [cur, flags], [nxt])
                        cur = nxt
                    A_copy(Sb_all[:, c, :], cur[:, :], [cur], [Sb_all])
                    pu = PSU[n % 2]
                    pe(lambda E, pu=pu, c=c: E.matmul(pu[:, 0:256], kdt[:, c, :], gvt[:, c, :], start=True, stop=True), [kdt, gvt], [pu])
                    nxt = chain[(ci + 1) % NCH]
                    ci += 1
                    dve(lambda E, nxt=nxt, cur=cur, pu=pu, c=c: E.scalar_tensor_tensor(out=nxt[:, :], in0=cur[:, :], scalar=dec[:, c:c + 1], in1=pu[:, 0:256],
                                                                                       op0=ALU.mult, op1=ALU.add), [cur, dec, pu], [nxt])
                    cur = nxt
                    if n % 2 == 1:
                        S.dma("sp", so[l, c // 2, hd], cur[:, :], r=[cur])
                for g in range(4):
                    cs = order[4 * g:4 * g + 4]
                    g4 = min(cs) // 4
                    pa = PS[0] if g % 2 == 0 else PS[6]
                    att = At4[g % 2]

                    def fnA(E, pa=pa, cs=cs):
                        ins = None
                        for c in cs:
                            k = c % 4
                            ins = E.matmul(pa[:, k * 128:(k + 1) * 128], ke[:, c * 128:(c + 1) * 128], qe[:, c * 128:(c + 1) * 128], start=True, stop=True)
                        return ins
                    pe(fnA, [ke, qe], [pa])
                    dve(lambda E, att=att, pa=pa, dr=dr: E.tensor_tensor(
                        out=att[:, :].rearrange("p (k t) -> p k t", k=4), in0=pa[:, :].rearrange("p (k t) -> p k t", k=4),
                        in1=masks_b[:, 2 + dr, :].unsqueeze(1).to_broadcast([128, 4, 128]), op=ALU.mult), [pa, masks_b], [att])
                    pov = (PS[1], PS[4]) if g % 2 == 0 else (PS[5], PS[7])
                    for vc in range(2):
                        po = pov[vc]

                        def fnO(E, po=po, cs=cs, vc=vc, att=att):
                            ins = None
                            for c in cs:
                                k = c % 4
                                E.matmul(po[:, k * 128:(k + 1) * 128], gvt[:, c, vc * 128:(vc + 1) * 128], att[:, k * 128:(k + 1) * 128], start=True, stop=False)
                                ins = E.matmul(po[:, k * 128:(k + 1) * 128], Sb_all[:, c, vc * 128:(vc + 1) * 128], qe[:, c * 128:(c + 1) * 128], start=False, stop=True)
                            return ins
                        pe(fnO, [gvt, att, Sb_all, qe], [po])
                        if dr == 0:
                            A_copy(o_sb[:, vc, g4 * 512:(g4 + 1) * 512], po[:, :], [po], [o_sb])
                        else:
                            dve(lambda E, po=po, vc=vc, g4=g4: E.tensor_tensor(out=o_sb[:, vc, g4 * 512:(g4 + 1) * 512], in0=o_sb[:, vc, g4 * 512:(g4 + 1) * 512],
                                                                              in1=po[:, :], op=ALU.add), [po, o_sb], [o_sb])
            wg = [next_w(GO0 + hd * 256), next_w(GO0 + hd * 256 + 128)]
            for tt in range(4):
                pm = PS[tt % 2]
                for vc in range(2):
                    sqv = sq[vc]
                    act(lambda E, sqv=sqv, vc=vc, tt=tt: E.activation(out=sqv[:, :], in_=o_sb[:, vc, tt * 512:(tt + 1) * 512], func=AF.Square), [o_sb], [sqv])
                mm_group(pm[:, :], [(ones_b[:, :], sq[0][:, :]), (ones_b[:, :], sq[1][:, :])], [ones_b, sq[0], sq[1]], [pm])
                act(lambda E, pm=pm: E.activation(out=rs[:, :], in_=pm[:, :], func=AF.Sqrt, scale=1.0 / 256, bias=1e-5), [pm], [rs])
                dve(lambda E: E.reciprocal(out=rs[:, :], in_=rs[:, :]), [rs], [rs])
                for vc in range(2):
                    pg = PS[2 + vc]
                    proj_fm(wg[vc], 0, tt, pg)
                    sgv = sg[vc]
                    act(lambda E, sgv=sgv, pg=pg: E.activation(out=sgv[:, :], in_=pg[:, :], func=AF.Silu), [pg], [sgv])
                    dve(lambda E, vc=vc, tt=tt: E.tensor_tensor(out=tq[:, :], in0=o_sb[:, vc, tt * 512:(tt + 1) * 512], in1=rs[:, :], op=ALU.mult), [o_sb, rs], [tq])
                    dve(lambda E, vc=vc, tt=tt, sgv=sgv: E.scalar_tensor_tensor(out=catg[vc][:, tt * 512:(tt + 1) * 512], in0=tq[:, :], scalar=gnw[l][:, vc:vc + 1],
                                                                               in1=sgv[:, :], op0=ALU.mult, op1=ALU.mult), [tq, gnw[l], sgv], [catg[vc]])
            for vc in range(2):
                S.dma("sp", catT_d.ap[8 + 2 * hd + vc, :, :], catg[vc][:, :], r=[catg[vc]], w=[catT_d])
        S.barrier()
        A.top = hT_base
        ckpt(f"l{l}b", [("catT", catT_d.ap, [KC, 128, T], BF16, [catT_d])])

        S.label = f"l{l}_p3"
        wob = [A.bf16([128, KC, 256], f"wob{i}") for i in range(3)]
        wocnt = [0]
        catt = [A.bf16([128, KC, 512], f"catt{i}") for i in range(2)]
        zbuf = [A.f32([128, KC, 512], f"z{i}") for i in range(2)]
        zch = [mk_chunks(zb_, KC) for zb_ in zbuf]
        xc = [A.f32([128, 512], f"xc{i}") for i in range(3)]
        tb3 = ([A.bf16([128, 512], f"lsq{i}") for i in range(2)], [A.bf16([128, 512], f"lzb{i}") for i in range(2)],
               A.f32([128, 512], "lmean"), A.f32([128, 512], "lrstd"), PS[4], PS[5])
        xo = [A.f32([128, 512], f"lxo{i}") for i in range(3)]
        ho = [A.bf16([128, 512], f"ho{i}") for i in range(3)]
        def p3_group(tt, oc):
            ct = catt[tt % 2]
            zc = zch[tt % 2]
            if oc == 0:
                S.dma("sp", ct[:, :, :], catT_d.ap[:, :, tt * 512:(tt + 1) * 512].rearrange("c p t -> p c t"), r=[catT_d], w=[ct])
            pb = PS[oc % 4]
            xct = xc[oc % 3]
            S.dma("sp", xct[:, :], xin.ap[oc, :, tt * 512:(tt + 1) * 512], r=[xin], w=[xct])
            if oc % 2 == 0:
                wo_t = wob[wocnt[0] % 3]
                wocnt[0] += 1
                load_w(wo_t, w_out_d[l][:, oc * 128:oc * 128 + 256])
                wocur[0] = wo_t
            wo_t = wocur[0]
            o2 = oc % 2
            mm_group(pb[:, :], [(wo_t[:, kc, o2 * 128:(o2 + 1) * 128], ct[:, kc, :]) for kc in range(KC)], [wo_t, ct], [pb])
            dve(lambda E, pb=pb, oc=oc, xct=xct, zt=zc[oc]: E.scalar_tensor_tensor(out=zt[:, :], in0=pb[:, :], scalar=g1a[l][:, oc:oc + 1], in1=xct[:, :],
                                                                                 op0=ALU.mult, op1=ALU.add), [pb, g1a[l], xct], [zc[oc]])

        def p3_outs(tt):
            def out_x(c, zt):
                xoc = xo[c % 3]
                A_ident(xoc[:, :], zt[:, :], ln1w[l][:, c:c + 1], ln1b[l][:, c:c + 1], [zt, ln1w[l], ln1b[l]], [xoc])
                S.dma("sp", xmid.ap[c, :, tt * 512:(tt + 1) * 512], xoc[:, :], r=[xoc], w=[xmid])

            def out_h(c, zt):
                hoc = ho[c % 3]
                A_ident(hoc[:, :], zt[:, :], A2[l][:, c:c + 1], B2[l][:, c:c + 1], [zt, A2[l], B2[l]], [hoc])
                S.dma("sp", h2T_d.ap[c, :, tt * 512:(tt + 1) * 512], hoc[:, :], r=[hoc], w=[h2T_d])
            return out_x, out_h

        wocur = [None]
        for oc in range(KC):
            p3_group(0, oc)
        for tt in range(4):
            for oc in range(KC):
                if tt + 1 < 4:
                    p3_group(tt + 1, oc)
                ln_stats_chunk(zch[tt % 2], tb3, oc)
            ox, oh = p3_outs(tt)
            ln_norm(zch[tt % 2], tb3, ox, oh)
        S.barrier()
        A.top = hT_base
        ckpt(f"l{l}p3", [("x1T", xmid.ap, [KC, 128, T], F32, [xmid]), ("h2T", h2T_d.ap, [KC, 128, T], BF16, [h2T_d])])

        actT = A.bf16([128, NH, 1024], "actT", ww=True)
        p4 = A.top
        last = (l == L - 1)
        for hf in range(2):
            A.top = p4
            S.label = f"l{l}_up{hf}"
            t0 = hf * 1024
            h2h = A.bf16([128, KC, 1024], "h2h")
            halo = A.bf16([128, KC, 2], "halo")
            up_top = A.top
            S.dma("sp", h2h[:, :, :], h2T_d.ap[:, :, t0:t0 + 1024].rearrange("c p t -> p c t"), r=[h2T_d], w=[h2h])
            ht = 1024 if hf == 0 else 1023
            S.dma("sp", halo[:, :, 0:1], h2T_d.ap[:, :, ht:ht + 1].rearrange("c p t -> p c t"), r=[h2T_d], w=[halo], slow=True)
            wup = [[A.bf16([128, KC, 256], f"wupa{i}") for i in range(2)], [A.bf16([128, KC, 256], f"wupg{i}") for i in range(2)]]
            U = [[A.f32([128, 4, 258], f"Ua{i}") for i in range(2)], [A.f32([128, 4, 258], f"Ug{i}") for i in range(2)]]
            acc = [[A.f32([128, 1024], f"acca{i}") for i in range(2)], [A.f32([128, 1024], f"accg{i}") for i in range(2)]]
            for ag in range(2):
                for i in range(2):
                    dve(lambda E, ag=ag, i=i: E.memset(U[ag][i][:, :, :], 0.0), [], [U[ag][i]])
            for j in range(NH):
                if j % 2 == 0:
                    for ag in range(2):
                        wt = wup[ag][(j // 2) % 2]
                        load_w(wt, w_up_d[l][:, ag * DFF + j * 128: ag * DFF + j * 128 + 256])
                for ag in range(2):
                    wt = wup[ag][(j // 2) % 2]
                    wcol = (j % 2)
                    Ut = U[ag][j % 2]
                    at = acc[ag][j % 2]
                    idx = ag * NH + j
                    for t2_ in range(2):
                        pb = PS[ag * 2 + t2_]
                        mm_group(pb[:, :], [(wt[:, kc, wcol * 128:(wcol + 1) * 128], h2h[:, kc, t2_ * 512:(t2_ + 1) * 512]) for kc in range(KC)], [wt, h2h], [pb])
                        A_copy(Ut[:, 2 * t2_:2 * t2_ + 2, 1:257], pb[:, :].rearrange("p (s t) -> p s t", s=2), [pb], [Ut])
                    pmn = PS[4 + j % 2]
                    mcol = ag * 4
                    mm_group(pmn[:, mcol:mcol + 1], [(wt[:, kc, wcol * 128:(wcol + 1) * 128], halo[:, kc, 0:1]) for kc in range(KC)], [wt, halo], [pmn])
                    dve(lambda E, Ut=Ut: E.tensor_scalar(out=Ut[:, 1:4, 0:1], in0=Ut[:, 0:3, 256:257], scalar1=cont, scalar2=None, op0=ALU.mult), [Ut, flags], [Ut])
                    dve(lambda E, Ut=Ut: E.tensor_scalar(out=Ut[:, 0:3, 257:258], in0=Ut[:, 1:4, 1:2], scalar1=cont, scalar2=None, op0=ALU.mult), [Ut, flags], [Ut])
                    if hf == 0:
                        dve(lambda E, Ut=Ut, pmn=pmn, mcol=mcol: E.tensor_scalar(out=Ut[:, 3, 257:258], in0=pmn[:, mcol:mcol + 1], scalar1=cont, scalar2=None, op0=ALU.mult),
                            [pmn, flags], [Ut])
                    else:
                        dve(lambda E, Ut=Ut, pmn=pmn, mcol=mcol: E.tensor_scalar(out=Ut[:, 0, 0:1], in0=pmn[:, mcol:mcol + 1], scalar1=cont, scalar2=None, op0=ALU.mult),
                            [pmn, flags], [Ut])
                    a3 = at[:, :].rearrange("p (s t) -> p s t", s=4)
                    A_ident(a3, Ut[:, :, 1:257], convw[l][:, 88 + idx:88 + idx + 1], convb[l][:, idx:idx + 1], [Ut, convw[l], convb[l]], [at])
                    dve(lambda E, a3=a3, Ut=Ut, idx=idx: E.scalar_tensor_tensor(out=a3, in0=Ut[:, :, 0:256], scalar=convw[l][:, idx:idx + 1], in1=a3, op0=ALU.mult, op1=ALU.add),
                        [Ut, convw[l], at], [at])
                    dve(lambda E, a3=a3, Ut=Ut, idx=idx: E.scalar_tensor_tensor(out=a3, in0=Ut[:, :, 2:258], scalar=convw[l][:, 176 + idx:176 + idx + 1], in1=a3, op0=ALU.mult, op1=ALU.add),
                        [Ut, convw[l], at], [at])
                aa, gg = acc[0][j % 2], acc[1][j % 2]
                act(lambda E, gg=gg: E.activation(out=gg[:, :], in_=gg[:, :], func=AF.Silu), [gg], [gg])
                dve(lambda E, aa=aa, gg=gg, j=j: E.tensor_tensor(out=actT[:, j, :], in0=aa[:, :], in1=gg[:, :], op=ALU.mult), [aa, gg], [actT])
            S.barrier()
            S.label = f"l{l}_dn{hf}"
            A.top = p4
            wdn = [A.bf16([128, NH, 128], f"wdn{i}") for i in range(2)]
            z2 = [A.f32([128, KC, 512], f"z2{i}") for i in range(2)]
            z2cs = [mk_chunks(z2[i], KC) for i in range(2)]
            xc2 = [A.f32([128, 512], f"xc2{i}") for i in range(2)]
            tb4 = ([A.bf16([128, 512], f"l2sq{i}") for i in range(2)], [A.bf16([128, 512], f"l2zb{i}") for i in range(2)],
                   A.f32([128, 512], "l2mean"), A.f32([128, 512], "l2rstd"), PS[4], PS[5])
            xo2 = [A.f32([128, 512], f"l2xo{i}") for i in range(2)]
            xTo = [A.f32([128, 512], f"xTo{i}") for i in range(2)]
            for oc in range(KC):
                wt = wdn[oc % 2]
                load_w(wt, w_down_d[l][:, oc * 128:(oc + 1) * 128])
                for t2_ in range(2):
                    tt = hf * 2 + t2_
                    pb = PS[(oc * 2 + t2_) % 4]
                    xct = xc2[t2_]
                    S.dma("sp", xct[:, :], xmid.ap[oc, :, tt * 512:(tt + 1) * 512], r=[xmid], w=[xct])
                    mm_group(pb[:, :], [(wt[:, j, :], actT[:, j, t2_ * 512:(t2_ + 1) * 512]) for j in range(NH)], [wt, actT], [pb])
                    dve(lambda E, pb=pb, oc=oc, xct=xct, zt=z2cs[t2_][oc]: E.scalar_tensor_tensor(out=zt[:, :], in0=pb[:, :], scalar=g2a[l][:, oc:oc + 1], in1=xct[:, :],
                                                                                                op0=ALU.mult, op1=ALU.add), [pb, g2a[l], xct], [z2cs[t2_][oc]])
            for t2_ in range(2):
                tt = hf * 2 + t2_
                z2c = z2cs[t2_]

                def out_x2(c, zt, tt=tt):
                    xoc = xo2[c % 2]
                    A_ident(xoc[:, :], zt[:, :], ln2w[l][:, c:c + 1], ln2b[l][:, c:c + 1], [zt, ln2w[l], ln2b[l]], [xoc])
                    if not last:
                        S.dma("sp", xin.ap[c, :, tt * 512:(tt + 1) * 512], xoc[:, :], r=[xoc], w=[xin])
                    else:
                        pt = PS[6 + c % 2]

                        def fn(E, pt=pt, xoc=xoc):
                            ins = None
                            for j in range(4):
                                ins = E.transpose(pt[:, j * 128:(j + 1) * 128], xoc[:, j * 128:(j + 1) * 128], ident_f[:, :])
                            return ins
                        pe(fn, [xoc, ident_f], [pt])
                        yo = xTo[c % 2]
                        A_copy(yo[:, :], pt[:, :], [pt], [yo])
                        S.dma("sp", y_o[tt * 512:(tt + 1) * 512, c * 128:(c + 1) * 128].rearrange("(j p) f -> p j f", p=128),
                              yo[:, :].rearrange("p (j f) -> p j f", j=4), r=[yo])

                def out_h2(c, zt, tt=tt):
                    hoc = xTo[c % 2]
                    hob = hoc[:, 0:256].bitcast(BF16)
                    A_ident(hob, zt[:, :], A1n[:, c:c + 1], B1n[:, c:c + 1], [zt, A1n, B1n], [hoc])
                    S.dma("sp", catT_d.ap[c, :, tt * 512:(tt + 1) * 512], hob, r=[hoc], w=[catT_d])
                ln_tile(z2c, tb4, ln2w[l], ln2b[l], out_x2, None if last else out_h2)
            S.barrier()
        A.top = work0
        ckpt(f"l{l}p4", [("x2T", xin.ap, [KC, 128, T], F32, [xin])])
        if not last:
            for c in range(KC):
                S.dma("sp", hT[:, c, :], catT_d.ap[c, :, :], r=[catT_d], w=[hT])
            S.barrier(new_epoch=True)

    for l_ in range(L):
        layer(l_)
    S.barrier()
    S.emit()


def make_ln_finish(S, act, dve, A_ident, z, mean, rstd, s1p, s2p):
    def fin():
        act(lambda E: E.activation(out=mean[:, :], in_=s1p[:, :], func=AF.Identity, scale=1.0 / D), [s1p], [mean])
        dve(lambda E: E.tensor_tensor(out=rstd[:, :], in0=mean[:, :], in1=mean[:, :], op=ALU.mult), [mean], [rstd])
        dve(lambda E: E.scalar_tensor_tensor(out=rstd[:, :], in0=s2p[:, :], scalar=1.0 / D, in1=rstd[:, :], op0=ALU.mult, op1=ALU.subtract), [s2p, rstd], [rstd])
        dve(lambda E: E.tensor_scalar(out=rstd[:, :], in0=rstd[:, :], scalar1=EPS1, scalar2=None, op0=ALU.add), [rstd], [rstd])
        act(lambda E: E.activation(out=rstd[:, :], in_=rstd[:, :], func=AF.Sqrt), [rstd], [rstd])
        dve(lambda E: E.reciprocal(out=rstd[:, :], in_=rstd[:, :]), [rstd], [rstd])
    return fin


def _consts(is_sample):
    ident = np.eye(128, dtype=np.float32)
    P = np.zeros((128, 128), np.float32)
    for d in range(128):
        if (d % 64) < 32:
            P[d + 32, d] = -1.0
        else:
            P[d - 32, d] = 1.0
    cosT = np.ones((128, T), np.float32)
    sinT = np.zeros((128, T), np.float32)
    if is_sample:
        t = np.arange(T)
        row = (t // 64).astype(np.float32)
        col = (t % 64).astype(np.float32)
        freqs = (np.float32(10000.0) ** (-np.arange(32, dtype=np.float32) / np.float32(32))).astype(np.float32)
        for d in range(128):
            pos = row if d < 64 else col
            ang = (pos * freqs[d % 32]).astype(np.float32)
            cosT[d] = np.cos(ang)
            sinT[d] = np.sin(ang)
    kj = np.arange(128)[:, None]
    qi = np.arange(128)[None, :]
    masks = np.ones((128, 4, 128), np.float32)
    if is_sample:
        masks[:, 0, :] = (kj >= qi)
        masks[:, 1, :] = (kj <= qi)
    masks[:, 2, :] = (kj <= qi)
    masks[:, 3, :] = (kj >= qi)
    abias = np.zeros((128, 8), np.float32)
    if not is_sample:
        abias[:, 0] = -30000.0
        abias[:, 3] = -30000.0
        abias[:, 4] = -30000.0
    flags = np.zeros((128, 2), np.float32)
    flags[:, 0] = 1.0 if is_sample else 0.0
    return dict(ident=ident, ropeP=P, cosT=cosT, sinT=sinT, masks=masks, abias=abias, flags=flags)


_NC_CACHE = {}


def make_in_maps(x_prompt, x_sample, cache_k, cache_v, state_gla_fwd, state_gla_bwd, c, c_ctx,
                 w_ada, b_ada, w_in, attn_sink, w_gate_f, b_gate_f, w_gate_b, b_gate_b, gla_norm_w,
                 w_out, ln1_w, ln1_b, w_up, conv_w, conv_b, w_down, ln2_w, ln2_b):
    f = lambda a: np.ascontiguousarray(np.asarray(a, dtype=np.float32))
    shared = dict(
        w_ada=f(w_ada), b_ada=f(b_ada).reshape(L, 96, 128), w_in=f(w_in), attn_sink=f(attn_sink),
        w_gate_f=f(w_gate_f), b_gate_f=f(b_gate_f).reshape(L, 4, 128), w_gate_b=f(w_gate_b), b_gate_b=f(b_gate_b).reshape(L, 4, 128),
        gla_norm_w=f(gla_norm_w).reshape(L, 2, 128), w_out=f(w_out), ln1_w=f(ln1_w).reshape(L, KC, 128), ln1_b=f(ln1_b).reshape(L, KC, 128),
        w_up=f(w_up), conv_w=f(conv_w).reshape(L, 3 * 88, 128), conv_b=f(conv_b).reshape(L, 88, 128), w_down=f(w_down),
        ln2_w=f(ln2_w).reshape(L, KC, 128), ln2_b=f(ln2_b).reshape(L, KC, 128))
    x_prompt = f(x_prompt)
    x_sample = f(x_sample)
    cs = [_consts(False), _consts(True)]
    in_maps = []
    for core in range(8):
        m = dict(shared)
        if core < 4:
            m.update(cs[0])
            m["x"] = x_prompt[core * 8:(core + 1) * 8].reshape(T, D)
            m["cvec"] = f(c_ctx).reshape(KC, 128)
            m["ck"] = np.zeros((L, 256, 2, 128), np.float32)
            m["cv"] = np.zeros((L, 256, 2, 128), np.float32)
            m["s0f"] = np.zeros((L, 4, 128, 256), np.float32)
            m["s0b"] = np.zeros((L, 4, 128, 256), np.float32)
        else:
            b = core - 4
            m.update(cs[1])
            m["x"] = x_sample[b]
            m["cvec"] = f(c)[b].reshape(KC, 128)
            m["ck"] = f(cache_k)[b]
            m["cv"] = f(cache_v)[b]
            m["s0f"] = f(state_gla_fwd)[b]
            m["s0b"] = f(state_gla_bwd)[b]
        in_maps.append(m)
    return in_maps


def kernel(**inputs):
    in_maps = make_in_maps(**inputs)
    if "nc" not in _NC_CACHE:
        _NC_CACHE["nc"] = build_program()
    nc = _NC_CACHE["nc"]
    res = run_bass_kernel_spmd(nc, in_maps, core_ids=list(range(8)))
    R = res.results
    y_prompt = np.stack([R[cr]["y"] for cr in range(4)]).reshape(32, 256, D)
    y_sample = np.stack([R[cr]["y"] for cr in range(4, 8)]).reshape(4, T, D)
    nk = np.concatenate([R[cr]["kc_o"].reshape(L, 8, 256, 2, 128).transpose(1, 0, 2, 3, 4) for cr in range(4)], axis=0)
    nv = np.concatenate([R[cr]["vc_o"].reshape(L, 8, 256, 2, 128).transpose(1, 0, 2, 3, 4) for cr in range(4)], axis=0)
    sf = np.concatenate([R[cr]["sf_o"].transpose(1, 0, 2, 3, 4) for cr in range(4)], axis=0)
    sb = np.concatenate([R[cr]["sb_o"].transpose(1, 0, 2, 3, 4) for cr in range(4)], axis=0)
    return (np.ascontiguousarray(y_prompt), np.ascontiguousarray(y_sample), np.ascontiguousarray(nk), np.ascontiguousarray(nv),
            np.ascontiguousarray(sf), np.ascontiguousarray(sb))
```
